# Optimizing a Trainium2 kernel written in Bass

```python
import math
import jax, jax.numpy as jnp
from jax import lax
import numpy as np

D_MODEL = 2048
BATCH = 4
SEQ = 2048
DEPTH = 1

MEM_LEN = 256
CONV_CH = D_MODEL // 2
CONV_WIDTH = 31
DA_HEADS = 4
DA_HEAD_DIM = 128
DA_QK = DA_HEADS * 2 * DA_HEAD_DIM
DA_WIDTH = DA_HEADS * 2 * DA_HEAD_DIM
XA_HEADS = 4
XA_HEAD_DIM = D_MODEL // 8
XA_WIDTH = XA_HEADS * XA_HEAD_DIM
N_BRANCH = 3
D_FF = 4 * D_MODEL
ROPE_THETA = 10000.0
Q_BLOCK = 128
NORM_EPS = 1e-6
IN_SIZES = (2 * CONV_CH, DA_QK, DA_QK, DA_WIDTH, XA_WIDTH, N_BRANCH * D_MODEL)
D_IN = sum(IN_SIZES)

kernel_name = "hybrid_conformer_diffattn_memxattn_gated"


def lambda_init_fn(layer_idx):
    return 0.8 - 0.6 * math.exp(-0.3 * layer_idx)


def rmsnorm(x, g):
    xf = x.astype(jnp.float32)
    y = xf * lax.rsqrt(jnp.mean(xf * xf, axis=-1, keepdims=True) + NORM_EPS)
    return (y * g.astype(jnp.float32)).astype(x.dtype)


def layernorm(x, g, b):
    xf = x.astype(jnp.float32)
    mu = jnp.mean(xf, axis=-1, keepdims=True)
    var = jnp.mean(jnp.square(xf - mu), axis=-1, keepdims=True)
    y = (xf - mu) * lax.rsqrt(var + NORM_EPS)
    return (y * g.astype(jnp.float32) + b.astype(jnp.float32)).astype(x.dtype)


def rope_tables(positions):
    inv_freq = 1.0 / (ROPE_THETA ** (jnp.arange(0, DA_HEAD_DIM, 2, dtype=jnp.float32) / DA_HEAD_DIM))
    ang = positions.astype(jnp.float32)[..., None] * inv_freq
    return jnp.cos(ang), jnp.sin(ang)


def apply_rope(t, cos, sin):
    c = cos[:, :, None, None, :].astype(t.dtype)
    s = sin[:, :, None, None, :].astype(t.dtype)
    t1, t2 = jnp.split(t, 2, axis=-1)
    return jnp.concatenate([t1 * c - t2 * s, t2 * c + t1 * s], axis=-1)


def conformer_branch(u_glu, w_dw, b_dw, g_ln, b_ln, w_out):
    a, b = jnp.split(u_glu, 2, axis=-1)
    u = a * jax.nn.sigmoid(b)
    u = lax.conv_general_dilated(
        u, w_dw[:, None, :].astype(u.dtype), window_strides=(1,),
        padding=[(CONV_WIDTH - 1, 0)],
        dimension_numbers=("NWC", "WIO", "NWC"),
        feature_group_count=CONV_CH) + b_dw
    u = jax.nn.silu(layernorm(u, g_ln, b_ln))
    return u @ w_out


def diff_attention(q, k, v, lam, lambda_init, g_subln, w_out):
    bsz, seq = q.shape[0], q.shape[1]
    scale = DA_HEAD_DIM ** -0.5
    outs = []
    for blk in range(seq // Q_BLOCK):
        q0 = blk * Q_BLOCK
        kv_len = q0 + Q_BLOCK
        qb = q[:, q0:kv_len]
        kb = k[:, :kv_len]
        vb = v[:, :kv_len]
        s = jnp.einsum("bqhmd,bkhmd->bhmqk", qb, kb).astype(jnp.float32) * scale
        mask = (q0 + jnp.arange(Q_BLOCK))[:, None] >= jnp.arange(kv_len)[None, :]
        s = jnp.where(mask, s, -jnp.inf)
        p = jax.nn.softmax(s, axis=-1)
        a = p[:, :, 0] - lam * p[:, :, 1]
        outs.append(jnp.einsum("bhqk,bkhe->bqhe", a.astype(vb.dtype), vb))
    o = jnp.concatenate(outs, axis=1)
    o = rmsnorm(o, g_subln) * (1.0 - lambda_init)
    return o.reshape(bsz, seq, DA_WIDTH) @ w_out


def memory_cross_attention(xq, mem_n, w_kv, w_out):
    bsz, seq = xq.shape[0], xq.shape[1]
    q = xq.reshape(bsz, seq, XA_HEADS, XA_HEAD_DIM)
    kv = mem_n @ w_kv
    k, v = jnp.split(kv, 2, axis=-1)
    k = k.reshape(bsz, -1, XA_HEADS, XA_HEAD_DIM)
    v = v.reshape(bsz, -1, XA_HEADS, XA_HEAD_DIM)
    s = jnp.einsum("bshd,bmhd->bhsm", q, k).astype(jnp.float32) * (XA_HEAD_DIM ** -0.5)
    p = jax.nn.softmax(s, axis=-1)
    o = jnp.einsum("bhsm,bmhd->bshd", p.astype(v.dtype), v)
    return o.reshape(bsz, seq, XA_WIDTH) @ w_out


def setup_inputs(seed: int = 0) -> dict:
    key = jax.random.key(seed)
    ks = jax.random.split(key, 32)
    f32 = jnp.float32

    def nrm(k, shape, fan_in):
        return jax.random.normal(k, shape, f32) * (fan_in ** -0.5)

    def gain(k, shape):
        return 1.0 + 0.01 * jax.random.normal(k, shape, f32)

    L = DEPTH
    return {
        "x": jax.random.normal(ks[0], (BATCH, SEQ, D_MODEL), f32),
        "mem": jax.random.normal(ks[1], (BATCH, MEM_LEN, D_MODEL), f32),
        "positions": jnp.arange(SEQ, dtype=jnp.int32)[None, :]
                     + jax.random.randint(ks[2], (BATCH, 1), 0, 1024, dtype=jnp.int32),
        "g_mix": gain(ks[3], (L, D_MODEL)),
        "w_in": nrm(ks[4], (L, D_MODEL, D_IN), D_MODEL),
        "w_dw": nrm(ks[5], (L, CONV_WIDTH, CONV_CH), CONV_WIDTH),
        "b_dw": 0.01 * jax.random.normal(ks[6], (L, CONV_CH), f32),
        "g_conv_ln": gain(ks[7], (L, CONV_CH)),
        "b_conv_ln": 0.01 * jax.random.normal(ks[8], (L, CONV_CH), f32),
        "w_conv_out": nrm(ks[9], (L, CONV_CH, D_MODEL), CONV_CH),
        "lambda_q1": 0.1 * jax.random.normal(ks[10], (L, DA_HEAD_DIM), f32),
        "lambda_k1": 0.1 * jax.random.normal(ks[11], (L, DA_HEAD_DIM), f32),
        "lambda_q2": 0.1 * jax.random.normal(ks[12], (L, DA_HEAD_DIM), f32),
        "lambda_k2": 0.1 * jax.random.normal(ks[13], (L, DA_HEAD_DIM), f32),
        "g_subln": gain(ks[14], (L, 2 * DA_HEAD_DIM)),
        "w_da_out": nrm(ks[15], (L, DA_WIDTH, D_MODEL), DA_WIDTH),
        "g_mem": gain(ks[16], (L, D_MODEL)),
        "w_mem_kv": nrm(ks[17], (L, D_MODEL, 2 * XA_WIDTH), D_MODEL),
        "w_xa_out": nrm(ks[18], (L, XA_WIDTH, D_MODEL), XA_WIDTH),
        "w_mix_out": nrm(ks[19], (L, D_MODEL, D_MODEL), D_MODEL),
        "g_mlp": gain(ks[20], (L, D_MODEL)),
        "w_up": nrm(ks[21], (L, D_MODEL, D_FF), D_MODEL),
        "w_down": nrm(ks[22], (L, D_FF, D_MODEL), D_FF),
        "g_final": gain(ks[23], (D_MODEL,)),
    }


def reference(x, mem, positions, g_mix, w_in, w_dw, b_dw, g_conv_ln, b_conv_ln, w_conv_out,
              lambda_q1, lambda_k1, lambda_q2, lambda_k2, g_subln, w_da_out,
              g_mem, w_mem_kv, w_xa_out, w_mix_out, g_mlp, w_up, w_down, g_final):
    bsz, seq = x.shape[0], x.shape[1]
    cos, sin = rope_tables(positions)
    split_idx = [int(v) for v in np.cumsum(IN_SIZES)[:-1]]
    for l in range(DEPTH):
        lambda_init = lambda_init_fn(l)
        h = rmsnorm(x, g_mix[l])
        proj = h @ w_in[l]
        u_glu, q_da, k_da, v_da, q_xa, gate_pre = jnp.split(proj, split_idx, axis=-1)

        y_conv = conformer_branch(u_glu, w_dw[l], b_dw[l], g_conv_ln[l], b_conv_ln[l], w_conv_out[l])

        q = apply_rope(q_da.reshape(bsz, seq, DA_HEADS, 2, DA_HEAD_DIM), cos, sin)
        k = apply_rope(k_da.reshape(bsz, seq, DA_HEADS, 2, DA_HEAD_DIM), cos, sin)
        v = v_da.reshape(bsz, seq, DA_HEADS, 2 * DA_HEAD_DIM)
        lam = (jnp.exp(jnp.sum(lambda_q1[l].astype(jnp.float32) * lambda_k1[l].astype(jnp.float32)))
               - jnp.exp(jnp.sum(lambda_q2[l].astype(jnp.float32) * lambda_k2[l].astype(jnp.float32)))
               + lambda_init)
        y_da = diff_attention(q, k, v, lam, lambda_init, g_subln[l], w_da_out[l])

        mem_n = rmsnorm(mem, g_mem[l])
        y_xa = memory_cross_attention(q_xa, mem_n, w_mem_kv[l], w_xa_out[l])

        gates = jax.nn.sigmoid(gate_pre).reshape(bsz, seq, N_BRANCH, D_MODEL)
        merged = gates[:, :, 0] * y_conv + gates[:, :, 1] * y_da + gates[:, :, 2] * y_xa
        x = x + merged @ w_mix_out[l]

        h2 = rmsnorm(x, g_mlp[l])
        x = x + jnp.square(jax.nn.relu(h2 @ w_up[l])) @ w_down[l]
    return rmsnorm(x, g_final)
```

```python
import math
import numpy as np
import concourse.bass as bass
import concourse.mybir as mybir
from concourse.bass_utils import run_bass_kernel_spmd

F32 = mybir.dt.float32
BF16 = mybir.dt.bfloat16
I32 = mybir.dt.int32
AF = mybir.ActivationFunctionType
ALU = mybir.AluOpType
AX = mybir.AxisListType

D = 2048
S = 2048
B = 4
NTOK = 1024
NCTX = 1024
MEM = 256
DIN = 12288
DFF = 8192
EPS = 1e-6
LAMBDA_INIT = 0.8 - 0.6 * math.exp(0.0)
PI = math.pi
SCALE_DA = 128 ** -0.5
SCALE_XA = 256 ** -0.5

C_GLU_A, C_GLU_B, C_QDA, C_KDA, C_VDA, C_QXA, C_GATE = 0, 1024, 2048, 3072, 4096, 5120, 6144

CF_ID, CF_ONES, CF_TRI, CF_SWAP, CF_INVF, CF_SGN, CF_CBIAS, CF_EPS, NCF = 0, 128, 256, 384, 512, 513, 514, 515, 520

STRICT_SAME_ENGINE = True
SB_BASE = 16640
SB_END = 229120


class Op:
    __slots__ = ("eng", "fn", "deps", "signal", "dma", "val")

    def __init__(self, eng, fn):
        self.eng = eng
        self.fn = fn
        self.deps = []
        self.signal = False
        self.dma = None
        self.val = None


class Prog:
    ENGS = ("pe", "act", "dve", "pool", "sp")

    def __init__(self, nc):
        self.nc = nc
        self.ops = []
        self.res = {}
        self.dma_cnt = {}
        self.last = {e: None for e in self.ENGS}
        self.lastc = {e: None for e in self.ENGS}
        self.strict = False
        self.bar = {e: [] for e in self.ENGS}
        self.zones = {}
        self.offs = {}
        self.nalloc = 0

    def zone(self, name, lo, hi):
        self.zones[name] = [lo, hi, lo, hi]

    def alloc(self, name, shape, dtype, zone, top=False):
        esz = 4 if dtype in (F32, I32) else 2
        n = 1
        for s in shape[1:]:
            n *= s
        nbytes = (n * esz + 63) // 64 * 64
        z = self.zones[zone]
        if top:
            z[3] -= nbytes
            off = z[3]
        else:
            off = z[2]
            z[2] += nbytes
        assert z[2] <= z[3], (name, zone, z)
        self.offs[name] = off
        return self.alloc_at(name, shape, dtype, off)

    def alloc_at(self, name, shape, dtype, off):
        self.nalloc += 1
        return self.nc.alloc_sbuf_tensor_at(f"{name}_{self.nalloc}", list(shape), dtype, offset=off)

    def mark(self, zone):
        return self.zones[zone][2]

    def reset(self, zone, m=None, top=False):
        z = self.zones[zone]
        z[2] = z[0] if m is None else m
        if top:
            z[3] = z[1]
        self.barrier()

    def barrier(self):
        snap = [o for o in self.lastc.values() if o is not None]
        if self.strict:
            snap += [o for o in self.last.values() if o is not None and o.dma is not None]
        for e in self.ENGS:
            if e != "pe":
                self.bar[e] = list(snap)

    def op(self, eng, fn, reads=(), writes=(), dma=None):
        o = Op(eng, fn)
        pr = [r for r in reads if isinstance(r, tuple) and r[0] == "ps"]
        if pr:
            writes = list(writes) + [r for r in pr if r not in writes]
        if dma is not None:
            c = self.dma_cnt.get(dma, 0) + 16
            self.dma_cnt[dma] = c
            o.dma = (dma, c)
        deps = {}

        def add(d, kind):
            deps.setdefault(id(d), [d, set()])[1].add(kind)

        for r in reads:
            st = self.res.get(r)
            if st is not None and st[0] is not None:
                add(st[0], "raw")
        for w in writes:
            st = self.res.get(w)
            if st is not None:
                if st[0] is not None:
                    add(st[0], "waw")
                for rd in st[1].values():
                    add(rd, "war")
                for rd in st[2]:
                    add(rd, "war")
        for b in self.bar[eng]:
            add(b, "raw")
        self.bar[eng] = []
        for d, kinds in deps.values():
            if d is o:
                continue
            if d.dma is not None:
                if o.dma is not None and o.dma[0] == d.dma[0] and kinds == {"waw"}:
                    continue
                o.deps.append(d)
                continue
            if d.eng == eng and o.dma is None:
                if eng == "pe":
                    continue
                if "raw" not in kinds and not STRICT_SAME_ENGINE:
                    continue
            d.signal = True
            o.deps.append(d)
        for r in reads:
            st = self.res.setdefault(r, [None, {}, []])
            if o.dma is not None:
                st[2].append(o)
            else:
                st[1][eng] = o
        for w in writes:
            self.res[w] = [o, {}, []]
        self.ops.append(o)
        self.last[eng] = o
        if o.dma is None and fn is not None:
            self.lastc[eng] = o
        return o

    def emit(self):
        nc = self.nc
        per = {e: [o for o in self.ops if o.eng == e] for e in self.ENGS}
        for e in self.ENGS:
            c = 0
            for o in per[e]:
                if o.dma is None and o.signal:
                    c += 1
                    o.val = c
        self.stats = {e: (len(per[e]), sum(1 for o in per[e] if o.signal)) for e in self.ENGS}
        import contextlib
        with contextlib.ExitStack() as st:
            esem = {e: st.enter_context(nc.semaphore(f"s_{e}")) for e in self.ENGS}
            dsem = {k: st.enter_context(nc.semaphore(f"d_{i}")) for i, k in enumerate(self.dma_cnt)}
            block = st.enter_context(nc.Block())

            def run(eng_name):
                def body(eng):
                    seen = {}
                    for o in per[eng_name]:
                        for d in o.deps:
                            if d.dma is not None:
                                sem, v = dsem[d.dma[0]], d.dma[1]
                                key = ("d", d.dma[0])
                            else:
                                sem, v = esem[d.eng], d.val
                                key = ("e", d.eng)
                            if seen.get(key, 0) >= v:
                                continue
                            seen[key] = v
                            eng.wait_ge(sem, v)
                        if o.fn is None:
                            continue
                        ins = o.fn(eng)
                        if o.dma is not None:
                            ins.then_inc(dsem[o.dma[0]], 16)
                        elif o.signal:
                            ins.then_inc(esem[eng_name], 1)
                return body

            block.tensor(run("pe"))
            block.scalar(run("act"))
            block.vector(run("dve"))
            block.gpsimd(run("pool"))
            block.sync(run("sp"))


def build(stop="all", dbg=(), plan=None):
    nc = bass.Bass("TRN2", target_bir_lowering=False)
    P = Prog(nc)
    P.strict = bool(dbg)
    o = SB_BASE
    P.zone("P", o, o + 7168); o += 7168
    P.zone("WS", o, o + 3 * 16384); o += 3 * 16384
    P.zone("H", o, o + 32768); o += 32768
    P.zone("X", o, o + 16384); o += 16384
    P.zone("Y", o, SB_END)

    def din(name, shape, dt=F32):
        return nc.dram_tensor(name, list(shape), dt, kind="ExternalInput").ap()

    xo = din("xo", [NTOK, D])
    xc = din("xc", [NCTX, D])
    memx = din("memx", [MEM, D])
    posa = din("posa", [1, NCTX + NTOK], I32)
    cf_d = din("cf", [128, NCF])
    g_mix = din("g_mix", [1, D])
    w_in = din("w_in", [D, DIN])
    w_dw = din("w_dw", [31 * 8, 128])
    vecs = din("vecs", [40, 128])
    w_conv_out = din("w_conv_out", [1024, D])
    lams = din("lams", [4, 128])
    g_subln = din("g_subln", [1, 256])
    w_da_out = din("w_da_out", [1024, D])
    g_mem = din("g_mem", [1, D])
    w_mem_kv = din("w_mem_kv", [D, D])
    w_xa_out = din("w_xa_out", [1024, D])
    w_mix_out = din("w_mix_out", [D, D])
    w_up = din("w_up", [D, DFF])
    w_down = din("w_down", [DFF, D])
    g_final = din("g_final", [1, D])
    y = nc.dram_tensor("y", [NTOK, D], F32, kind="ExternalOutput").ap()
    dbg_out = {}
    for name, shape in dbg:
        dbg_out[name] = nc.dram_tensor("dbg_" + name, list(shape), F32, kind="ExternalOutput").ap()

    PS = nc.alloc_psum_tensor("ps", [128, 4096], F32)

    def bank(b, n=512, off=0):
        return PS[:, b * 512 + off: b * 512 + off + n]

    def span(b, n):
        return PS[:, b * 512: b * 512 + n]

    def bank_bf(b, n=1024):
        return PS[:, b * 512:(b + 1) * 512].bitcast(BF16)[:, 0:n]

    def PB(b):
        return ("ps", b)

    rot = {}
    pend = []

    def defer(delay, fn):
        pend.append([delay, fn])

    def tick():
        due = []
        for it in pend:
            it[0] -= 1
        while pend and pend[0][0] <= 0:
            due.append(pend.pop(0)[1])
        for fn in due:
            fn()

    def flush():
        while pend:
            pend.pop(0)[1]()

    def nxt(name, choices):
        i = rot.get(name, 0)
        rot[name] = i + 1
        return choices[i % len(choices)]

    CF = P.alloc("CF", [128, NCF], F32, "P")
    CB = P.alloc("CB", [128, 512], BF16, "P")
    ID_F = CF[:, CF_ID:CF_ID + 128]
    EPSB = CF[:, CF_EPS:CF_EPS + 1]
    ID_B = CB[:, 0:128]
    ONES_B = CB[:, 128:256]
    TRI_B = CB[:, 256:384]
    SWAP_B = CB[:, 384:512]
    SV = P.alloc("SV", [128, 128], F32, "P")
    VECT = P.alloc("VECT", [128, 40], F32, "P")
    WDWT = P.alloc("WDWT", [128, 248], F32, "P")
    GSUB = P.alloc("GSUB", [128, 256], F32, "P")
    WS = [P.alloc(f"WS{i}", [128, 16, 512], BF16, "WS") for i in range(3)]

    def wd_view(i):
        return WS[i][:, :, :].rearrange("p k c -> p (k c)").rearrange("p (f c) -> p f c", f=4)

    WT = {"w_in": w_in, "w_mem_kv": w_mem_kv, "w_conv_out": w_conv_out, "w_da_out": w_da_out,
          "w_xa_out": w_xa_out, "w_mix_out": w_mix_out, "w_up": w_up}
    plan_out = []
    wst = {"n": 0, "emitted": 0, "rel": 0, "cap": 1}

    def slab_src(desc):
        if desc[0] == "cols":
            _, wname, c0, nk = desc
            return w_cols(WT[wname], c0), nk
        _, g = desc
        return w_down[g * 512:(g + 1) * 512, :].rearrange("(f p) c -> p f c", p=128), None

    def emit_load(m, desc):
        src, nk = slab_src(desc)
        i = m % 3
        dst = wd_view(i) if nk is None else WS[i][:, 0:nk, :]
        P.op("pool", lambda e, dst=dst, src=src: e.dma_start(out=dst, in_=src), [], [("WS", i)], dma=("ws", i))

    def prefetch():
        if plan is None:
            return
        while wst["emitted"] < min(wst["rel"] + 3, len(plan), wst["cap"]):
            emit_load(wst["emitted"], plan[wst["emitted"]])
            wst["emitted"] += 1

    def wslab(*desc):
        n = wst["n"]
        wst["n"] += 1
        plan_out.append(desc)
        assert n == wst["rel"], "slabs are consumed one at a time"
        if plan is None:
            emit_load(n, desc)
        else:
            assert tuple(plan[n]) == tuple(desc), (n, plan[n], desc)
            prefetch()
            assert wst["emitted"] > n
        return n % 3

    def wrel():
        wst["rel"] += 1
        prefetch()

    def w_cols(w, c0, n=512):
        return w[:, c0:c0 + n].rearrange("(k p) c -> p k c", p=128)

    SV_NEGLAM, SV_QMAX, SV_KMAX, SV_NEGM, SV_NEGMC = 0, 1, 9, 17, 25
    SV_QXMAX, SV_KMMAX, SV_NEGMX = 33, 37, 41
    SV_STK, SV_STQ, SV_STX, SV_T = 48, 80, 96, 112

    def sv(c, n=1):
        return SV[:, c:c + n]

    def dma(q, out, in_, reads, writes, sem):
        return P.op(q, lambda e, out=out, in_=in_: e.dma_start(out=out, in_=in_), reads, writes, dma=sem)

    def dump(name, ap_sb, keys, rows=None):
        if name in dbg_out:
            dma("pool", dbg_out[name] if rows is None else dbg_out[name][rows], ap_sb, list(keys), ["dbg_" + name], "dbg")

    def finish():
        P.op("sp", None, ["dbg_" + n for n in dbg_out] + [("y", t) for t in range(8)], [])
        P.emit()
        nc.plan_out = plan_out
        return nc

    KT = P.alloc("KT", [128, 8, 2048], BF16, "Y")
    VA = P.alloc("VA", [128, 16, 4, 257], BF16, "Y")
    QT = P.alloc("QT", [128, 8, 1024], BF16, "Y")
    HTH = P.alloc("HTH", [128, 16, 32], BF16, "P")
    mY = P.mark("Y")
    P.op("dve", lambda e: e.memset(VA[:, :, :, 256:257], 1.0), [], ["VA1"])

    XTk = [P.alloc_at(f"XTk{i}", [128, D], F32, P.offs["KT"] + i * 8192)[:, :] for i in range(4)]
    va_own = P.offs["VA"] + 8 * 4 * 257 * 2
    va_own = (va_own + 63) // 64 * 64
    XTv = [P.alloc_at(f"XTv{i}", [128, D], F32, va_own + i * 8192)[:, :] for i in range(2)]
    assert va_own + 2 * 8192 <= P.offs["VA"] + 16 * 4 * 257 * 2
    dma("sp", CF[:, :], cf_d[:, :], [], ["CF"], "cf")
    for t in range(4):
        dma("sp", XTk[t][:, :], xc[t * 128:(t + 1) * 128, :], [], [("XT", t)], ("xt", t))
    P.op("dve", lambda e: e.tensor_copy(out=CB[:, :], in_=CF[:, 0:512]), ["CF"], ["CB"])
    prefetch()

    VROW = P.alloc("VROW", [40, 128], F32, "X")
    WROW0 = P.alloc("WROW0", [128, 128], F32, "X")
    WROW1 = P.alloc("WROW1", [120, 128], F32, "X")
    LAMB = P.alloc("LAMB", [128, 4, 128], F32, "X")
    LTMP = P.alloc("LTMP", [128, 2, 128], F32, "X")
    dma("sp", VROW[:, :], vecs[:, :], [], ["VROW"], ("misc", 1))
    dma("sp", WROW0[:, :], w_dw[0:128, :], [], ["WROW0"], ("misc", 2))
    dma("sp", WROW1[:, :], w_dw[128:248, :], [], ["WROW1"], ("misc", 3))
    dma("sp", LAMB[:, :, :], lams.partition_broadcast(128), [], ["LAMB"], ("misc", 4))
    dma("sp", GSUB[:, :], g_subln.partition_broadcast(128), [], ["GSUB"], ("misc", 5))
    P.op("pe", lambda e: e.matmul(bank(0, 40), lhsT=VROW[:, :], rhs=CF[0:40, CF_ID:CF_ID + 40], start=True, stop=True),
         ["VROW", "CF"], [PB(0)])
    P.op("pe", lambda e: e.matmul(bank(1, 128), lhsT=WROW0[:, :], rhs=ID_F, start=True, stop=True),
         ["WROW0", "CF"], [PB(1)])
    P.op("pe", lambda e: e.matmul(bank(2, 120), lhsT=WROW1[:, :], rhs=CF[0:120, CF_ID:CF_ID + 120], start=True, stop=True),
         ["WROW1", "CF"], [PB(2)])
    P.op("dve", lambda e: e.tensor_copy(out=VECT[:, :], in_=bank(0, 40)), [PB(0)], ["VECT"])
    P.op("dve", lambda e: e.tensor_copy(out=WDWT[:, 0:128], in_=bank(1, 128)), [PB(1)], ["WDWT"])
    P.op("dve", lambda e: e.tensor_copy(out=WDWT[:, 128:248], in_=bank(2, 120)), [PB(2)], ["WDWT"])
    P.op("dve", lambda e: e.tensor_scalar(out=GSUB[:, :], in0=GSUB[:, :], scalar1=float(1.0 - LAMBDA_INIT), scalar2=None,
                                          op0=ALU.mult), ["GSUB"], ["GSUB"])
    P.op("dve", lambda e: e.tensor_tensor(out=LTMP[:, 0, :], in0=LAMB[:, 0, :], in1=LAMB[:, 1, :], op=ALU.mult), ["LAMB"], ["LTMP0"])
    P.op("dve", lambda e: e.tensor_tensor(out=LTMP[:, 1, :], in0=LAMB[:, 2, :], in1=LAMB[:, 3, :], op=ALU.mult), ["LAMB"], ["LTMP1"])
    P.op("dve", lambda e: e.tensor_reduce(out=sv(SV_T, 2), in_=LTMP[:, :, :], axis=AX.X, op=ALU.add), ["LTMP0", "LTMP1"], ["SVT0"])
    P.op("act", lambda e: e.activation(out=sv(SV_T + 2, 2), in_=sv(SV_T, 2), func=AF.Exp), ["SVT0"], ["SVT2"])
    P.op("dve", lambda e: e.tensor_tensor(out=sv(SV_T + 4), in0=sv(SV_T + 3), in1=sv(SV_T + 2), op=ALU.subtract), ["SVT2"], ["SVT4"])
    P.op("dve", lambda e: e.tensor_scalar(out=sv(SV_NEGLAM), in0=sv(SV_T + 4), scalar1=float(-LAMBDA_INIT), scalar2=None,
                                          op0=ALU.add), ["SVT4"], ["NEGLAM"])

    HT = P.alloc("HT", [128, 16, 1024], BF16, "H")
    def norm_transpose(src_rows, ntiles, gvec, HTd, htkey, xt_alias=None, preloaded=0):
        GBC = P.alloc("GBC", [128, D], F32, "Y")
        HB = [P.alloc(f"HB{i}", [128, D], BF16, "Y") for i in range(2)]
        SS = P.alloc("SS", [128, 8], F32, "Y")
        if xt_alias is not None:
            XT = xt_alias
        else:
            XT = [P.alloc(f"XT{i}", [128, D], F32, "Y") for i in range(2)]
        dma("sp", GBC[:, :], gvec.partition_broadcast(128), [], ["GBC"], ("misc", 7))

        nx = len(XT)

        def s1(t):
            b = t % 2
            xb = t % nx
            if t >= preloaded:
                dma("sp", XT[xb][:, :], src_rows[t * 128:(t + 1) * 128, :], [], [("XT", xb)], ("xt", xb))
            P.op("act", lambda e: e.activation(out=HB[b][:, :], in_=XT[xb][:, :], func=AF.Square,
                                               accum_out=SS[:, b:b + 1]), [("XT", xb)], [("HB", b), ("SS", b)])
            P.op("act", lambda e: e.activation(out=SS[:, 4 + b:5 + b], in_=SS[:, b:b + 1], func=AF.Sqrt,
                                               scale=1.0 / D, bias=EPSB), [("SS", b), "CF"], [("SQ", b)])
            P.op("dve", lambda e: e.reciprocal(out=SS[:, 2 + b:3 + b], in_=SS[:, 4 + b:5 + b]), [("SQ", b)], [("RS", b)])
            P.op("dve", lambda e: e.scalar_tensor_tensor(out=HB[b][:, :], in0=XT[xb][:, :], scalar=SS[:, 2 + b:3 + b],
                                                         in1=GBC[:, :], op0=ALU.mult, op1=ALU.mult),
                 [("XT", xb), ("RS", b), "GBC"], [("HB", b)])

        def s2(t):
            b = t % 2
            for hh in range(2):
                pb = nxt("ntp", [0, 1, 2, 3])
                for j in range(8):
                    k = hh * 8 + j
                    P.op("pe", lambda e, k=k, j=j, pb=pb: e.transpose(
                        out=bank_bf(pb)[:, j * 128:(j + 1) * 128], in_=HB[b][:, k * 128:(k + 1) * 128], identity=ID_B),
                        [("HB", b), "CB"], [PB(pb)])
                if hh == 0:
                    P.op("act", lambda e, hh=hh, pb=pb: e.activation(
                        out=HTd[:, hh * 8:(hh + 1) * 8, t * 128:(t + 1) * 128],
                        in_=bank_bf(pb).rearrange("p (j t) -> p j t", j=8), func=AF.Copy),
                        [PB(pb)], [(htkey, t)])
                else:
                    P.op("dve", lambda e, hh=hh, pb=pb: e.tensor_copy(
                        out=HTd[:, hh * 8:(hh + 1) * 8, t * 128:(t + 1) * 128],
                        in_=bank_bf(pb).rearrange("p (j t) -> p j t", j=8)),
                        [PB(pb), (htkey, t)], [(htkey, t)])

        for t in range(ntiles):
            s1(t)
            if t >= 1:
                s2(t - 1)
            if t == 5 and wst["cap"] < 10 ** 6:
                wst["cap"] = 10 ** 6
                prefetch()
        s2(ntiles - 1)

    def rope_tmps():
        TB = [P.alloc(f"TB{i}", [128, 512], BF16, "Y") for i in range(3)]
        T1 = [P.alloc(f"T1{i}", [128, 512], F32, "Y") for i in range(3)]
        T2 = [P.alloc(f"T2{i}", [128, 512], F32, "Y") for i in range(3)]
        SQ = [P.alloc(f"SQ{i}", [128, 512], BF16, "Y") for i in range(3)]
        return TB, T1, T2, SQ

    def rope_evac(tm, pb, dest, destkey, tc, statcol):
        TB, T1, T2, SQ = tm
        r = nxt("rope", [0, 1, 2])

        def stage1():
            P.op("act", lambda e: e.activation(out=TB[r][:, :], in_=bank(pb), func=AF.Copy), [PB(pb)], [("TB", r)])
            pb2 = nxt("swb", [4, 5])
            P.op("pe", lambda e: e.matmul(bank(pb2), lhsT=SWAP_B, rhs=TB[r][:, :], start=True, stop=True),
                 [("TB", r), "CB"], [PB(pb2)])
            P.op("dve", lambda e: e.tensor_tensor(out=T1[r][:, :], in0=bank(pb), in1=COS[:, tc:tc + 512], op=ALU.mult),
                 [PB(pb), "COS%d" % (tc // NCTX)], [("T1", r)])
            P.op("dve", lambda e: e.tensor_tensor(out=T2[r][:, :], in0=bank(pb2), in1=SINS[:, tc:tc + 512], op=ALU.mult),
                 [PB(pb2), "SINS%d" % (tc // NCTX)], [("T2", r)])
            P.op("pool", lambda e: e.tensor_tensor(out=dest, in0=T1[r][:, :], in1=T2[r][:, :], op=ALU.add),
                 [("T1", r), ("T2", r)], [destkey])
            P.op("act", lambda e: e.activation(out=SQ[r][:, :], in_=dest, func=AF.Square), [destkey], [("SQ", r)])
            defer(1, stage2)

        def stage2():
            pb3 = nxt("stb", [6, 7])
            P.op("pe", lambda e: e.matmul(bank(pb3), lhsT=ONES_B, rhs=SQ[r][:, :], start=True, stop=True),
                 [("SQ", r), "CB"], [PB(pb3)])
            P.op("dve", lambda e: e.tensor_reduce(out=sv(statcol), in_=bank(pb3), axis=AX.X, op=ALU.max),
                 [PB(pb3)], [("SVc", statcol)])

        defer(1, stage1)

    rgen = [None]

    def kv_proj(ctx, tm):
        tbs = (0, 1) if ctx else (2, 3)
        def k_part():
            for s in range(2):
                i = wslab("cols", "w_in", C_KDA + s * 512, 16)
                for jl in range(4):
                    j = 4 * s + jl
                    for tb in tbs:
                        toff = (tb % 2) * 512
                        pb = nxt("kb", [0, 1, 2, 3])
                        for k in range(16):
                            P.op("pe", lambda e, i=i, k=k, jl=jl, toff=toff, pb=pb: e.matmul(
                                bank(pb), lhsT=WS[i][:, k, jl * 128:(jl + 1) * 128], rhs=HT[:, k, toff:toff + 512],
                                start=(k == 0), stop=(k == 15)),
                                [("WS", i)] + [("HT", toff // 128 + t) for t in range(4)], [PB(pb)])
                        tick()
                        rope_evac(tm, pb, KT[:, j, tb * 512:(tb + 1) * 512], ("KT", j, tb), tb * 512, SV_STK + j * 4 + tb)
                wrel()

        def v_part():
            for n in range(2):
                i = wslab("cols", "w_in", C_VDA + n * 512, 16)
                for tl in range(8):
                    tt = tl + (0 if ctx else 8)
                    pb = nxt("kb", [0, 1, 2, 3])
                    for k in range(16):
                        P.op("pe", lambda e, i=i, k=k, tl=tl, pb=pb: e.matmul(
                            bank(pb), lhsT=HT[:, k, tl * 128:(tl + 1) * 128], rhs=WS[i][:, k, :],
                            start=(k == 0), stop=(k == 15)), [("WS", i), ("HT", tl)], [PB(pb)])
                    tick()
                    P.op("act", lambda e, tt=tt, n=n, pb=pb: e.activation(
                        out=VA[:, tt, 2 * n:2 * n + 2, 0:256], in_=bank(pb).rearrange("p (h e) -> p h e", h=2), func=AF.Copy),
                        [PB(pb)], [("VA", tt, n)])
                    if rgen[0] is not None and tl % 3 == 2:
                        next(rgen[0], None)
                wrel()

        k_part()
        rgen[0] = rope_tables_gen(NCTX, NCTX + NTOK, 256, final_reset=False) if ctx else None
        v_part()
        if rgen[0] is not None:
            for _ in rgen[0]:
                pass
            rgen[0] = None

    P_tabs = {}

    def rope_tables(c_lo, c_hi, NH, final_reset=True):
        for _ in rope_tables_gen(c_lo, c_hi, NH, final_reset):
            pass
        return P_tabs["COS"], P_tabs["SINS"]

    def rope_tables_gen(c_lo, c_hi, NH, final_reset=True):
        if "COS" not in P_tabs:
            P.reset("X")
            P_tabs["COS"] = P.alloc("COS", [128, NCTX + NTOK], F32, "X")
            P_tabs["SINS"] = P.alloc("SINS", [128, NCTX + NTOK], F32, "X")
        COS, SINS = P_tabs["COS"], P_tabs["SINS"]
        m_in = P.mark("Y")
        POSI = P.alloc("POSI", [128, NH], I32, "Y")
        XA_ = P.alloc("XA", [128, NH], F32, "Y")
        TT_ = P.alloc("TT", [128, NH], F32, "Y")
        KI_ = P.alloc("KI", [128, NH], I32, "Y")
        C1 = float(np.float32(2 * PI))
        C2 = float(2 * PI - np.float64(np.float32(2 * PI)))
        for c0 in range(c_lo, c_hi, NH):
            cs = slice(c0, c0 + NH)
            dma("sp", POSI[:, :], posa[:, cs].partition_broadcast(128), [], ["POSI"], ("misc", 6))
            P.op("dve", lambda e: e.tensor_copy(out=TT_[:, :], in_=POSI[:, :]), ["POSI"], ["TT"])
            P.op("dve", lambda e: e.tensor_scalar(out=XA_[:, :], in0=TT_[:, :], scalar1=CF[:, CF_INVF:CF_INVF + 1], scalar2=None,
                                                  op0=ALU.mult), ["TT", "CF"], ["XA"])
            P.op("dve", lambda e: e.tensor_scalar(out=KI_[:, :], in0=XA_[:, :], scalar1=float(1.0 / (2 * PI)), scalar2=None,
                                                  op0=ALU.mult), ["XA"], ["KI"])
            P.op("dve", lambda e: e.tensor_copy(out=TT_[:, :], in_=KI_[:, :]), ["KI"], ["TT"])
            P.op("dve", lambda e: e.scalar_tensor_tensor(out=XA_[:, :], in0=TT_[:, :], scalar=-C1, in1=XA_[:, :],
                                                         op0=ALU.mult, op1=ALU.add), ["TT", "XA"], ["XA"])
            P.op("dve", lambda e: e.scalar_tensor_tensor(out=XA_[:, :], in0=TT_[:, :], scalar=-C2, in1=XA_[:, :],
                                                         op0=ALU.mult, op1=ALU.add), ["TT", "XA"], ["XA"])
            P.op("dve", lambda e: e.tensor_scalar(out=TT_[:, :], in0=XA_[:, :], scalar1=PI, scalar2=2 * PI,
                                                  op0=ALU.is_gt, op1=ALU.mult), ["XA"], ["TT"])
            P.op("dve", lambda e: e.tensor_tensor(out=XA_[:, :], in0=XA_[:, :], in1=TT_[:, :], op=ALU.subtract), ["XA", "TT"], ["XA"])
            P.op("dve", lambda e: e.tensor_scalar(out=TT_[:, :], in0=XA_[:, :], scalar1=-PI, scalar2=2 * PI,
                                                  op0=ALU.is_lt, op1=ALU.mult), ["XA"], ["TT"])
            P.op("dve", lambda e: e.tensor_tensor(out=XA_[:, :], in0=XA_[:, :], in1=TT_[:, :], op=ALU.add), ["XA", "TT"], ["XA"])
            P.op("act", lambda e, cs=cs: e.activation(out=SINS[:, cs], in_=XA_[:, :], func=AF.Sin, scale=CF[:, CF_SGN:CF_SGN + 1]),
                 ["XA", "CF"], ["SINS%d" % (c0 // NCTX)])
            P.op("dve", lambda e: e.tensor_scalar(out=TT_[:, :], in0=XA_[:, :], scalar1=PI / 2, scalar2=2 * PI,
                                                  op0=ALU.is_gt, op1=ALU.mult), ["XA", "SINS%d" % (c0 // NCTX)], ["TT"])
            P.op("dve", lambda e: e.scalar_tensor_tensor(out=XA_[:, :], in0=XA_[:, :], scalar=PI / 2, in1=TT_[:, :],
                                                         op0=ALU.add, op1=ALU.subtract), ["XA", "TT"], ["XA"])
            P.op("act", lambda e, cs=cs: e.activation(out=COS[:, cs], in_=XA_[:, :], func=AF.Sin), ["XA"], ["COS%d" % (c0 // NCTX)])
            yield
        if final_reset:
            P.reset("Y", m_in)
        else:
            P.zones["Y"][2] = m_in

    XTq = [QT[:, 0:4, :].rearrange("p a b -> p (a b)").bitcast(F32), QT[:, 4:8, :].rearrange("p a b -> p (a b)").bitcast(F32)]
    norm_transpose(xc, NCTX // 128, g_mix, HT, "HT", xt_alias=XTk, preloaded=4)
    for t in range(2):
        dma("sp", XTq[t][:, :], xo[t * 128:(t + 1) * 128, :], [], [("XT", t)], ("xt", t))
    P.reset("Y", mY)
    COS, SINS = rope_tables(0, NCTX, 1024)
    if stop == "A0":
        return finish()
    P.op("dve", lambda e: e.tensor_copy(out=HTH[:, :, :], in_=HT[:, :, 992:1024]), [("HT", 7)], ["HTH"])
    if "HTC" in dbg_out:
        for k in range(16):
            dump("HTC", HT[:, k, :], [("HT", t) for t in range(8)], rows=slice(k * 128, (k + 1) * 128))
    tm = rope_tmps()
    kv_proj(True, tm)
    flush()
    P.reset("Y", mY)
    norm_transpose(xo, NTOK // 128, g_mix, HT, "HT", xt_alias=XTq + XTv, preloaded=2)
    P.op("dve", lambda e: e.memset(VA[:, 8:16, :, 256:257], 1.0), [], [("XT", 2), ("XT", 3), "VA1"])
    P.reset("Y", mY)
    if "HTO" in dbg_out:
        for k in range(16):
            dump("HTO", HT[:, k, :], [("HT", t) for t in range(8)], rows=slice(k * 128, (k + 1) * 128))
    if stop == "A":
        return finish()
    tm = rope_tmps()
    kv_proj(False, tm)
    for s in range(2):
        i = wslab("cols", "w_in", C_QDA + s * 512, 16)
        for jl in range(4):
            j = 4 * s + jl
            for tbl in range(2):
                toff = tbl * 512
                pb = nxt("kb", [0, 1, 2, 3])
                for k in range(16):
                    P.op("pe", lambda e, i=i, k=k, jl=jl, toff=toff, pb=pb: e.matmul(
                        bank(pb), lhsT=WS[i][:, k, jl * 128:(jl + 1) * 128], rhs=HT[:, k, toff:toff + 512],
                        start=(k == 0), stop=(k == 15)),
                        [("WS", i)] + [("HT", toff // 128 + t) for t in range(4)], [PB(pb)])
                tick()
                rope_evac(tm, pb, QT[:, j, toff:toff + 512], ("QT", j, tbl), 1024 + toff, SV_STQ + j * 2 + tbl)
        wrel()
    flush()
    P.op("dve", lambda e: e.tensor_reduce(out=sv(SV_KMAX, 8), in_=sv(SV_STK, 32).rearrange("p (j t) -> p j t", t=4),
                                          axis=AX.X, op=ALU.max), [("SVc", SV_STK + c) for c in range(32)], ["KMAX"])
    P.op("dve", lambda e: e.tensor_reduce(out=sv(SV_QMAX, 8), in_=sv(SV_STQ, 16).rearrange("p (j t) -> p j t", t=2),
                                          axis=AX.X, op=ALU.max), [("SVc", SV_STQ + c) for c in range(16)], ["QMAX"])
    P.op("dve", lambda e: e.tensor_tensor(out=sv(SV_T, 8), in0=sv(SV_QMAX, 8), in1=sv(SV_KMAX, 8), op=ALU.mult),
         ["KMAX", "QMAX"], ["SVT0"])
    P.op("act", lambda e: e.activation(out=sv(SV_T + 8, 8), in_=sv(SV_T, 8), func=AF.Sqrt), ["SVT0"], ["SVT8"])
    P.op("dve", lambda e: e.tensor_scalar(out=sv(SV_NEGM, 8), in0=sv(SV_T + 8, 8), scalar1=float(-SCALE_DA), scalar2=None,
                                          op0=ALU.mult), ["SVT8"], ["NEGM"])
    P.op("dve", lambda e: e.tensor_scalar(out=sv(SV_NEGMC, 8), in0=sv(SV_NEGM, 8), scalar1=CF[:, CF_CBIAS:CF_CBIAS + 1],
                                          scalar2=None, op0=ALU.add), ["NEGM", "CF"], ["NEGMC"])
    P.reset("Y", mY)
    P.reset("X")
    if "KT" in dbg_out:
        for j in range(8):
            dump("KT", KT[:, j, :], [("KT", j, tb) for tb in range(4)], rows=slice(j * 128, (j + 1) * 128))
            dump("QT", QT[:, j, :], [("QT", j, tb) for tb in range(2)], rows=slice(j * 128, (j + 1) * 128))
        for tt in range(16):
            dump("VA", VA[:, tt, :, :].rearrange("p h e -> p (h e)"), [("VA", tt, 0), ("VA", tt, 1), "VA1"],
                 rows=slice(tt * 128, (tt + 1) * 128))
        dump("SVB", SV[:, :], ["NEGM", "NEGMC", "NEGLAM"])
    if stop == "B":
        return finish()

    OT_DA = P.alloc("OT_DA", [128, 8, 1024], BF16, "X")
    PTD = [P.alloc(f"PT{i}", [128, 512], BF16, "Y") for i in range(4)]
    O1 = P.alloc("O1", [128, 4, 257], F32, "Y")
    O2 = P.alloc("O2", [128, 4, 257], F32, "Y")
    OS = P.alloc("OS", [128, 4, 256], F32, "Y")
    ON = P.alloc("ON", [128, 4, 256], BF16, "Y")
    JK = P.alloc("JK", [128, 256], F32, "Y")
    RR = P.alloc("RR", [128, 32], F32, "Y")

    aotb_banks = [7]

    def attn_out_group(bufs, OTd, otkey, ch0, qcol0, subln, delay=4):
        O1g, O2g, OSg, ONg, RRg, JKg = bufs
        for qi in range(4):
            P.op("dve", lambda e, qi=qi: e.tensor_copy(out=O2g[:, qi, :], in_=bank(qi, 257)), [PB(qi)], [("O2", qi)])
        o2k = [("O2", qi) for qi in range(4)]
        def stage_d():
            for qi in range(4):
                P.op("dve", lambda e, qi=qi: e.scalar_tensor_tensor(out=ONg[:, qi, :], in0=OSg[:, qi, :], scalar=RRg[:, 20 + qi:21 + qi],
                                                                    in1=GSUB[:, :], op0=ALU.mult, op1=ALU.mult),
                     [("OS", qi), ("RR", 5), "GSUB"], [("ON", qi)])
            defer(2, part_b)

        def stage_c():
            for qi in range(4):
                P.op("dve", lambda e, qi=qi: e.scalar_tensor_tensor(out=JKg[:, :], in0=OSg[:, qi, :], scalar=1.0, in1=OSg[:, qi, :],
                                                                    op0=ALU.mult, op1=ALU.mult, accum_out=RRg[:, 12 + qi:13 + qi]),
                     [("OS", qi)], ["JK", ("RR", 3, qi)])
            defer(3, stage_c2)

        def stage_c2():
            P.op("act", lambda e: e.activation(out=RRg[:, 16:20], in_=RRg[:, 12:16], func=AF.Ln, scale=1.0 / 256, bias=EPSB),
                 [("RR", 3, qi) for qi in range(4)] + ["CF"], [("RR", 4)])
            P.op("act", lambda e: e.activation(out=RRg[:, 20:24], in_=RRg[:, 16:20], func=AF.Exp, scale=-0.5), [("RR", 4)], [("RR", 5)])
            defer(1, stage_d)

        def stage_b():
            o1k = [("O1", qi) for qi in range(4)]
            P.op("dve", lambda e: e.reciprocal(out=RRg[:, 0:4], in_=O1g[:, :, 256]), o1k, [("RR", 0)])
            P.op("dve", lambda e: e.reciprocal(out=RRg[:, 4:8], in_=O2g[:, :, 256]), o2k, [("RR", 1)])
            P.op("dve", lambda e: e.tensor_scalar(out=RRg[:, 8:12], in0=RRg[:, 4:8], scalar1=sv(SV_NEGLAM), scalar2=None, op0=ALU.mult),
                 [("RR", 1), "NEGLAM"], [("RR", 2)])
            for qi in range(4):
                P.op("dve", lambda e, qi=qi: e.tensor_scalar(out=OSg[:, qi, :], in0=O1g[:, qi, 0:256], scalar1=RRg[:, qi:qi + 1], scalar2=None,
                                                             op0=ALU.mult), [("O1", qi), ("RR", 0)], [("OS", qi)])
            for qi in range(4):
                P.op("dve", lambda e, qi=qi: e.scalar_tensor_tensor(out=OSg[:, qi, :], in0=O2g[:, qi, 0:256], scalar=RRg[:, 8 + qi:9 + qi],
                                                                    in1=OSg[:, qi, :], op0=ALU.mult, op1=ALU.add),
                     [("O2", qi), ("RR", 2), ("OS", qi)], [("OS", qi)])
            defer(2, stage_c)

        if subln:
            defer(1, stage_b)
        else:
            P.op("dve", lambda e: e.reciprocal(out=RRg[:, 0:4], in_=O2g[:, :, 256]), o2k, [("RR", 0)])
            for qi in range(4):
                P.op("dve", lambda e, qi=qi: e.tensor_scalar(out=ONg[:, qi, :], in0=O2g[:, qi, 0:256], scalar1=RRg[:, qi:qi + 1], scalar2=None,
                                                             op0=ALU.mult), [("O2", qi), ("RR", 0)], [("ON", qi)])

        def part_b():
            for qi in range(4):
                tb_ = nxt("aotb", aotb_banks)
                for ec in range(2):
                    P.op("pe", lambda e, ec=ec, qi=qi, tb_=tb_: e.transpose(out=bank_bf(tb_)[:, ec * 128:(ec + 1) * 128],
                                                                         in_=ONg[:, qi, ec * 128:(ec + 1) * 128], identity=ID_B),
                         [("ON", qi), "CB"], [PB(tb_)])
                qcol = qcol0 + qi * 128
                P.op("dve", lambda e, qcol=qcol, tb_=tb_: e.tensor_copy(out=OTd[:, ch0:ch0 + 2, qcol:qcol + 128],
                                                                      in_=bank_bf(tb_, 256).rearrange("p (c q) -> p c q", c=2)),
                     [PB(tb_)], [(otkey, ch0, qcol)])
        if not subln:
            defer(delay, part_b)

    def span_ap(key):
        return bank(key[1], 256)

    steps = []
    for hd in range(4):
        for qb in range(2):
            for mp in range(2):
                nkv = 8 + 4 * qb + 4
                for kt in range(nkv):
                    steps.append((hd, qb, mp, kt, kt == nkv - 1))
    stinfo = {}

    def emit_st(si):
        hd, qb, mp, kt, _ = steps[si]
        j = hd * 2 + mp
        if kt < 8:
            q_lo = 4 * qb
            bias = sv(SV_NEGMC + j)
            bkey = "NEGMC"
        else:
            q_lo = max(kt - 8, 4 * qb)
            bias = sv(SV_NEGM + j)
            bkey = "NEGM"
        q0 = q_lo * 128
        nq = (4 * qb + 4 - q_lo) * 128
        spb = nxt("spbd", [4, 5, 6])
        r = nxt("pt", [0, 1, 2, 3])
        P.op("pe", lambda e: e.matmul(bank(spb, nq), lhsT=KT[:, j, kt * 128:(kt + 1) * 128], rhs=QT[:, j, q0:q0 + nq],
                                      start=True, stop=True), [("KT", j, kt // 4), ("QT", j, qb)], [PB(spb)])
        P.op("act", lambda e: e.activation(out=PTD[r][:, 0:nq], in_=bank(spb, nq), func=AF.Exp, scale=float(SCALE_DA), bias=bias),
             [PB(spb), bkey], [("PT", r)])
        if kt >= 8 and (kt - 8) >= 4 * qb:
            P.op("dve", lambda e: e.tensor_tensor(out=PTD[r][:, 0:128], in0=PTD[r][:, 0:128], in1=TRI_B, op=ALU.mult),
                 [("PT", r), "CB"], [("PT", r)])
        stinfo[si] = (r, q_lo, nq)

    def emit_pv(si):
        hd, qb, mp, kt, last = steps[si]
        r, q_lo, nq = stinfo.pop(si)
        for t in range(nq // 128):
            qi = q_lo - 4 * qb + t
            P.op("pe", lambda e, t=t, qi=qi: e.matmul(
                bank(qi, 257), lhsT=PTD[r][:, t * 128:(t + 1) * 128], rhs=VA[:, kt, hd, :],
                start=(kt == 0), stop=(kt == 8 + 4 * qb + qi)),
                [("PT", r), ("VA", kt, hd // 2), "VA1"], [PB(qi)])
        if last:
            if mp == 0:
                for qi in range(4):
                    P.op("dve", lambda e, qi=qi: e.tensor_copy(out=O1[:, qi, :], in_=bank(qi, 257)),
                         [PB(qi)], [("O1", qi)])
            else:
                attn_out_group((O1, O2, OS, ON, RR, JK), OT_DA, "OTDA", hd * 2, 4 * qb * 128, True)

    emit_st(0)
    emit_st(1)
    for si in range(len(steps)):
        if si + 2 < len(steps):
            emit_st(si + 2)
        emit_pv(si)
        tick()
    flush()
    if "OTDA" in dbg_out:
        for c in range(8):
            dump("OTDA", OT_DA[:, c, :], [("OTDA", (c // 2) * 2, q * 128) for q in range(8)], rows=slice(c * 128, (c + 1) * 128))
    P.reset("Y")
    if stop == "C":
        return finish()

    CACT = P.alloc("CACT", [128, 8, 1024], BF16, "Y")
    QXT = P.alloc("QXT", [128, 8, 1024], BF16, "Y", top=True)
    mY2 = P.mark("Y")
    ACC = [P.alloc(f"ACC{c}", [128, 1024], F32, "Y") for c in range(8)]
    mY3 = P.mark("Y")
    U = [P.alloc(f"U{c}", [128, 1056], BF16, "Y") for c in range(8)]
    SGB = [P.alloc(f"SGB{i}", [128, 1056], BF16, "Y") for i in range(4)]
    DG = [P.alloc(f"DG{i}", [128, 31, 128], BF16, "Y") for i in range(2)]

    NPE = 31

    def dg_build(c):
        d = c % 2
        for jt in range(NPE):
            eng = "dve" if jt % 2 == 0 else "act"
            if eng == "dve":
                P.op("dve", lambda e, jt=jt: e.tensor_scalar(out=DG[d][:, jt, :], in0=ID_B, scalar1=WDWT[:, jt * 8 + c:jt * 8 + c + 1],
                                                             scalar2=None, op0=ALU.mult), ["CB", "WDWT"], [("DG", d, 0)])
            else:
                P.op("act", lambda e, jt=jt: e.activation(out=DG[d][:, jt, :], in_=ID_B, func=AF.Copy,
                                                          scale=WDWT[:, jt * 8 + c:jt * 8 + c + 1]), ["CB", "WDWT"], [("DG", d, 1)])

    def conv_chunk(c):
        d = c % 2
        pp = nxt("cvb", [4, 6])
        for half in range(2):
            for jt in range(NPE):
                P.op("pe", lambda e, jt=jt, half=half: e.matmul(
                    bank(pp + half), lhsT=DG[d][:, jt, :], rhs=U[c][:, 2 + jt + half * 512: 2 + jt + half * 512 + 512],
                    start=(jt == 0), stop=(jt == NPE - 1)), [("DG", d, 0), ("DG", d, 1), ("U", c)], [PB(pp + half)])
        P.op("act", lambda e: e.activation(out=ACC[c][:, :], in_=span(pp, 1024), func=AF.Identity, bias=VECT[:, c:c + 1]),
             [PB(pp), PB(pp + 1), "VECT"], [("ACC", c)])

    def conv_tail(c0, c1):
        for jt in range(NPE, 31):
            for c in (c0, c1):
                P.op("dve", lambda e, c=c, jt=jt: e.scalar_tensor_tensor(
                    out=ACC[c][:, :], in0=U[c][:, 2 + jt:2 + jt + 1024], scalar=WDWT[:, jt * 8 + c:jt * 8 + c + 1],
                    in1=ACC[c][:, :], op0=ALU.mult, op1=ALU.add), [("U", c), ("ACC", c), "WDWT"], [("ACC", c)])

    for cg in range(2):
        ib = wslab("cols", "w_in", C_GLU_B + cg * 512, 16)
        for cl in range(4):
            pbh = nxt("glu", [0, 1, 2, 3])
            for k in range(16):
                P.op("pe", lambda e, k=k, cl=cl, pbh=pbh, ib=ib: e.matmul(
                    bank(pbh, 32), lhsT=WS[ib][:, k, cl * 128:(cl + 1) * 128], rhs=HTH[:, k, :],
                    start=(k == 0), stop=(k == 15)), [("WS", ib), "HTH"], [PB(pbh)])
            P.op("act", lambda e, cl=cl, pbh=pbh: e.activation(out=SGB[cl][:, 0:32], in_=bank(pbh, 32), func=AF.Sigmoid),
                 [PB(pbh)], [("SGB", cl)])
            for half in range(2):
                pb_ = nxt("glu", [0, 1, 2, 3])
                for k in range(16):
                    P.op("pe", lambda e, k=k, cl=cl, half=half, pb_=pb_, ib=ib: e.matmul(
                        bank(pb_), lhsT=WS[ib][:, k, cl * 128:(cl + 1) * 128], rhs=HT[:, k, half * 512:(half + 1) * 512],
                        start=(k == 0), stop=(k == 15)), [("WS", ib)] + [("HT", half * 4 + t) for t in range(4)], [PB(pb_)])
                P.op("act", lambda e, cl=cl, half=half, pb_=pb_: e.activation(
                    out=SGB[cl][:, 32 + half * 512:32 + (half + 1) * 512], in_=bank(pb_), func=AF.Sigmoid), [PB(pb_)], [("SGB", cl)])
        wrel()
        ia = wslab("cols", "w_in", C_GLU_A + cg * 512, 16)
        for cl in range(4):
            c = cg * 4 + cl
            dg_build(c)
            pbh = nxt("glu", [0, 1, 2, 3])
            for k in range(16):
                P.op("pe", lambda e, k=k, cl=cl, pbh=pbh, ia=ia: e.matmul(
                    bank(pbh, 32), lhsT=WS[ia][:, k, cl * 128:(cl + 1) * 128], rhs=HTH[:, k, :],
                    start=(k == 0), stop=(k == 15)), [("WS", ia), "HTH"], [PB(pbh)])
            P.op("dve", lambda e, c=c, cl=cl, pbh=pbh: e.tensor_tensor(out=U[c][:, 0:32], in0=bank(pbh, 32), in1=SGB[cl][:, 0:32], op=ALU.mult),
                 [PB(pbh), ("SGB", cl)], [("U", c)])
            for half in range(2):
                pa = nxt("glu", [0, 1, 2, 3])
                for k in range(16):
                    P.op("pe", lambda e, k=k, cl=cl, half=half, pa=pa, ia=ia: e.matmul(
                        bank(pa), lhsT=WS[ia][:, k, cl * 128:(cl + 1) * 128], rhs=HT[:, k, half * 512:(half + 1) * 512],
                        start=(k == 0), stop=(k == 15)), [("WS", ia)] + [("HT", half * 4 + t) for t in range(4)], [PB(pa)])
                P.op("dve", lambda e, c=c, cl=cl, half=half, pa=pa: e.tensor_tensor(
                    out=U[c][:, 32 + half * 512:32 + (half + 1) * 512], in0=bank(pa),
                    in1=SGB[cl][:, 32 + half * 512:32 + (half + 1) * 512], op=ALU.mult), [PB(pa), ("SGB", cl)], [("U", c)])
            if cl >= 1:
                conv_chunk(c - 1)
            if cl == 2:
                conv_tail(c - 2, c - 1)
        wrel()
        conv_chunk(cg * 4 + 3)
        conv_tail(cg * 4 + 2, cg * 4 + 3)
    if "ACC" in dbg_out:
        for c in range(8):
            dump("ACC", ACC[c][:, :], [("ACC", c)], rows=slice(c * 128, (c + 1) * 128))
    P.reset("Y", mY3)
    AB = [P.alloc(f"AB{i}", [128, 1024], BF16, "Y") for i in range(2)]
    SQL = [P.alloc(f"SQL{i}", [128, 1024], BF16, "Y") for i in range(2)]
    MEAN = P.alloc("MEAN", [128, 1024], F32, "Y")
    RSTD = P.alloc("RSTD", [128, 1024], F32, "Y")
    TL = [P.alloc(f"TL{i}", [128, 1024], F32, "Y") for i in range(2)]
    xtm_off = P.zones["Y"][0] + 72 * 1024
    assert P.mark("Y") <= xtm_off and xtm_off + 16384 <= P.zones["Y"][3]
    XTm = [P.alloc_at(f"XTm{i}", [128, D], F32, xtm_off + i * 8192)[:, :] for i in range(2)]
    for t in range(2):
        dma("sp", XTm[t][:, :], memx[t * 128:(t + 1) * 128, :], [], [("XT", t)], ("xt", t))
    def qxa_blocks():
        for s_ in range(2):
            i = wslab("cols", "w_in", C_QXA + s_ * 512, 16)
            for jl in range(4):
                j = 4 * s_ + jl
                for half in range(2):
                    pb = nxt("qxb", [4, 5, 6, 7])
                    for k in range(16):
                        P.op("pe", lambda e, i=i, k=k, jl=jl, half=half, pb=pb: e.matmul(
                            bank(pb), lhsT=WS[i][:, k, jl * 128:(jl + 1) * 128], rhs=HT[:, k, half * 512:(half + 1) * 512],
                            start=(k == 0), stop=(k == 15)), [("WS", i)] + [("HT", half * 4 + t) for t in range(4)], [PB(pb)])
                    P.op("act", lambda e, j=j, half=half, pb=pb: e.activation(out=QXT[:, j, half * 512:(half + 1) * 512], in_=bank(pb),
                                                                             func=AF.Copy), [PB(pb)], [("QXT", j, half)])
                    yield
            wrel()
    qgen = qxa_blocks()

    def qstep():
        next(qgen, None)

    for c in range(8):
        r = c % 2
        P.op("act", lambda e, c=c, r=r: e.activation(out=AB[r][:, :], in_=ACC[c][:, :], func=AF.Copy), [("ACC", c)], [("AB", r)])
        P.op("act", lambda e, c=c, r=r: e.activation(out=SQL[r][:, :], in_=ACC[c][:, :], func=AF.Square), [("ACC", c)], [("SQL", r)])
        qstep()
        for half in range(2):
            P.op("pe", lambda e, c=c, r=r, half=half: e.matmul(bank(half), lhsT=ONES_B, rhs=AB[r][:, half * 512:(half + 1) * 512],
                                                              start=(c == 0), stop=(c == 7)), [("AB", r), "CB"], [PB(half)])
            P.op("pe", lambda e, c=c, r=r, half=half: e.matmul(bank(2 + half), lhsT=ONES_B, rhs=SQL[r][:, half * 512:(half + 1) * 512],
                                                              start=(c == 0), stop=(c == 7)), [("SQL", r), "CB"], [PB(2 + half)])
    P.op("act", lambda e: e.activation(out=MEAN[:, :], in_=span(0, 1024), func=AF.Copy, scale=1.0 / 1024), [PB(0), PB(1)], ["MEAN"])
    P.op("pool", lambda e: e.tensor_tensor(out=RSTD[:, :], in0=MEAN[:, :], in1=MEAN[:, :], op=ALU.mult), ["MEAN"], ["RSTD"])
    P.op("dve", lambda e: e.scalar_tensor_tensor(out=RSTD[:, :], in0=span(2, 1024), scalar=1.0 / 1024, in1=RSTD[:, :],
                                                 op0=ALU.mult, op1=ALU.subtract), [PB(2), PB(3), "RSTD"], ["RSTD"])
    P.op("act", lambda e: e.activation(out=RSTD[:, :], in_=RSTD[:, :], func=AF.Ln, bias=EPSB), ["RSTD", "CF"], ["RSTD"])
    P.op("act", lambda e: e.activation(out=RSTD[:, :], in_=RSTD[:, :], func=AF.Exp, scale=-0.5), ["RSTD"], ["RSTD"])

    def ln_apply(c):
        r = c % 2
        if c % 2 == 0:
            P.op("pool", lambda e: e.tensor_tensor(out=TL[r][:, :], in0=ACC[c][:, :], in1=MEAN[:, :], op=ALU.subtract),
                 [("ACC", c), "MEAN"], [("TL", r)])
        else:
            P.op("dve", lambda e: e.tensor_tensor(out=TL[r][:, :], in0=ACC[c][:, :], in1=MEAN[:, :], op=ALU.subtract),
                 [("ACC", c), "MEAN"], [("TL", r)])
        P.op("dve", lambda e: e.tensor_tensor(out=TL[r][:, :], in0=TL[r][:, :], in1=RSTD[:, :], op=ALU.mult),
             [("TL", r), "RSTD"], [("TL", r)])
        P.op("act", lambda e: e.activation(out=CACT[:, c, :], in_=TL[r][:, :], func=AF.Silu,
                                           scale=VECT[:, 8 + c:9 + c], bias=VECT[:, 16 + c:17 + c]),
             [("TL", r), "VECT"], [("CACT", c)])

    for c in range(8):
        qstep()
        ln_apply(c)
    for _ in range(16):
        qstep()
    if "CACT" in dbg_out:
        for c in range(8):
            dump("CACT", CACT[:, c, :], [("CACT", c)], rows=slice(c * 128, (c + 1) * 128))
    P.reset("Y", mY2)
    if stop == "D":
        return finish()

    OT_XA = P.alloc("OT_XA", [128, 8, 1024], BF16, "Y")
    mY4 = P.mark("Y")
    MEMT = P.alloc("MEMT", [128, 16, 256], BF16, "Y")
    KMT = P.alloc("KMT", [128, 8, 256], BF16, "Y")
    VMA = P.alloc("VMA", [128, 2, 4, 257], BF16, "Y")
    mY5 = P.mark("Y")
    P.op("dve", lambda e: e.memset(VMA[:, :, :, 256:257], 1.0), [], ["VMA1"])
    norm_transpose(memx, 2, g_mem, MEMT, "MEMT", xt_alias=XTm, preloaded=2)
    assert P.mark("Y") <= xtm_off
    P.reset("Y", mY5)
    aotb_banks[:] = [7]
    SQX = [P.alloc(f"SQX{i}", [128, 512], BF16, "Y") for i in range(4)]
    PTX = [P.alloc(f"PTx{i}", [128, 512], BF16, "Y") for i in range(4)]
    ONX = P.alloc("ONx", [128, 4, 256], BF16, "Y")
    O2X = P.alloc("O2x", [128, 4, 257], F32, "Y")
    RRX = P.alloc("RRx", [128, 32], F32, "Y")
    assert P.mark("Y") <= xtm_off
    for h in range(4):
        for half in range(2):
            pb3 = nxt("stb", [6, 7])
            for cc in range(2):
                r = nxt("sqx", [0, 1, 2, 3])
                P.op("act", lambda e, h=h, half=half, cc=cc, r=r: e.activation(
                    out=SQX[r][:, :], in_=QXT[:, 2 * h + cc, half * 512:(half + 1) * 512], func=AF.Square),
                    [("QXT", 2 * h + cc, half)], [("SQX", r)])
                P.op("pe", lambda e, r=r, cc=cc, pb3=pb3: e.matmul(bank(pb3), lhsT=ONES_B, rhs=SQX[r][:, :], start=(cc == 0), stop=(cc == 1)),
                     [("SQX", r), "CB"], [PB(pb3)])
            P.op("dve", lambda e, h=h, half=half, pb3=pb3: e.tensor_reduce(out=sv(SV_STX + h * 2 + half), in_=bank(pb3), axis=AX.X, op=ALU.max),
                 [PB(pb3)], [("SVc", SV_STX + h * 2 + half)])
    P.op("dve", lambda e: e.tensor_reduce(out=sv(SV_QXMAX, 4), in_=sv(SV_STX, 8).rearrange("p (j t) -> p j t", t=2),
                                          axis=AX.X, op=ALU.max), [("SVc", SV_STX + c) for c in range(8)], ["QXMAX"])
    for hp in range(2):
        i = wslab("cols", "w_mem_kv", hp * 512, 16)
        for jl in range(4):
            j = 4 * hp + jl
            pb = nxt("kb", [0, 1, 2, 3])
            for k in range(16):
                P.op("pe", lambda e, i=i, k=k, jl=jl, pb=pb: e.matmul(
                    bank(pb, 256), lhsT=WS[i][:, k, jl * 128:(jl + 1) * 128], rhs=MEMT[:, k, :], start=(k == 0), stop=(k == 15)),
                    [("WS", i), ("MEMT", 0), ("MEMT", 1)], [PB(pb)])
            P.op("act", lambda e, j=j, pb=pb: e.activation(out=KMT[:, j, :], in_=bank(pb, 256), func=AF.Copy), [PB(pb)], [("KMT", j)])
        wrel()
        i = wslab("cols", "w_mem_kv", 1024 + hp * 512, 16)
        for mt in range(2):
            pb = nxt("kb", [0, 1, 2, 3])
            for k in range(16):
                P.op("pe", lambda e, i=i, k=k, mt=mt, pb=pb: e.matmul(
                    bank(pb), lhsT=MEMT[:, k, mt * 128:(mt + 1) * 128], rhs=WS[i][:, k, :], start=(k == 0), stop=(k == 15)),
                    [("WS", i), ("MEMT", mt)], [PB(pb)])
            P.op("act", lambda e, mt=mt, hp=hp, pb=pb: e.activation(
                out=VMA[:, mt, 2 * hp:2 * hp + 2, 0:256], in_=bank(pb).rearrange("p (h e) -> p h e", h=2), func=AF.Copy),
                [PB(pb)], [("VMA", mt, hp)])
        wrel()
        for h in (2 * hp, 2 * hp + 1):
            pb3 = nxt("stb", [6, 7])
            for cc in range(2):
                r = nxt("sqx", [0, 1, 2, 3])
                P.op("act", lambda e, h=h, cc=cc, r=r: e.activation(out=SQX[r][:, 0:256], in_=KMT[:, 2 * h + cc, :], func=AF.Square),
                     [("KMT", 2 * h + cc)], [("SQX", r)])
                P.op("pe", lambda e, r=r, cc=cc, pb3=pb3: e.matmul(bank(pb3, 256), lhsT=ONES_B, rhs=SQX[r][:, 0:256], start=(cc == 0), stop=(cc == 1)),
                     [("SQX", r), "CB"], [PB(pb3)])
            P.op("dve", lambda e, h=h, pb3=pb3: e.tensor_reduce(out=sv(SV_KMMAX + h), in_=bank(pb3, 256), axis=AX.X, op=ALU.max),
                 [PB(pb3)], [("SVk", h)])
        h0 = 2 * hp
        P.op("dve", lambda e, h0=h0: e.tensor_tensor(out=sv(SV_T + h0, 2), in0=sv(SV_QXMAX + h0, 2), in1=sv(SV_KMMAX + h0, 2), op=ALU.mult),
             ["QXMAX", ("SVk", h0), ("SVk", h0 + 1)], [("SVT0x", hp)])
        P.op("act", lambda e, h0=h0: e.activation(out=sv(SV_T + 8 + h0, 2), in_=sv(SV_T + h0, 2), func=AF.Sqrt), [("SVT0x", hp)], [("SVT8x", hp)])
        P.op("dve", lambda e, h0=h0: e.tensor_scalar(out=sv(SV_NEGMX + h0, 2), in0=sv(SV_T + 8 + h0, 2), scalar1=float(-SCALE_XA), scalar2=None,
                                                     op0=ALU.mult), [("SVT8x", hp)], [("NEGMX", hp)])
        xsteps = [(h, qb, mt) for h in (2 * hp, 2 * hp + 1) for qb in range(2) for mt in range(2)]
        xinfo = {}

        def x_st(si):
            h, qb, mt = xsteps[si]
            spb = nxt("spbx", [4, 5, 6])
            r = nxt("pt", [0, 1, 2, 3])
            for cc in range(2):
                P.op("pe", lambda e, cc=cc: e.matmul(
                    bank(spb), lhsT=KMT[:, 2 * h + cc, mt * 128:(mt + 1) * 128], rhs=QXT[:, 2 * h + cc, qb * 512:(qb + 1) * 512],
                    start=(cc == 0), stop=(cc == 1)), [("KMT", 2 * h + cc), ("QXT", 2 * h + cc, qb)], [PB(spb)])
            P.op("act", lambda e: e.activation(out=PTX[r][:, :], in_=bank(spb), func=AF.Exp,
                                               scale=float(SCALE_XA), bias=sv(SV_NEGMX + h)),
                 [PB(spb), ("NEGMX", hp)], [("PT", r)])
            xinfo[si] = r

        def x_pv(si):
            h, qb, mt = xsteps[si]
            r = xinfo.pop(si)
            for qi in range(4):
                P.op("pe", lambda e, qi=qi: e.matmul(
                    bank(qi, 257), lhsT=PTX[r][:, qi * 128:(qi + 1) * 128], rhs=VMA[:, mt, h, :], start=(mt == 0), stop=(mt == 1)),
                    [("PT", r), ("VMA", mt, h // 2), "VMA1"], [PB(qi)])
            if mt == 1:
                attn_out_group((None, O2X, None, ONX, RRX, None), OT_XA, "OTXA", h * 2, 4 * qb * 128, False, delay=2)

        x_st(0)
        x_st(1)
        for si in range(len(xsteps)):
            if si + 2 < len(xsteps):
                x_st(si + 2)
            x_pv(si)
            tick()
        flush()
    flush()
    if "OTXA" in dbg_out:
        for c in range(8):
            dump("OTXA", OT_XA[:, c, :], [("OTXA", (c // 2) * 2, q * 128) for q in range(8)], rows=slice(c * 128, (c + 1) * 128))
    P.reset("Y", mY4, top=True)
    if stop == "E":
        return finish()

    MERGED = P.alloc("MERGED", [128, 16, 1024], BF16, "Y", top=True)
    MG = [P.alloc(f"MG{i}", [128, 1024], F32, "Y") for i in range(4)]
    SIGB = [[P.alloc(f"SIGB{a}{i}", [128, 1024], BF16, "Y") for i in range(4)] for a in range(2)]
    TMPm = [P.alloc(f"TMPm{i}", [128, 512], F32, "Y") for i in range(2)]
    branches = [("w_conv_out", CACT, "CACT"), ("w_da_out", OT_DA, "OTDA"), ("w_xa_out", OT_XA, "OTXA")]

    def act_keys(r, k, half):
        if r == 0:
            return [("CACT", k)]
        nm = "OTDA" if r == 1 else "OTXA"
        return [(nm, (k // 2) * 2, (half * 4 + t) * 128) for t in range(4)]

    for cg in range(4):
        for r in range(3):
            wsrc, ACTr, _ = branches[r]
            sa = nxt("sigb", [0, 1])
            ig = wslab("cols", "w_in", C_GATE + r * 2048 + cg * 512, 16)
            for cl in range(4):
                for half in range(2):
                    pg = nxt("mgg", [0, 1, 2, 3])
                    hs = slice(half * 512, (half + 1) * 512)
                    for k in range(16):
                        P.op("pe", lambda e, ig=ig, k=k, cl=cl, hs=hs, pg=pg: e.matmul(
                            bank(pg), lhsT=WS[ig][:, k, cl * 128:(cl + 1) * 128], rhs=HT[:, k, hs],
                            start=(k == 0), stop=(k == 15)), [("WS", ig)] + [("HT", half * 4 + t) for t in range(4)], [PB(pg)])
                    P.op("act", lambda e, sa=sa, cl=cl, hs=hs, pg=pg: e.activation(out=SIGB[sa][cl][:, hs], in_=bank(pg), func=AF.Sigmoid),
                         [PB(pg)], [("SIGB", sa, cl, half)])
            wrel()
            io = wslab("cols", wsrc, cg * 512, 8)
            for cl in range(4):
                c = cg * 4 + cl
                for half in range(2):
                    py = nxt("mgy", [4, 5, 6, 7])
                    hs = slice(half * 512, (half + 1) * 512)
                    for k in range(8):
                        P.op("pe", lambda e, io=io, k=k, cl=cl, hs=hs, py=py, ACTr=ACTr: e.matmul(
                            bank(py), lhsT=WS[io][:, k, cl * 128:(cl + 1) * 128], rhs=ACTr[:, k, hs],
                            start=(k == 0), stop=(k == 7)), [("WS", io)] + act_keys(r, k, half), [PB(py)])
                    if r == 0:
                        P.op("dve", lambda e, sa=sa, py=py, cl=cl, hs=hs: e.tensor_tensor(out=MG[cl][:, hs], in0=bank(py), in1=SIGB[sa][cl][:, hs], op=ALU.mult),
                             [PB(py), ("SIGB", sa, cl, half)], [("MG", cl, half)])
                    else:
                        q = nxt("tmpm", [0, 1])
                        P.op("dve", lambda e, sa=sa, q=q, py=py, cl=cl, hs=hs: e.tensor_tensor(out=TMPm[q][:, :], in0=bank(py), in1=SIGB[sa][cl][:, hs], op=ALU.mult),
                             [PB(py), ("SIGB", sa, cl, half)], [("TMPm", q)])
                        if r == 1:
                            P.op("pool", lambda e, q=q, cl=cl, hs=hs: e.tensor_tensor(out=MG[cl][:, hs], in0=MG[cl][:, hs], in1=TMPm[q][:, :], op=ALU.add),
                                 [("MG", cl, half), ("TMPm", q)], [("MG", cl, half)])
                        else:
                            P.op("pool", lambda e, q=q, cl=cl, hs=hs, c=c: e.tensor_tensor(out=MERGED[:, c, hs], in0=MG[cl][:, hs], in1=TMPm[q][:, :], op=ALU.add),
                                 [("MG", cl, half), ("TMPm", q)], [("MERGED", c, half)])
            wrel()
    if "MERGED" in dbg_out:
        for c in range(16):
            dump("MERGED", MERGED[:, c, :], [("MERGED", c, 0), ("MERGED", c, 1)], rows=slice(c * 128, (c + 1) * 128))
    P.reset("Y")
    P.reset("H")
    P.reset("X")
    if stop == "F":
        return finish()

    X1T = ([P.alloc(f"X1T{c}", [128, 1024], F32, "H") for c in range(8)]
           + [P.alloc(f"X1T{c}", [128, 1024], F32, "X") for c in range(8, 12)]
           + [P.alloc(f"X1T{c}", [128, 1024], F32, "Y") for c in range(12, 16)])
    H2T = P.alloc("H2T", [128, 16, 1024], BF16, "Y")
    RS2 = P.alloc("RS2", [128, 1024], F32, "Y")
    mY6 = P.mark("Y")
    XR = [P.alloc(f"XR{i}", [128, 8, 128], F32, "Y") for i in range(2)]
    SQm = [P.alloc(f"SQm{i}", [128, 1024], BF16, "Y") for i in range(2)]

    def ssq_mm(c, q):
        for half in range(2):
            P.op("pe", lambda e, half=half: e.matmul(bank(6 + half), lhsT=ONES_B, rhs=SQm[q][:, half * 512:(half + 1) * 512],
                                                     start=(c == 0), stop=(c == 15)), [("SQm", q), "CB"], [PB(6 + half)])

    for cg in range(4):
        i = wslab("cols", "w_mix_out", cg * 512, 16)
        for cl in range(4):
            c = cg * 4 + cl
            q = c % 2
            dma("sp", XR[q][:, :, :], xo[:, c * 128:(c + 1) * 128].rearrange("(t p) c -> p t c", p=128), [], [("XR", q)], ("xr", q))
            pp = nxt("mix", [0, 2, 4])
            for half in range(2):
                pb = pp + half
                for k in range(16):
                    P.op("pe", lambda e, i=i, k=k, cl=cl, half=half, pb=pb: e.matmul(
                        bank(pb), lhsT=WS[i][:, k, cl * 128:(cl + 1) * 128], rhs=MERGED[:, k, half * 512:(half + 1) * 512],
                        start=(k == 0), stop=False), [("WS", i), ("MERGED", k, half)], [PB(pb)])
                for tl in range(4):
                    P.op("pe", lambda e, q=q, half=half, tl=tl, pb=pb: e.matmul(
                        bank(pb, 128, tl * 128), lhsT=XR[q][:, half * 4 + tl, :], rhs=ID_F, start=False, stop=(tl == 3)),
                        [("XR", q), "CF"], [PB(pb)])
            tick()
            P.op("act", lambda e, c=c, pp=pp: e.activation(out=X1T[c][:, :], in_=span(pp, 1024), func=AF.Copy), [PB(pp), PB(pp + 1)], [("X1T", c)])
            P.op("act", lambda e, c=c, pp=pp: e.activation(out=H2T[:, c, :], in_=span(pp, 1024), func=AF.Copy, scale=VECT[:, 24 + c:25 + c]),
                 [PB(pp), PB(pp + 1), "VECT"], [("H2T", c)])
            P.op("act", lambda e, q=q, pp=pp: e.activation(out=SQm[q][:, :], in_=span(pp, 1024), func=AF.Square), [PB(pp), PB(pp + 1)], [("SQm", q)])
            defer(1, lambda c=c, q=q: ssq_mm(c, q))
        wrel()
    flush()
    P.op("act", lambda e: e.activation(out=RS2[:, :], in_=span(6, 1024), func=AF.Ln, scale=1.0 / D, bias=EPSB), [PB(6), PB(7), "CF"], ["RS2"])
    P.op("act", lambda e: e.activation(out=RS2[:, :], in_=RS2[:, :], func=AF.Exp, scale=-0.5), ["RS2"], ["RS2"])
    if "X1T" in dbg_out:
        for c in range(16):
            dump("X1T", X1T[c][:, :], [("X1T", c)], rows=slice(c * 128, (c + 1) * 128))
            dump("H2T", H2T[:, c, :], [("H2T", c)], rows=slice(c * 128, (c + 1) * 128))
    P.reset("Y", mY6, top=True)
    if stop == "G":
        return finish()

    ACTG = [P.alloc(f"ACTG{i}", [128, 4, 1024], BF16, "Y") for i in range(2)]
    RL = [P.alloc(f"RL{i}", [128, 1024], BF16, "Y") for i in range(2)]
    TRL = [P.alloc(f"TRL{i}", [128, 1024], BF16, "Y") for i in range(2)]

    def ffn_up(g):
        iu = wslab("cols", "w_up", g * 512, 16)
        for f in range(4):
            pp = nxt("up", [0, 2])
            for half in range(2):
                for k in range(16):
                    P.op("pe", lambda e, iu=iu, k=k, f=f, half=half, pp=pp: e.matmul(
                        bank(pp + half), lhsT=WS[iu][:, k, f * 128:(f + 1) * 128], rhs=H2T[:, k, half * 512:(half + 1) * 512],
                        start=(k == 0), stop=(k == 15)), [("WS", iu), ("H2T", k)], [PB(pp + half)])
            q = nxt("rl", [0, 1])
            P.op("act", lambda e, q=q, pp=pp: e.activation(out=RL[q][:, :], in_=span(pp, 1024), func=AF.Relu), [PB(pp), PB(pp + 1)], [("RL", q)])
            P.op("dve", lambda e, q=q: e.tensor_tensor(out=TRL[q][:, :], in0=RL[q][:, :], in1=RS2[:, :], op=ALU.mult),
                 [("RL", q), "RS2"], [("TRL", q)])
            P.op("dve", lambda e, q=q, g=g, f=f: e.tensor_tensor(out=ACTG[g % 2][:, f, :], in0=TRL[q][:, :], in1=TRL[q][:, :], op=ALU.mult),
                 [("TRL", q)], [("ACTG", g % 2, f)])
        wrel()

    def ffn_down(g):
        idn = wslab("rows", g)
        WD = wd_view(idn)
        for c in range(16):
            pp = nxt("dn", [4, 6])
            for half in range(2):
                for f in range(4):
                    P.op("pe", lambda e, WD=WD, f=f, c=c, half=half, pp=pp, g=g: e.matmul(
                        bank(pp + half), lhsT=WD[:, f, c * 128:(c + 1) * 128], rhs=ACTG[g % 2][:, f, half * 512:(half + 1) * 512],
                        start=(f == 0), stop=(f == 3)), [("WS", idn), ("ACTG", g % 2, f)], [PB(pp + half)])
            P.op("dve", lambda e, c=c, pp=pp: e.tensor_tensor(out=X1T[c][:, :], in0=span(pp, 1024), in1=X1T[c][:, :], op=ALU.add),
                 [PB(pp), PB(pp + 1), ("X1T", c)], [("X1T", c)])
        wrel()

    NG = DFF // 512
    ffn_up(0)
    for g in range(NG):
        if g + 1 < NG:
            ffn_up(g + 1)
        ffn_down(g)
    if "X2T" in dbg_out:
        for c in range(16):
            dump("X2T", X1T[c][:, :], [("X1T", c)], rows=slice(c * 128, (c + 1) * 128))
    P.reset("Y", mY6)

    GF = P.alloc("GF", [128, D], F32, "Y")
    YT = [P.alloc(f"YT{i}", [128, D], F32, "Y") for i in range(2)]
    JKF = P.alloc("JKF", [128, D], BF16, "Y")
    SF = P.alloc("SF", [128, 8], F32, "Y")
    dma("sp", GF[:, :], g_final.partition_broadcast(128), [], ["GF"], ("misc", 8))
    for t in range(8):
        st = t % 2
        for c in range(16):
            P.op("pe", lambda e, c=c, t=t, st=st: e.transpose(out=PS[:, st * 2048 + c * 128: st * 2048 + (c + 1) * 128],
                                                             in_=X1T[c][:, t * 128:(t + 1) * 128], identity=ID_F),
                 [("X1T", c), "CF"], [PB(st * 4 + c // 4)])
        pkeys = [PB(st * 4 + b) for b in range(4)]
        P.op("act", lambda e, st=st: e.activation(out=JKF[:, :], in_=PS[:, st * 2048:(st + 1) * 2048], func=AF.Square,
                                                  accum_out=SF[:, st:st + 1]), pkeys, ["JKF", ("SF", st)])
        P.op("act", lambda e, st=st: e.activation(out=SF[:, 2 + st:3 + st], in_=SF[:, st:st + 1], func=AF.Sqrt, scale=1.0 / D, bias=EPSB),
             [("SF", st), "CF"], [("SF2", st)])
        P.op("dve", lambda e, st=st: e.reciprocal(out=SF[:, 4 + st:5 + st], in_=SF[:, 2 + st:3 + st]), [("SF2", st)], [("SF4", st)])
        P.op("dve", lambda e, st=st: e.scalar_tensor_tensor(out=YT[st][:, :], in0=PS[:, st * 2048:(st + 1) * 2048], scalar=SF[:, 4 + st:5 + st],
                                                            in1=GF[:, :], op0=ALU.mult, op1=ALU.mult), pkeys + [("SF4", st), "GF"], [("YT", st)])
        dma("sp", y[t * 128:(t + 1) * 128, :], YT[st][:, :], [("YT", st)], [("y", t)], ("yo", st))
    return finish()


def make_consts(half):
    cf = np.zeros((128, NCF), np.float32)
    cf[:, CF_ID:CF_ID + 128] = np.eye(128, dtype=np.float32)
    cf[:, CF_ONES:CF_ONES + 128] = 1.0
    k = np.arange(128)[:, None]
    q = np.arange(128)[None, :]
    cf[:, CF_TRI:CF_TRI + 128] = (q >= k).astype(np.float32)
    cf[:, CF_SWAP:CF_SWAP + 128] = (k == (q + 64) % 128).astype(np.float32)
    inv_freq = (1.0 / (np.float32(10000.0) ** (np.arange(0, 128, 2, dtype=np.float32) / np.float32(128)))).astype(np.float32)
    cf[:, CF_INVF] = np.concatenate([inv_freq, inv_freq])
    cf[:64, CF_SGN] = -1.0
    cf[64:, CF_SGN] = 1.0
    cf[:, CF_CBIAS] = 0.0 if half == 1 else -30000.0
    cf[:, CF_EPS] = EPS
    return cf


def make_in_maps(inp, cores=range(8)):
    x = np.asarray(inp["x"], np.float32)
    mem = np.asarray(inp["mem"], np.float32)
    pos = np.asarray(inp["positions"], np.int32)
    f = lambda k: np.ascontiguousarray(np.asarray(inp[k], np.float32))
    shared = {
        "g_mix": f("g_mix").reshape(1, D),
        "w_in": f("w_in").reshape(D, DIN),
        "w_dw": f("w_dw").reshape(31 * 8, 128),
        "vecs": np.concatenate([f("b_dw").reshape(8, 128), f("g_conv_ln").reshape(8, 128),
                                f("b_conv_ln").reshape(8, 128), f("g_mlp").reshape(16, 128)], axis=0),
        "w_conv_out": f("w_conv_out").reshape(1024, D),
        "lams": np.concatenate([f("lambda_q1").reshape(1, 128), f("lambda_k1").reshape(1, 128),
                                f("lambda_q2").reshape(1, 128), f("lambda_k2").reshape(1, 128)], axis=0),
        "g_subln": f("g_subln").reshape(1, 256),
        "w_da_out": f("w_da_out").reshape(1024, D),
        "g_mem": f("g_mem").reshape(1, D),
        "w_mem_kv": f("w_mem_kv").reshape(D, D),
        "w_xa_out": f("w_xa_out").reshape(1024, D),
        "w_mix_out": f("w_mix_out").reshape(D, D),
        "w_up": f("w_up").reshape(D, DFF),
        "w_down": f("w_down").reshape(DFF, D),
        "g_final": f("g_final").reshape(1, D),
    }
    maps = []
    for c in cores:
        b, half = c // 2, c % 2
        m = dict(shared)
        m["xo"] = np.ascontiguousarray(x[b, half * NTOK:(half + 1) * NTOK])
        if half == 1:
            m["xc"] = np.ascontiguousarray(x[b, 0:NCTX])
            pc = pos[b, 0:NCTX]
        else:
            m["xc"] = np.zeros((NCTX, D), np.float32)
            pc = np.zeros((NCTX,), np.int32)
        m["posa"] = np.concatenate([pc, pos[b, half * NTOK:(half + 1) * NTOK]]).reshape(1, -1).astype(np.int32)
        m["memx"] = np.ascontiguousarray(mem[b])
        m["cf"] = make_consts(half)
        maps.append(m)
    return maps


_NC_CACHE = {}


def kernel(**inputs):
    if "nc" not in _NC_CACHE:
        plan = build().plan_out
        _NC_CACHE["nc"] = build(plan=plan)
    nc = _NC_CACHE["nc"]
    maps = make_in_maps(inputs)
    res = run_bass_kernel_spmd(nc, maps, core_ids=list(range(8)))
    out = np.zeros((B, S, D), np.float32)
    for c in range(8):
        b, half = c // 2, c % 2
        out[b, half * NTOK:(half + 1) * NTOK] = res.results[c]["y"]
    return out
```

```python
import math
import numpy as np
import concourse.bass as bass
import concourse.mybir as mybir
from concourse.bass_utils import run_bass_kernel_spmd

F32 = mybir.dt.float32
BF16 = mybir.dt.bfloat16
I32 = mybir.dt.int32
AF = mybir.ActivationFunctionType
ALU = mybir.AluOpType
AX = mybir.AxisListType

D = 2048
S = 2048
B = 4
NTOK = 1024
NCTX = 1024
MEM = 256
DIN = 12288
DFF = 8192
EPS = 1e-6
LAMBDA_INIT = 0.8 - 0.6 * math.exp(0.0)
PI = math.pi
SCALE_DA = 128 ** -0.5
SCALE_XA = 256 ** -0.5

C_GLU_A, C_GLU_B, C_QDA, C_KDA, C_VDA, C_QXA, C_GATE = 0, 1024, 2048, 3072, 4096, 5120, 6144

CF_ID, CF_ONES, CF_TRI, CF_SWAP, CF_INVF, CF_SGN, CF_CBIAS, CF_EPS, NCF = 0, 128, 256, 384, 512, 513, 514, 515, 520

STRICT_SAME_ENGINE = True
SB_BASE = 16640
SB_END = 229120


class Op:
    __slots__ = ("eng", "fn", "deps", "signal", "dma", "val")

    def __init__(self, eng, fn):
        self.eng = eng
        self.fn = fn
        self.deps = []
        self.signal = False
        self.dma = None
        self.val = None


class Prog:
    ENGS = ("pe", "act", "dve", "pool", "sp")

    def __init__(self, nc):
        self.nc = nc
        self.ops = []
        self.res = {}
        self.dma_cnt = {}
        self.last = {e: None for e in self.ENGS}
        self.lastc = {e: None for e in self.ENGS}
        self.strict = False
        self.bar = {e: [] for e in self.ENGS}
        self.zones = {}
        self.offs = {}
        self.nalloc = 0

    def zone(self, name, lo, hi):
        self.zones[name] = [lo, hi, lo, hi]

    def alloc(self, name, shape, dtype, zone, top=False):
        esz = 4 if dtype in (F32, I32) else 2
        n = 1
        for s in shape[1:]:
            n *= s
        nbytes = (n * esz + 63) // 64 * 64
        z = self.zones[zone]
        if top:
            z[3] -= nbytes
            off = z[3]
        else:
            off = z[2]
            z[2] += nbytes
        assert z[2] <= z[3], (name, zone, z)
        self.offs[name] = off
        return self.alloc_at(name, shape, dtype, off)

    def alloc_at(self, name, shape, dtype, off):
        self.nalloc += 1
        return self.nc.alloc_sbuf_tensor_at(f"{name}_{self.nalloc}", list(shape), dtype, offset=off)

    def mark(self, zone):
        return self.zones[zone][2]

    def reset(self, zone, m=None, top=False):
        z = self.zones[zone]
        z[2] = z[0] if m is None else m
        if top:
            z[3] = z[1]
        self.barrier()

    def barrier(self):
        snap = [o for o in self.lastc.values() if o is not None]
        if self.strict:
            snap += [o for o in self.last.values() if o is not None and o.dma is not None]
        for e in self.ENGS:
            if e != "pe":
                self.bar[e] = list(snap)

    def op(self, eng, fn, reads=(), writes=(), dma=None):
        o = Op(eng, fn)
        pr = [r for r in reads if isinstance(r, tuple) and r[0] == "ps"]
        if pr:
            writes = list(writes) + [r for r in pr if r not in writes]
        if dma is not None:
            c = self.dma_cnt.get(dma, 0) + 16
            self.dma_cnt[dma] = c
            o.dma = (dma, c)
        deps = {}

        def add(d, kind):
            deps.setdefault(id(d), [d, set()])[1].add(kind)

        for r in reads:
            st = self.res.get(r)
            if st is not None and st[0] is not None:
                add(st[0], "raw")
        for w in writes:
            st = self.res.get(w)
            if st is not None:
                if st[0] is not None:
                    add(st[0], "waw")
                for rd in st[1].values():
                    add(rd, "war")
                for rd in st[2]:
                    add(rd, "war")
        for b in self.bar[eng]:
            add(b, "raw")
        self.bar[eng] = []
        for d, kinds in deps.values():
            if d is o:
                continue
            if d.dma is not None:
                if o.dma is not None and o.dma[0] == d.dma[0] and kinds == {"waw"}:
                    continue
                o.deps.append(d)
                continue
            if d.eng == eng and o.dma is None:
                if eng == "pe":
                    continue
                if "raw" not in kinds and not STRICT_SAME_ENGINE:
                    continue
            d.signal = True
            o.deps.append(d)
        for r in reads:
            st = self.res.setdefault(r, [None, {}, []])
            if o.dma is not None:
                st[2].append(o)
            else:
                st[1][eng] = o
        for w in writes:
            self.res[w] = [o, {}, []]
        self.ops.append(o)
        self.last[eng] = o
        if o.dma is None and fn is not None:
            self.lastc[eng] = o
        return o

    def emit(self):
        nc = self.nc
        per = {e: [o for o in self.ops if o.eng == e] for e in self.ENGS}
        for e in self.ENGS:
            c = 0
            for o in per[e]:
                if o.dma is None and o.signal:
                    c += 1
                    o.val = c
        self.stats = {e: (len(per[e]), sum(1 for o in per[e] if o.signal)) for e in self.ENGS}
        import contextlib
        with contextlib.ExitStack() as st:
            esem = {e: st.enter_context(nc.semaphore(f"s_{e}")) for e in self.ENGS}
            dsem = {k: st.enter_context(nc.semaphore(f"d_{i}")) for i, k in enumerate(self.dma_cnt)}
            block = st.enter_context(nc.Block())

            def run(eng_name):
                def body(eng):
                    seen = {}
                    for o in per[eng_name]:
                        for d in o.deps:
                            if d.dma is not None:
                                sem, v = dsem[d.dma[0]], d.dma[1]
                                key = ("d", d.dma[0])
                            else:
                                sem, v = esem[d.eng], d.val
                                key = ("e", d.eng)
                            if seen.get(key, 0) >= v:
                                continue
                            seen[key] = v
                            eng.wait_ge(sem, v)
                        if o.fn is None:
                            continue
                        ins = o.fn(eng)
                        if o.dma is not None:
                            ins.then_inc(dsem[o.dma[0]], 16)
                        elif o.signal:
                            ins.then_inc(esem[eng_name], 1)
                return body

            block.tensor(run("pe"))
            block.scalar(run("act"))
            block.vector(run("dve"))
            block.gpsimd(run("pool"))
            block.sync(run("sp"))


def build(stop="all", dbg=(), plan=None):
    nc = bass.Bass("TRN2", target_bir_lowering=False)
    P = Prog(nc)
    P.strict = bool(dbg)
    o = SB_BASE
    P.zone("P", o, o + 7168); o += 7168
    P.zone("WS", o, o + 3 * 16384); o += 3 * 16384
    P.zone("H", o, o + 32768); o += 32768
    P.zone("X", o, o + 16384); o += 16384
    P.zone("Y", o, SB_END)

    def din(name, shape, dt=F32):
        return nc.dram_tensor(name, list(shape), dt, kind="ExternalInput").ap()

    xo = din("xo", [NTOK, D])
    xc = din("xc", [NCTX, D])
    memx = din("memx", [MEM, D])
    posa = din("posa", [1, NCTX + NTOK], I32)
    cf_d = din("cf", [128, NCF])
    g_mix = din("g_mix", [1, D])
    w_in = din("w_in", [D, DIN])
    w_dw = din("w_dw", [31 * 8, 128])
    vecs = din("vecs", [40, 128])
    w_conv_out = din("w_conv_out", [1024, D])
    lams = din("lams", [4, 128])
    g_subln = din("g_subln", [1, 256])
    w_da_out = din("w_da_out", [1024, D])
    g_mem = din("g_mem", [1, D])
    w_mem_kv = din("w_mem_kv", [D, D])
    w_xa_out = din("w_xa_out", [1024, D])
    w_mix_out = din("w_mix_out", [D, D])
    w_up = din("w_up", [D, DFF])
    w_down = din("w_down", [DFF, D])
    g_final = din("g_final", [1, D])
    y = nc.dram_tensor("y", [NTOK, D], F32, kind="ExternalOutput").ap()
    dbg_out = {}
    for name, shape in dbg:
        dbg_out[name] = nc.dram_tensor("dbg_" + name, list(shape), F32, kind="ExternalOutput").ap()

    PS = nc.alloc_psum_tensor("ps", [128, 4096], F32)

    def bank(b, n=512, off=0):
        return PS[:, b * 512 + off: b * 512 + off + n]

    def span(b, n):
        return PS[:, b * 512: b * 512 + n]

    def bank_bf(b, n=1024):
        return PS[:, b * 512:(b + 1) * 512].bitcast(BF16)[:, 0:n]

    def PB(b):
        return ("ps", b)

    rot = {}
    pend = []

    def defer(delay, fn):
        pend.append([delay, fn])

    def tick():
        due = []
        for it in pend:
            it[0] -= 1
        while pend and pend[0][0] <= 0:
            due.append(pend.pop(0)[1])
        for fn in due:
            fn()

    def flush():
        while pend:
            pend.pop(0)[1]()

    def nxt(name, choices):
        i = rot.get(name, 0)
        rot[name] = i + 1
        return choices[i % len(choices)]

    CF = P.alloc("CF", [128, NCF], F32, "P")
    CB = P.alloc("CB", [128, 512], BF16, "P")
    ID_F = CF[:, CF_ID:CF_ID + 128]
    EPSB = CF[:, CF_EPS:CF_EPS + 1]
    ID_B = CB[:, 0:128]
    ONES_B = CB[:, 128:256]
    TRI_B = CB[:, 256:384]
    SWAP_B = CB[:, 384:512]
    SV = P.alloc("SV", [128, 128], F32, "P")
    VECT = P.alloc("VECT", [128, 40], F32, "P")
    WDWT = P.alloc("WDWT", [128, 248], F32, "P")
    GSUB = P.alloc("GSUB", [128, 256], F32, "P")
    WS = [P.alloc(f"WS{i}", [128, 16, 512], BF16, "WS") for i in range(3)]

    def wd_view(i):
        return WS[i][:, :, :].rearrange("p k c -> p (k c)").rearrange("p (f c) -> p f c", f=4)

    WT = {"w_in": w_in, "w_mem_kv": w_mem_kv, "w_conv_out": w_conv_out, "w_da_out": w_da_out,
          "w_xa_out": w_xa_out, "w_mix_out": w_mix_out, "w_up": w_up}
    plan_out = []
    wst = {"n": 0, "emitted": 0, "rel": 0, "cap": 1}

    def slab_src(desc):
        if desc[0] == "cols":
            _, wname, c0, nk = desc
            return w_cols(WT[wname], c0), nk
        _, g = desc
        return w_down[g * 512:(g + 1) * 512, :].rearrange("(f p) c -> p f c", p=128), None

    def emit_load(m, desc):
        src, nk = slab_src(desc)
        i = m % 3
        dst = wd_view(i) if nk is None else WS[i][:, 0:nk, :]
        P.op("pool", lambda e, dst=dst, src=src: e.dma_start(out=dst, in_=src), [], [("WS", i)], dma=("ws", i))

    def prefetch():
        if plan is None:
            return
        while wst["emitted"] < min(wst["rel"] + 3, len(plan), wst["cap"]):
            emit_load(wst["emitted"], plan[wst["emitted"]])
            wst["emitted"] += 1

    def wslab(*desc):
        n = wst["n"]
        wst["n"] += 1
        plan_out.append(desc)
        assert n == wst["rel"], "slabs are consumed one at a time"
        if plan is None:
            emit_load(n, desc)
        else:
            assert tuple(plan[n]) == tuple(desc), (n, plan[n], desc)
            prefetch()
            assert wst["emitted"] > n
        return n % 3

    def wrel():
        wst["rel"] += 1
        prefetch()

    def w_cols(w, c0, n=512):
        return w[:, c0:c0 + n].rearrange("(k p) c -> p k c", p=128)

    SV_NEGLAM, SV_QMAX, SV_KMAX, SV_NEGM, SV_NEGMC = 0, 1, 9, 17, 25
    SV_QXMAX, SV_KMMAX, SV_NEGMX = 33, 37, 41
    SV_STK, SV_STQ, SV_STX, SV_T = 48, 80, 96, 112

    def sv(c, n=1):
        return SV[:, c:c + n]

    def dma(q, out, in_, reads, writes, sem):
        return P.op(q, lambda e, out=out, in_=in_: e.dma_start(out=out, in_=in_), reads, writes, dma=sem)

    def dump(name, ap_sb, keys, rows=None):
        if name in dbg_out:
            dma("pool", dbg_out[name] if rows is None else dbg_out[name][rows], ap_sb, list(keys), ["dbg_" + name], "dbg")

    def finish():
        P.op("sp", None, ["dbg_" + n for n in dbg_out] + [("y", t) for t in range(8)], [])
        P.emit()
        nc.plan_out = plan_out
        return nc

    KT = P.alloc("KT", [128, 8, 2048], BF16, "Y")
    VA = P.alloc("VA", [128, 16, 4, 257], BF16, "Y")
    QT = P.alloc("QT", [128, 8, 1024], BF16, "Y")
    HTH = P.alloc("HTH", [128, 16, 32], BF16, "P")
    mY = P.mark("Y")
    P.op("dve", lambda e: e.memset(VA[:, :, :, 256:257], 1.0), [], ["VA1"])

    XTk = [P.alloc_at(f"XTk{i}", [128, D], F32, P.offs["KT"] + i * 8192)[:, :] for i in range(4)]
    va_own = P.offs["VA"] + 8 * 4 * 257 * 2
    va_own = (va_own + 63) // 64 * 64
    XTv = [P.alloc_at(f"XTv{i}", [128, D], F32, va_own + i * 8192)[:, :] for i in range(2)]
    assert va_own + 2 * 8192 <= P.offs["VA"] + 16 * 4 * 257 * 2
    dma("sp", CF[:, :], cf_d[:, :], [], ["CF"], "cf")
    for t in range(4):
        dma("sp", XTk[t][:, :], xc[t * 128:(t + 1) * 128, :], [], [("XT", t)], ("xt", t))
    P.op("dve", lambda e: e.tensor_copy(out=CB[:, :], in_=CF[:, 0:512]), ["CF"], ["CB"])
    prefetch()

    VROW = P.alloc("VROW", [40, 128], F32, "X")
    WROW0 = P.alloc("WROW0", [128, 128], F32, "X")
    WROW1 = P.alloc("WROW1", [120, 128], F32, "X")
    LAMB = P.alloc("LAMB", [128, 4, 128], F32, "X")
    LTMP = P.alloc("LTMP", [128, 2, 128], F32, "X")
    dma("sp", VROW[:, :], vecs[:, :], [], ["VROW"], ("misc", 1))
    dma("sp", WROW0[:, :], w_dw[0:128, :], [], ["WROW0"], ("misc", 2))
    dma("sp", WROW1[:, :], w_dw[128:248, :], [], ["WROW1"], ("misc", 3))
    dma("sp", LAMB[:, :, :], lams.partition_broadcast(128), [], ["LAMB"], ("misc", 4))
    dma("sp", GSUB[:, :], g_subln.partition_broadcast(128), [], ["GSUB"], ("misc", 5))
    P.op("pe", lambda e: e.matmul(bank(0, 40), lhsT=VROW[:, :], rhs=CF[0:40, CF_ID:CF_ID + 40], start=True, stop=True),
         ["VROW", "CF"], [PB(0)])
    P.op("pe", lambda e: e.matmul(bank(1, 128), lhsT=WROW0[:, :], rhs=ID_F, start=True, stop=True),
         ["WROW0", "CF"], [PB(1)])
    P.op("pe", lambda e: e.matmul(bank(2, 120), lhsT=WROW1[:, :], rhs=CF[0:120, CF_ID:CF_ID + 120], start=True, stop=True),
         ["WROW1", "CF"], [PB(2)])
    P.op("dve", lambda e: e.tensor_copy(out=VECT[:, :], in_=bank(0, 40)), [PB(0)], ["VECT"])
    P.op("dve", lambda e: e.tensor_copy(out=WDWT[:, 0:128], in_=bank(1, 128)), [PB(1)], ["WDWT"])
    P.op("dve", lambda e: e.tensor_copy(out=WDWT[:, 128:248], in_=bank(2, 120)), [PB(2)], ["WDWT"])
    P.op("dve", lambda e: e.tensor_scalar(out=GSUB[:, :], in0=GSUB[:, :], scalar1=float(1.0 - LAMBDA_INIT), scalar2=None,
                                          op0=ALU.mult), ["GSUB"], ["GSUB"])
    P.op("dve", lambda e: e.tensor_tensor(out=LTMP[:, 0, :], in0=LAMB[:, 0, :], in1=LAMB[:, 1, :], op=ALU.mult), ["LAMB"], ["LTMP0"])
    P.op("dve", lambda e: e.tensor_tensor(out=LTMP[:, 1, :], in0=LAMB[:, 2, :], in1=LAMB[:, 3, :], op=ALU.mult), ["LAMB"], ["LTMP1"])
    P.op("dve", lambda e: e.tensor_reduce(out=sv(SV_T, 2), in_=LTMP[:, :, :], axis=AX.X, op=ALU.add), ["LTMP0", "LTMP1"], ["SVT0"])
    P.op("act", lambda e: e.activation(out=sv(SV_T + 2, 2), in_=sv(SV_T, 2), func=AF.Exp), ["SVT0"], ["SVT2"])
    P.op("dve", lambda e: e.tensor_tensor(out=sv(SV_T + 4), in0=sv(SV_T + 3), in1=sv(SV_T + 2), op=ALU.subtract), ["SVT2"], ["SVT4"])
    P.op("dve", lambda e: e.tensor_scalar(out=sv(SV_NEGLAM), in0=sv(SV_T + 4), scalar1=float(-LAMBDA_INIT), scalar2=None,
                                          op0=ALU.add), ["SVT4"], ["NEGLAM"])

    HT = P.alloc("HT", [128, 16, 1024], BF16, "H")
    def norm_transpose(src_rows, ntiles, gvec, HTd, htkey, xt_alias=None, preloaded=0):
        GBC = P.alloc("GBC", [128, D], F32, "Y")
        HB = [P.alloc(f"HB{i}", [128, D], BF16, "Y") for i in range(2)]
        SS = P.alloc("SS", [128, 8], F32, "Y")
        if xt_alias is not None:
            XT = xt_alias
        else:
            XT = [P.alloc(f"XT{i}", [128, D], F32, "Y") for i in range(2)]
        dma("sp", GBC[:, :], gvec.partition_broadcast(128), [], ["GBC"], ("misc", 7))

        nx = len(XT)

        def s1(t):
            b = t % 2
            xb = t % nx
            if t >= preloaded:
                dma("sp", XT[xb][:, :], src_rows[t * 128:(t + 1) * 128, :], [], [("XT", xb)], ("xt", xb))
            P.op("act", lambda e: e.activation(out=HB[b][:, :], in_=XT[xb][:, :], func=AF.Square,
                                               accum_out=SS[:, b:b + 1]), [("XT", xb)], [("HB", b), ("SS", b)])
            P.op("act", lambda e: e.activation(out=SS[:, 4 + b:5 + b], in_=SS[:, b:b + 1], func=AF.Sqrt,
                                               scale=1.0 / D, bias=EPSB), [("SS", b), "CF"], [("SQ", b)])
            P.op("dve", lambda e: e.reciprocal(out=SS[:, 2 + b:3 + b], in_=SS[:, 4 + b:5 + b]), [("SQ", b)], [("RS", b)])
            P.op("dve", lambda e: e.scalar_tensor_tensor(out=HB[b][:, :], in0=XT[xb][:, :], scalar=SS[:, 2 + b:3 + b],
                                                         in1=GBC[:, :], op0=ALU.mult, op1=ALU.mult),
                 [("XT", xb), ("RS", b), "GBC"], [("HB", b)])

        def s2(t):
            b = t % 2
            for hh in range(2):
                pb = nxt("ntp", [0, 1, 2, 3])
                for j in range(8):
                    k = hh * 8 + j
                    P.op("pe", lambda e, k=k, j=j, pb=pb: e.transpose(
                        out=bank_bf(pb)[:, j * 128:(j + 1) * 128], in_=HB[b][:, k * 128:(k + 1) * 128], identity=ID_B),
                        [("HB", b), "CB"], [PB(pb)])
                if hh == 0:
                    P.op("act", lambda e, hh=hh, pb=pb: e.activation(
                        out=HTd[:, hh * 8:(hh + 1) * 8, t * 128:(t + 1) * 128],
                        in_=bank_bf(pb).rearrange("p (j t) -> p j t", j=8), func=AF.Copy),
                        [PB(pb)], [(htkey, t)])
                else:
                    P.op("dve", lambda e, hh=hh, pb=pb: e.tensor_copy(
                        out=HTd[:, hh * 8:(hh + 1) * 8, t * 128:(t + 1) * 128],
                        in_=bank_bf(pb).rearrange("p (j t) -> p j t", j=8)),
                        [PB(pb), (htkey, t)], [(htkey, t)])

        for t in range(ntiles):
            s1(t)
            if t >= 1:
                s2(t - 1)
            if t == 5 and wst["cap"] < 10 ** 6:
                wst["cap"] = 10 ** 6
                prefetch()
        s2(ntiles - 1)

    def rope_tmps():
        TB = [P.alloc(f"TB{i}", [128, 512], BF16, "Y") for i in range(3)]
        T1 = [P.alloc(f"T1{i}", [128, 512], F32, "Y") for i in range(3)]
        T2 = [P.alloc(f"T2{i}", [128, 512], F32, "Y") for i in range(3)]
        SQ = [P.alloc(f"SQ{i}", [128, 512], BF16, "Y") for i in range(3)]
        return TB, T1, T2, SQ

    def rope_evac(tm, pb, dest, destkey, tc, statcol):
        TB, T1, T2, SQ = tm
        r = nxt("rope", [0, 1, 2])

        def stage1():
            P.op("act", lambda e: e.activation(out=TB[r][:, :], in_=bank(pb), func=AF.Copy), [PB(pb)], [("TB", r)])
            pb2 = nxt("swb", [4, 5])
            P.op("pe", lambda e: e.matmul(bank(pb2), lhsT=SWAP_B, rhs=TB[r][:, :], start=True, stop=True),
                 [("TB", r), "CB"], [PB(pb2)])
            P.op("dve", lambda e: e.tensor_tensor(out=T1[r][:, :], in0=bank(pb), in1=COS[:, tc:tc + 512], op=ALU.mult),
                 [PB(pb), "COS%d" % (tc // NCTX)], [("T1", r)])
            P.op("dve", lambda e: e.tensor_tensor(out=T2[r][:, :], in0=bank(pb2), in1=SINS[:, tc:tc + 512], op=ALU.mult),
                 [PB(pb2), "SINS%d" % (tc // NCTX)], [("T2", r)])
            P.op("pool", lambda e: e.tensor_tensor(out=dest, in0=T1[r][:, :], in1=T2[r][:, :], op=ALU.add),
                 [("T1", r), ("T2", r)], [destkey])
            P.op("act", lambda e: e.activation(out=SQ[r][:, :], in_=dest, func=AF.Square), [destkey], [("SQ", r)])
            defer(1, stage2)

        def stage2():
            pb3 = nxt("stb", [6, 7])
            P.op("pe", lambda e: e.matmul(bank(pb3), lhsT=ONES_B, rhs=SQ[r][:, :], start=True, stop=True),
                 [("SQ", r), "CB"], [PB(pb3)])
            P.op("dve", lambda e: e.tensor_reduce(out=sv(statcol), in_=bank(pb3), axis=AX.X, op=ALU.max),
                 [PB(pb3)], [("SVc", statcol)])

        defer(1, stage1)

    rgen = [None]

    def kv_proj(ctx, tm):
        tbs = (0, 1) if ctx else (2, 3)
        def k_part():
            for s in range(2):
                i = wslab("cols", "w_in", C_KDA + s * 512, 16)
                for tb in tbs:
                  for jl in range(4):
                    if True:
                        j = 4 * s + jl
                        toff = (tb % 2) * 512
                        pb = nxt("kb", [0, 1, 2, 3])
                        for k in range(16):
                            P.op("pe", lambda e, i=i, k=k, jl=jl, toff=toff, pb=pb: e.matmul(
                                bank(pb), lhsT=WS[i][:, k, jl * 128:(jl + 1) * 128], rhs=HT[:, k, toff:toff + 512],
                                start=(k == 0), stop=(k == 15)),
                                [("WS", i)] + [("HT", toff // 128 + t) for t in range(4)], [PB(pb)])
                        tick()
                        rope_evac(tm, pb, KT[:, j, tb * 512:(tb + 1) * 512], ("KT", j, tb), tb * 512, SV_STK + j * 4 + tb)
                wrel()

        def v_part():
            for n in range(2):
                i = wslab("cols", "w_in", C_VDA + n * 512, 16)
                for tl in range(8):
                    tt = tl + (0 if ctx else 8)
                    pb = nxt("kb", [0, 1, 2, 3])
                    for k in range(16):
                        P.op("pe", lambda e, i=i, k=k, tl=tl, pb=pb: e.matmul(
                            bank(pb), lhsT=HT[:, k, tl * 128:(tl + 1) * 128], rhs=WS[i][:, k, :],
                            start=(k == 0), stop=(k == 15)), [("WS", i), ("HT", tl)], [PB(pb)])
                    tick()
                    P.op("act", lambda e, tt=tt, n=n, pb=pb: e.activation(
                        out=VA[:, tt, 2 * n:2 * n + 2, 0:256], in_=bank(pb).rearrange("p (h e) -> p h e", h=2), func=AF.Copy),
                        [PB(pb)], [("VA", tt, n)])
                    if rgen[0] is not None and tl % 3 == 2:
                        next(rgen[0], None)
                wrel()

        k_part()
        rgen[0] = rope_tables_gen(NCTX, NCTX + NTOK, 256, final_reset=False) if ctx else None
        v_part()
        if rgen[0] is not None:
            for _ in rgen[0]:
                pass
            rgen[0] = None

    P_tabs = {}

    def rope_tables(c_lo, c_hi, NH, final_reset=True):
        for _ in rope_tables_gen(c_lo, c_hi, NH, final_reset):
            pass
        return P_tabs["COS"], P_tabs["SINS"]

    def rope_tables_gen(c_lo, c_hi, NH, final_reset=True):
        if "COS" not in P_tabs:
            P.reset("X")
            P_tabs["COS"] = P.alloc("COS", [128, NCTX + NTOK], F32, "X")
            P_tabs["SINS"] = P.alloc("SINS", [128, NCTX + NTOK], F32, "X")
        COS, SINS = P_tabs["COS"], P_tabs["SINS"]
        m_in = P.mark("Y")
        POSI = P.alloc("POSI", [128, NH], I32, "Y")
        XA_ = P.alloc("XA", [128, NH], F32, "Y")
        TT_ = P.alloc("TT", [128, NH], F32, "Y")
        KI_ = P.alloc("KI", [128, NH], I32, "Y")
        C1 = float(np.float32(2 * PI))
        C2 = float(2 * PI - np.float64(np.float32(2 * PI)))
        for c0 in range(c_lo, c_hi, NH):
            cs = slice(c0, c0 + NH)
            dma("sp", POSI[:, :], posa[:, cs].partition_broadcast(128), [], ["POSI"], ("misc", 6))
            P.op("dve", lambda e: e.tensor_copy(out=TT_[:, :], in_=POSI[:, :]), ["POSI"], ["TT"])
            P.op("dve", lambda e: e.tensor_scalar(out=XA_[:, :], in0=TT_[:, :], scalar1=CF[:, CF_INVF:CF_INVF + 1], scalar2=None,
                                                  op0=ALU.mult), ["TT", "CF"], ["XA"])
            P.op("dve", lambda e: e.tensor_scalar(out=KI_[:, :], in0=XA_[:, :], scalar1=float(1.0 / (2 * PI)), scalar2=None,
                                                  op0=ALU.mult), ["XA"], ["KI"])
            P.op("dve", lambda e: e.tensor_copy(out=TT_[:, :], in_=KI_[:, :]), ["KI"], ["TT"])
            P.op("dve", lambda e: e.scalar_tensor_tensor(out=XA_[:, :], in0=TT_[:, :], scalar=-C1, in1=XA_[:, :],
                                                         op0=ALU.mult, op1=ALU.add), ["TT", "XA"], ["XA"])
            P.op("dve", lambda e: e.scalar_tensor_tensor(out=XA_[:, :], in0=TT_[:, :], scalar=-C2, in1=XA_[:, :],
                                                         op0=ALU.mult, op1=ALU.add), ["TT", "XA"], ["XA"])
            P.op("dve", lambda e: e.tensor_scalar(out=TT_[:, :], in0=XA_[:, :], scalar1=PI, scalar2=2 * PI,
                                                  op0=ALU.is_gt, op1=ALU.mult), ["XA"], ["TT"])
            P.op("dve", lambda e: e.tensor_tensor(out=XA_[:, :], in0=XA_[:, :], in1=TT_[:, :], op=ALU.subtract), ["XA", "TT"], ["XA"])
            P.op("dve", lambda e: e.tensor_scalar(out=TT_[:, :], in0=XA_[:, :], scalar1=-PI, scalar2=2 * PI,
                                                  op0=ALU.is_lt, op1=ALU.mult), ["XA"], ["TT"])
            P.op("dve", lambda e: e.tensor_tensor(out=XA_[:, :], in0=XA_[:, :], in1=TT_[:, :], op=ALU.add), ["XA", "TT"], ["XA"])
            P.op("act", lambda e, cs=cs: e.activation(out=SINS[:, cs], in_=XA_[:, :], func=AF.Sin, scale=CF[:, CF_SGN:CF_SGN + 1]),
                 ["XA", "CF"], ["SINS%d" % (c0 // NCTX)])
            P.op("dve", lambda e: e.tensor_scalar(out=TT_[:, :], in0=XA_[:, :], scalar1=PI / 2, scalar2=2 * PI,
                                                  op0=ALU.is_gt, op1=ALU.mult), ["XA", "SINS%d" % (c0 // NCTX)], ["TT"])
            P.op("dve", lambda e: e.scalar_tensor_tensor(out=XA_[:, :], in0=XA_[:, :], scalar=PI / 2, in1=TT_[:, :],
                                                         op0=ALU.add, op1=ALU.subtract), ["XA", "TT"], ["XA"])
            P.op("act", lambda e, cs=cs: e.activation(out=COS[:, cs], in_=XA_[:, :], func=AF.Sin), ["XA"], ["COS%d" % (c0 // NCTX)])
            yield
        if final_reset:
            P.reset("Y", m_in)
        else:
            P.zones["Y"][2] = m_in

    XTq = [QT[:, 0:4, :].rearrange("p a b -> p (a b)").bitcast(F32), QT[:, 4:8, :].rearrange("p a b -> p (a b)").bitcast(F32)]
    norm_transpose(xc, NCTX // 128, g_mix, HT, "HT", xt_alias=XTk, preloaded=4)
    for t in range(2):
        dma("sp", XTq[t][:, :], xo[t * 128:(t + 1) * 128, :], [], [("XT", t)], ("xt", t))
    P.reset("Y", mY)
    COS, SINS = rope_tables(0, NCTX, 1024)
    if stop == "A0":
        return finish()
    P.op("dve", lambda e: e.tensor_copy(out=HTH[:, :, :], in_=HT[:, :, 992:1024]), [("HT", 7)], ["HTH"])
    if "HTC" in dbg_out:
        for k in range(16):
            dump("HTC", HT[:, k, :], [("HT", t) for t in range(8)], rows=slice(k * 128, (k + 1) * 128))
    tm = rope_tmps()
    kv_proj(True, tm)
    flush()
    P.reset("Y", mY)
    norm_transpose(xo, NTOK // 128, g_mix, HT, "HT", xt_alias=XTq + XTv, preloaded=2)
    P.op("dve", lambda e: e.memset(VA[:, 8:16, :, 256:257], 1.0), [], [("XT", 2), ("XT", 3), "VA1"])
    P.reset("Y", mY)
    if "HTO" in dbg_out:
        for k in range(16):
            dump("HTO", HT[:, k, :], [("HT", t) for t in range(8)], rows=slice(k * 128, (k + 1) * 128))
    if stop == "A":
        return finish()
    tm = rope_tmps()
    kv_proj(False, tm)
    for s in range(2):
        i = wslab("cols", "w_in", C_QDA + s * 512, 16)
        for jl in range(4):
            j = 4 * s + jl
            for tbl in range(2):
                toff = tbl * 512
                pb = nxt("kb", [0, 1, 2, 3])
                for k in range(16):
                    P.op("pe", lambda e, i=i, k=k, jl=jl, toff=toff, pb=pb: e.matmul(
                        bank(pb), lhsT=WS[i][:, k, jl * 128:(jl + 1) * 128], rhs=HT[:, k, toff:toff + 512],
                        start=(k == 0), stop=(k == 15)),
                        [("WS", i)] + [("HT", toff // 128 + t) for t in range(4)], [PB(pb)])
                tick()
                rope_evac(tm, pb, QT[:, j, toff:toff + 512], ("QT", j, tbl), 1024 + toff, SV_STQ + j * 2 + tbl)
        wrel()
    flush()
    P.op("dve", lambda e: e.tensor_reduce(out=sv(SV_KMAX, 8), in_=sv(SV_STK, 32).rearrange("p (j t) -> p j t", t=4),
                                          axis=AX.X, op=ALU.max), [("SVc", SV_STK + c) for c in range(32)], ["KMAX"])
    P.op("dve", lambda e: e.tensor_reduce(out=sv(SV_QMAX, 8), in_=sv(SV_STQ, 16).rearrange("p (j t) -> p j t", t=2),
                                          axis=AX.X, op=ALU.max), [("SVc", SV_STQ + c) for c in range(16)], ["QMAX"])
    P.op("dve", lambda e: e.tensor_tensor(out=sv(SV_T, 8), in0=sv(SV_QMAX, 8), in1=sv(SV_KMAX, 8), op=ALU.mult),
         ["KMAX", "QMAX"], ["SVT0"])
    P.op("act", lambda e: e.activation(out=sv(SV_T + 8, 8), in_=sv(SV_T, 8), func=AF.Sqrt), ["SVT0"], ["SVT8"])
    P.op("dve", lambda e: e.tensor_scalar(out=sv(SV_NEGM, 8), in0=sv(SV_T + 8, 8), scalar1=float(-SCALE_DA), scalar2=None,
                                          op0=ALU.mult), ["SVT8"], ["NEGM"])
    P.op("dve", lambda e: e.tensor_scalar(out=sv(SV_NEGMC, 8), in0=sv(SV_NEGM, 8), scalar1=CF[:, CF_CBIAS:CF_CBIAS + 1],
                                          scalar2=None, op0=ALU.add), ["NEGM", "CF"], ["NEGMC"])
    P.reset("Y", mY)
    P.reset("X")
    if "KT" in dbg_out:
        for j in range(8):
            dump("KT", KT[:, j, :], [("KT", j, tb) for tb in range(4)], rows=slice(j * 128, (j + 1) * 128))
            dump("QT", QT[:, j, :], [("QT", j, tb) for tb in range(2)], rows=slice(j * 128, (j + 1) * 128))
        for tt in range(16):
            dump("VA", VA[:, tt, :, :].rearrange("p h e -> p (h e)"), [("VA", tt, 0), ("VA", tt, 1), "VA1"],
                 rows=slice(tt * 128, (tt + 1) * 128))
        dump("SVB", SV[:, :], ["NEGM", "NEGMC", "NEGLAM"])
    if stop == "B":
        return finish()

    OT_DA = P.alloc("OT_DA", [128, 8, 1024], BF16, "X")
    PTD = [P.alloc(f"PT{i}", [128, 512], BF16, "Y") for i in range(4)]
    O1 = P.alloc("O1", [128, 4, 257], F32, "Y")
    O2 = P.alloc("O2", [128, 4, 257], F32, "Y")
    OS = P.alloc("OS", [128, 4, 256], F32, "Y")
    ON = P.alloc("ON", [128, 4, 256], BF16, "Y")
    JK = P.alloc("JK", [128, 256], F32, "Y")
    RR = P.alloc("RR", [128, 32], F32, "Y")

    aotb_banks = [7]

    def attn_out_group(bufs, OTd, otkey, ch0, qcol0, subln, delay=4):
        O1g, O2g, OSg, ONg, RRg, JKg = bufs
        for qi in range(4):
            P.op("dve", lambda e, qi=qi: e.tensor_copy(out=O2g[:, qi, :], in_=bank(qi, 257)), [PB(qi)], [("O2", qi)])
        o2k = [("O2", qi) for qi in range(4)]
        def stage_d():
            for qi in range(4):
                P.op("dve", lambda e, qi=qi: e.scalar_tensor_tensor(out=ONg[:, qi, :], in0=OSg[:, qi, :], scalar=RRg[:, 20 + qi:21 + qi],
                                                                    in1=GSUB[:, :], op0=ALU.mult, op1=ALU.mult),
                     [("OS", qi), ("RR", 5), "GSUB"], [("ON", qi)])
            defer(2, part_b)

        def stage_c():
            for qi in range(4):
                P.op("dve", lambda e, qi=qi: e.scalar_tensor_tensor(out=JKg[:, :], in0=OSg[:, qi, :], scalar=1.0, in1=OSg[:, qi, :],
                                                                    op0=ALU.mult, op1=ALU.mult, accum_out=RRg[:, 12 + qi:13 + qi]),
                     [("OS", qi)], ["JK", ("RR", 3, qi)])
            defer(3, stage_c2)

        def stage_c2():
            P.op("act", lambda e: e.activation(out=RRg[:, 16:20], in_=RRg[:, 12:16], func=AF.Ln, scale=1.0 / 256, bias=EPSB),
                 [("RR", 3, qi) for qi in range(4)] + ["CF"], [("RR", 4)])
            P.op("act", lambda e: e.activation(out=RRg[:, 20:24], in_=RRg[:, 16:20], func=AF.Exp, scale=-0.5), [("RR", 4)], [("RR", 5)])
            defer(1, stage_d)

        def stage_b():
            o1k = [("O1", qi) for qi in range(4)]
            P.op("dve", lambda e: e.reciprocal(out=RRg[:, 0:4], in_=O1g[:, :, 256]), o1k, [("RR", 0)])
            P.op("dve", lambda e: e.reciprocal(out=RRg[:, 4:8], in_=O2g[:, :, 256]), o2k, [("RR", 1)])
            P.op("dve", lambda e: e.tensor_scalar(out=RRg[:, 8:12], in0=RRg[:, 4:8], scalar1=sv(SV_NEGLAM), scalar2=None, op0=ALU.mult),
                 [("RR", 1), "NEGLAM"], [("RR", 2)])
            for qi in range(4):
                P.op("dve", lambda e, qi=qi: e.tensor_scalar(out=OSg[:, qi, :], in0=O1g[:, qi, 0:256], scalar1=RRg[:, qi:qi + 1], scalar2=None,
                                                             op0=ALU.mult), [("O1", qi), ("RR", 0)], [("OS", qi)])
            for qi in range(4):
                P.op("dve", lambda e, qi=qi: e.scalar_tensor_tensor(out=OSg[:, qi, :], in0=O2g[:, qi, 0:256], scalar=RRg[:, 8 + qi:9 + qi],
                                                                    in1=OSg[:, qi, :], op0=ALU.mult, op1=ALU.add),
                     [("O2", qi), ("RR", 2), ("OS", qi)], [("OS", qi)])
            defer(2, stage_c)

        if subln:
            defer(1, stage_b)
        else:
            P.op("dve", lambda e: e.reciprocal(out=RRg[:, 0:4], in_=O2g[:, :, 256]), o2k, [("RR", 0)])
            for qi in range(4):
                P.op("dve", lambda e, qi=qi: e.tensor_scalar(out=ONg[:, qi, :], in0=O2g[:, qi, 0:256], scalar1=RRg[:, qi:qi + 1], scalar2=None,
                                                             op0=ALU.mult), [("O2", qi), ("RR", 0)], [("ON", qi)])

        def part_b():
            for qi in range(4):
                tb_ = nxt("aotb", aotb_banks)
                for ec in range(2):
                    P.op("pe", lambda e, ec=ec, qi=qi, tb_=tb_: e.transpose(out=bank_bf(tb_)[:, ec * 128:(ec + 1) * 128],
                                                                         in_=ONg[:, qi, ec * 128:(ec + 1) * 128], identity=ID_B),
                         [("ON", qi), "CB"], [PB(tb_)])
                qcol = qcol0 + qi * 128
                P.op("dve", lambda e, qcol=qcol, tb_=tb_: e.tensor_copy(out=OTd[:, ch0:ch0 + 2, qcol:qcol + 128],
                                                                      in_=bank_bf(tb_, 256).rearrange("p (c q) -> p c q", c=2)),
                     [PB(tb_)], [(otkey, ch0, qcol)])
        if not subln:
            defer(delay, part_b)

    def span_ap(key):
        return bank(key[1], 256)

    steps = []
    for hd in range(4):
        for qb in range(2):
            for mp in range(2):
                nkv = 8 + 4 * qb + 4
                for kt in range(nkv):
                    steps.append((hd, qb, mp, kt, kt == nkv - 1))
    stinfo = {}

    def emit_st(si):
        hd, qb, mp, kt, _ = steps[si]
        j = hd * 2 + mp
        if kt < 8:
            q_lo = 4 * qb
            bias = sv(SV_NEGMC + j)
            bkey = "NEGMC"
        else:
            q_lo = max(kt - 8, 4 * qb)
            bias = sv(SV_NEGM + j)
            bkey = "NEGM"
        q0 = q_lo * 128
        nq = (4 * qb + 4 - q_lo) * 128
        spb = nxt("spbd", [4, 5, 6])
        r = nxt("pt", [0, 1, 2, 3])
        P.op("pe", lambda e: e.matmul(bank(spb, nq), lhsT=KT[:, j, kt * 128:(kt + 1) * 128], rhs=QT[:, j, q0:q0 + nq],
                                      start=True, stop=True), [("KT", j, kt // 4), ("QT", j, qb)], [PB(spb)])
        P.op("act", lambda e: e.activation(out=PTD[r][:, 0:nq], in_=bank(spb, nq), func=AF.Exp, scale=float(SCALE_DA), bias=bias),
             [PB(spb), bkey], [("PT", r)])
        if kt >= 8 and (kt - 8) >= 4 * qb:
            P.op("dve", lambda e: e.tensor_tensor(out=PTD[r][:, 0:128], in0=PTD[r][:, 0:128], in1=TRI_B, op=ALU.mult),
                 [("PT", r), "CB"], [("PT", r)])
        stinfo[si] = (r, q_lo, nq)

    def emit_pv(si):
        hd, qb, mp, kt, last = steps[si]
        r, q_lo, nq = stinfo.pop(si)
        for t in range(nq // 128):
            qi = q_lo - 4 * qb + t
            P.op("pe", lambda e, t=t, qi=qi: e.matmul(
                bank(qi, 257), lhsT=PTD[r][:, t * 128:(t + 1) * 128], rhs=VA[:, kt, hd, :],
                start=(kt == 0), stop=(kt == 8 + 4 * qb + qi)),
                [("PT", r), ("VA", kt, hd // 2), "VA1"], [PB(qi)])
        if last:
            if mp == 0:
                for qi in range(4):
                    P.op("dve", lambda e, qi=qi: e.tensor_copy(out=O1[:, qi, :], in_=bank(qi, 257)),
                         [PB(qi)], [("O1", qi)])
            else:
                attn_out_group((O1, O2, OS, ON, RR, JK), OT_DA, "OTDA", hd * 2, 4 * qb * 128, True)

    emit_st(0)
    emit_st(1)
    for si in range(len(steps)):
        if si + 2 < len(steps):
            emit_st(si + 2)
        emit_pv(si)
        tick()
    flush()
    if "OTDA" in dbg_out:
        for c in range(8):
            dump("OTDA", OT_DA[:, c, :], [("OTDA", (c // 2) * 2, q * 128) for q in range(8)], rows=slice(c * 128, (c + 1) * 128))
    P.reset("Y")
    if stop == "C":
        return finish()

    CACT = P.alloc("CACT", [128, 8, 1024], BF16, "Y")
    QXT = P.alloc("QXT", [128, 8, 1024], BF16, "Y", top=True)
    mY2 = P.mark("Y")
    ACC = [P.alloc(f"ACC{c}", [128, 1024], F32, "Y") for c in range(8)]
    mY3 = P.mark("Y")
    U = [P.alloc(f"U{c}", [128, 1056], BF16, "Y") for c in range(8)]
    SGB = [P.alloc(f"SGB{i}", [128, 1056], BF16, "Y") for i in range(4)]
    DG = [P.alloc(f"DG{i}", [128, 31, 128], BF16, "Y") for i in range(2)]

    NPE = 31

    def dg_build(c):
        d = c % 2
        for jt in range(NPE):
            eng = "dve" if jt % 2 == 0 else "act"
            if eng == "dve":
                P.op("dve", lambda e, jt=jt: e.tensor_scalar(out=DG[d][:, jt, :], in0=ID_B, scalar1=WDWT[:, jt * 8 + c:jt * 8 + c + 1],
                                                             scalar2=None, op0=ALU.mult), ["CB", "WDWT"], [("DG", d, 0)])
            else:
                P.op("act", lambda e, jt=jt: e.activation(out=DG[d][:, jt, :], in_=ID_B, func=AF.Copy,
                                                          scale=WDWT[:, jt * 8 + c:jt * 8 + c + 1]), ["CB", "WDWT"], [("DG", d, 1)])

    def conv_chunk(c):
        d = c % 2
        pp = nxt("cvb", [4, 6])
        for half in range(2):
            for jt in range(NPE):
                P.op("pe", lambda e, jt=jt, half=half: e.matmul(
                    bank(pp + half), lhsT=DG[d][:, jt, :], rhs=U[c][:, 2 + jt + half * 512: 2 + jt + half * 512 + 512],
                    start=(jt == 0), stop=(jt == NPE - 1)), [("DG", d, 0), ("DG", d, 1), ("U", c)], [PB(pp + half)])
        P.op("act", lambda e: e.activation(out=ACC[c][:, :], in_=span(pp, 1024), func=AF.Identity, bias=VECT[:, c:c + 1]),
             [PB(pp), PB(pp + 1), "VECT"], [("ACC", c)])

    def conv_tail(c0, c1):
        for jt in range(NPE, 31):
            for c in (c0, c1):
                P.op("dve", lambda e, c=c, jt=jt: e.scalar_tensor_tensor(
                    out=ACC[c][:, :], in0=U[c][:, 2 + jt:2 + jt + 1024], scalar=WDWT[:, jt * 8 + c:jt * 8 + c + 1],
                    in1=ACC[c][:, :], op0=ALU.mult, op1=ALU.add), [("U", c), ("ACC", c), "WDWT"], [("ACC", c)])

    for cg in range(2):
        ib = wslab("cols", "w_in", C_GLU_B + cg * 512, 16)
        for cl in range(4):
            pbh = nxt("glu", [0, 1, 2, 3])
            for k in range(16):
                P.op("pe", lambda e, k=k, cl=cl, pbh=pbh, ib=ib: e.matmul(
                    bank(pbh, 32), lhsT=WS[ib][:, k, cl * 128:(cl + 1) * 128], rhs=HTH[:, k, :],
                    start=(k == 0), stop=(k == 15)), [("WS", ib), "HTH"], [PB(pbh)])
            P.op("act", lambda e, cl=cl, pbh=pbh: e.activation(out=SGB[cl][:, 0:32], in_=bank(pbh, 32), func=AF.Sigmoid),
                 [PB(pbh)], [("SGB", cl)])
            for half in range(2):
                pb_ = nxt("glu", [0, 1, 2, 3])
                for k in range(16):
                    P.op("pe", lambda e, k=k, cl=cl, half=half, pb_=pb_, ib=ib: e.matmul(
                        bank(pb_), lhsT=WS[ib][:, k, cl * 128:(cl + 1) * 128], rhs=HT[:, k, half * 512:(half + 1) * 512],
                        start=(k == 0), stop=(k == 15)), [("WS", ib)] + [("HT", half * 4 + t) for t in range(4)], [PB(pb_)])
                P.op("act", lambda e, cl=cl, half=half, pb_=pb_: e.activation(
                    out=SGB[cl][:, 32 + half * 512:32 + (half + 1) * 512], in_=bank(pb_), func=AF.Sigmoid), [PB(pb_)], [("SGB", cl)])
        wrel()
        ia = wslab("cols", "w_in", C_GLU_A + cg * 512, 16)
        for cl in range(4):
            c = cg * 4 + cl
            dg_build(c)
            pbh = nxt("glu", [0, 1, 2, 3])
            for k in range(16):
                P.op("pe", lambda e, k=k, cl=cl, pbh=pbh, ia=ia: e.matmul(
                    bank(pbh, 32), lhsT=WS[ia][:, k, cl * 128:(cl + 1) * 128], rhs=HTH[:, k, :],
                    start=(k == 0), stop=(k == 15)), [("WS", ia), "HTH"], [PB(pbh)])
            P.op("dve", lambda e, c=c, cl=cl, pbh=pbh: e.tensor_tensor(out=U[c][:, 0:32], in0=bank(pbh, 32), in1=SGB[cl][:, 0:32], op=ALU.mult),
                 [PB(pbh), ("SGB", cl)], [("U", c)])
            for half in range(2):
                pa = nxt("glu", [0, 1, 2, 3])
                for k in range(16):
                    P.op("pe", lambda e, k=k, cl=cl, half=half, pa=pa, ia=ia: e.matmul(
                        bank(pa), lhsT=WS[ia][:, k, cl * 128:(cl + 1) * 128], rhs=HT[:, k, half * 512:(half + 1) * 512],
                        start=(k == 0), stop=(k == 15)), [("WS", ia)] + [("HT", half * 4 + t) for t in range(4)], [PB(pa)])
                P.op("dve", lambda e, c=c, cl=cl, half=half, pa=pa: e.tensor_tensor(
                    out=U[c][:, 32 + half * 512:32 + (half + 1) * 512], in0=bank(pa),
                    in1=SGB[cl][:, 32 + half * 512:32 + (half + 1) * 512], op=ALU.mult), [PB(pa), ("SGB", cl)], [("U", c)])
            if cl >= 1:
                conv_chunk(c - 1)
            if cl == 2:
                conv_tail(c - 2, c - 1)
        wrel()
        conv_chunk(cg * 4 + 3)
        conv_tail(cg * 4 + 2, cg * 4 + 3)
    if "ACC" in dbg_out:
        for c in range(8):
            dump("ACC", ACC[c][:, :], [("ACC", c)], rows=slice(c * 128, (c + 1) * 128))
    P.reset("Y", mY3)
    AB = [P.alloc(f"AB{i}", [128, 1024], BF16, "Y") for i in range(2)]
    SQL = [P.alloc(f"SQL{i}", [128, 1024], BF16, "Y") for i in range(2)]
    MEAN = P.alloc("MEAN", [128, 1024], F32, "Y")
    RSTD = P.alloc("RSTD", [128, 1024], F32, "Y")
    TL = [P.alloc(f"TL{i}", [128, 1024], F32, "Y") for i in range(2)]
    xtm_off = P.zones["Y"][0] + 72 * 1024
    assert P.mark("Y") <= xtm_off and xtm_off + 16384 <= P.zones["Y"][3]
    XTm = [P.alloc_at(f"XTm{i}", [128, D], F32, xtm_off + i * 8192)[:, :] for i in range(2)]
    for t in range(2):
        dma("sp", XTm[t][:, :], memx[t * 128:(t + 1) * 128, :], [], [("XT", t)], ("xt", t))
    def qxa_blocks():
        for s_ in range(2):
            i = wslab("cols", "w_in", C_QXA + s_ * 512, 16)
            for jl in range(4):
                j = 4 * s_ + jl
                for half in range(2):
                    pb = nxt("qxb", [4, 5, 6, 7])
                    for k in range(16):
                        P.op("pe", lambda e, i=i, k=k, jl=jl, half=half, pb=pb: e.matmul(
                            bank(pb), lhsT=WS[i][:, k, jl * 128:(jl + 1) * 128], rhs=HT[:, k, half * 512:(half + 1) * 512],
                            start=(k == 0), stop=(k == 15)), [("WS", i)] + [("HT", half * 4 + t) for t in range(4)], [PB(pb)])
                    P.op("act", lambda e, j=j, half=half, pb=pb: e.activation(out=QXT[:, j, half * 512:(half + 1) * 512], in_=bank(pb),
                                                                             func=AF.Copy), [PB(pb)], [("QXT", j, half)])
                    yield
            wrel()
    qgen = qxa_blocks()

    def qstep():
        next(qgen, None)

    for c in range(8):
        r = c % 2
        P.op("act", lambda e, c=c, r=r: e.activation(out=AB[r][:, :], in_=ACC[c][:, :], func=AF.Copy), [("ACC", c)], [("AB", r)])
        P.op("act", lambda e, c=c, r=r: e.activation(out=SQL[r][:, :], in_=ACC[c][:, :], func=AF.Square), [("ACC", c)], [("SQL", r)])
        qstep()
        for half in range(2):
            P.op("pe", lambda e, c=c, r=r, half=half: e.matmul(bank(half), lhsT=ONES_B, rhs=AB[r][:, half * 512:(half + 1) * 512],
                                                              start=(c == 0), stop=(c == 7)), [("AB", r), "CB"], [PB(half)])
            P.op("pe", lambda e, c=c, r=r, half=half: e.matmul(bank(2 + half), lhsT=ONES_B, rhs=SQL[r][:, half * 512:(half + 1) * 512],
                                                              start=(c == 0), stop=(c == 7)), [("SQL", r), "CB"], [PB(2 + half)])
    P.op("act", lambda e: e.activation(out=MEAN[:, :], in_=span(0, 1024), func=AF.Copy, scale=1.0 / 1024), [PB(0), PB(1)], ["MEAN"])
    P.op("pool", lambda e: e.tensor_tensor(out=RSTD[:, :], in0=MEAN[:, :], in1=MEAN[:, :], op=ALU.mult), ["MEAN"], ["RSTD"])
    P.op("dve", lambda e: e.scalar_tensor_tensor(out=RSTD[:, :], in0=span(2, 1024), scalar=1.0 / 1024, in1=RSTD[:, :],
                                                 op0=ALU.mult, op1=ALU.subtract), [PB(2), PB(3), "RSTD"], ["RSTD"])
    P.op("act", lambda e: e.activation(out=RSTD[:, :], in_=RSTD[:, :], func=AF.Ln, bias=EPSB), ["RSTD", "CF"], ["RSTD"])
    P.op("act", lambda e: e.activation(out=RSTD[:, :], in_=RSTD[:, :], func=AF.Exp, scale=-0.5), ["RSTD"], ["RSTD"])

    def ln_apply(c):
        r = c % 2
        if c % 2 == 0:
            P.op("pool", lambda e: e.tensor_tensor(out=TL[r][:, :], in0=ACC[c][:, :], in1=MEAN[:, :], op=ALU.subtract),
                 [("ACC", c), "MEAN"], [("TL", r)])
        else:
            P.op("dve", lambda e: e.tensor_tensor(out=TL[r][:, :], in0=ACC[c][:, :], in1=MEAN[:, :], op=ALU.subtract),
                 [("ACC", c), "MEAN"], [("TL", r)])
        P.op("dve", lambda e: e.tensor_tensor(out=TL[r][:, :], in0=TL[r][:, :], in1=RSTD[:, :], op=ALU.mult),
             [("TL", r), "RSTD"], [("TL", r)])
        P.op("act", lambda e: e.activation(out=CACT[:, c, :], in_=TL[r][:, :], func=AF.Silu,
                                           scale=VECT[:, 8 + c:9 + c], bias=VECT[:, 16 + c:17 + c]),
             [("TL", r), "VECT"], [("CACT", c)])

    for c in range(8):
        qstep()
        ln_apply(c)
    for _ in range(16):
        qstep()
    if "CACT" in dbg_out:
        for c in range(8):
            dump("CACT", CACT[:, c, :], [("CACT", c)], rows=slice(c * 128, (c + 1) * 128))
    P.reset("Y", mY2)
    if stop == "D":
        return finish()

    OT_XA = P.alloc("OT_XA", [128, 8, 1024], BF16, "Y")
    mY4 = P.mark("Y")
    MEMT = P.alloc("MEMT", [128, 16, 256], BF16, "Y")
    KMT = P.alloc("KMT", [128, 8, 256], BF16, "Y")
    VMA = P.alloc("VMA", [128, 2, 4, 257], BF16, "Y")
    mY5 = P.mark("Y")
    P.op("dve", lambda e: e.memset(VMA[:, :, :, 256:257], 1.0), [], ["VMA1"])
    norm_transpose(memx, 2, g_mem, MEMT, "MEMT", xt_alias=XTm, preloaded=2)
    assert P.mark("Y") <= xtm_off
    P.reset("Y", mY5)
    aotb_banks[:] = [6, 7]
    SQX = [P.alloc(f"SQX{i}", [128, 512], BF16, "Y") for i in range(4)]
    PTX = [P.alloc(f"PTx{i}", [128, 512], BF16, "Y") for i in range(4)]
    ONX = P.alloc("ONx", [128, 4, 256], BF16, "Y")
    O2X = P.alloc("O2x", [128, 4, 257], F32, "Y")
    RRX = P.alloc("RRx", [128, 32], F32, "Y")
    assert P.mark("Y") <= xtm_off
    for h in range(4):
        for half in range(2):
            pb3 = nxt("stb", [6, 7])
            for cc in range(2):
                r = nxt("sqx", [0, 1, 2, 3])
                P.op("act", lambda e, h=h, half=half, cc=cc, r=r: e.activation(
                    out=SQX[r][:, :], in_=QXT[:, 2 * h + cc, half * 512:(half + 1) * 512], func=AF.Square),
                    [("QXT", 2 * h + cc, half)], [("SQX", r)])
                P.op("pe", lambda e, r=r, cc=cc, pb3=pb3: e.matmul(bank(pb3), lhsT=ONES_B, rhs=SQX[r][:, :], start=(cc == 0), stop=(cc == 1)),
                     [("SQX", r), "CB"], [PB(pb3)])
            P.op("dve", lambda e, h=h, half=half, pb3=pb3: e.tensor_reduce(out=sv(SV_STX + h * 2 + half), in_=bank(pb3), axis=AX.X, op=ALU.max),
                 [PB(pb3)], [("SVc", SV_STX + h * 2 + half)])
    P.op("dve", lambda e: e.tensor_reduce(out=sv(SV_QXMAX, 4), in_=sv(SV_STX, 8).rearrange("p (j t) -> p j t", t=2),
                                          axis=AX.X, op=ALU.max), [("SVc", SV_STX + c) for c in range(8)], ["QXMAX"])
    for hp in range(2):
        i = wslab("cols", "w_mem_kv", hp * 512, 16)
        for jl in range(4):
            j = 4 * hp + jl
            pb = nxt("kb", [0, 1, 2, 3])
            for k in range(16):
                P.op("pe", lambda e, i=i, k=k, jl=jl, pb=pb: e.matmul(
                    bank(pb, 256), lhsT=WS[i][:, k, jl * 128:(jl + 1) * 128], rhs=MEMT[:, k, :], start=(k == 0), stop=(k == 15)),
                    [("WS", i), ("MEMT", 0), ("MEMT", 1)], [PB(pb)])
            P.op("act", lambda e, j=j, pb=pb: e.activation(out=KMT[:, j, :], in_=bank(pb, 256), func=AF.Copy), [PB(pb)], [("KMT", j)])
        wrel()
        i = wslab("cols", "w_mem_kv", 1024 + hp * 512, 16)
        for mt in range(2):
            pb = nxt("kb", [0, 1, 2, 3])
            for k in range(16):
                P.op("pe", lambda e, i=i, k=k, mt=mt, pb=pb: e.matmul(
                    bank(pb), lhsT=MEMT[:, k, mt * 128:(mt + 1) * 128], rhs=WS[i][:, k, :], start=(k == 0), stop=(k == 15)),
                    [("WS", i), ("MEMT", mt)], [PB(pb)])
            P.op("act", lambda e, mt=mt, hp=hp, pb=pb: e.activation(
                out=VMA[:, mt, 2 * hp:2 * hp + 2, 0:256], in_=bank(pb).rearrange("p (h e) -> p h e", h=2), func=AF.Copy),
                [PB(pb)], [("VMA", mt, hp)])
        wrel()
        for h in (2 * hp, 2 * hp + 1):
            pb3 = nxt("stb", [6, 7])
            for cc in range(2):
                r = nxt("sqx", [0, 1, 2, 3])
                P.op("act", lambda e, h=h, cc=cc, r=r: e.activation(out=SQX[r][:, 0:256], in_=KMT[:, 2 * h + cc, :], func=AF.Square),
                     [("KMT", 2 * h + cc)], [("SQX", r)])
                P.op("pe", lambda e, r=r, cc=cc, pb3=pb3: e.matmul(bank(pb3, 256), lhsT=ONES_B, rhs=SQX[r][:, 0:256], start=(cc == 0), stop=(cc == 1)),
                     [("SQX", r), "CB"], [PB(pb3)])
            P.op("dve", lambda e, h=h, pb3=pb3: e.tensor_reduce(out=sv(SV_KMMAX + h), in_=bank(pb3, 256), axis=AX.X, op=ALU.max),
                 [PB(pb3)], [("SVk", h)])
        h0 = 2 * hp
        P.op("dve", lambda e, h0=h0: e.tensor_tensor(out=sv(SV_T + h0, 2), in0=sv(SV_QXMAX + h0, 2), in1=sv(SV_KMMAX + h0, 2), op=ALU.mult),
             ["QXMAX", ("SVk", h0), ("SVk", h0 + 1)], [("SVT0x", hp)])
        P.op("act", lambda e, h0=h0: e.activation(out=sv(SV_T + 8 + h0, 2), in_=sv(SV_T + h0, 2), func=AF.Sqrt), [("SVT0x", hp)], [("SVT8x", hp)])
        P.op("dve", lambda e, h0=h0: e.tensor_scalar(out=sv(SV_NEGMX + h0, 2), in0=sv(SV_T + 8 + h0, 2), scalar1=float(-SCALE_XA), scalar2=None,
                                                     op0=ALU.mult), [("SVT8x", hp)], [("NEGMX", hp)])
        xsteps = [(h, qb, mt) for h in (2 * hp, 2 * hp + 1) for qb in range(2) for mt in range(2)]
        xinfo = {}

        def x_st(si):
            h, qb, mt = xsteps[si]
            spb = nxt("spbx", [4, 5])
            r = nxt("pt", [0, 1, 2, 3])
            for cc in range(2):
                P.op("pe", lambda e, cc=cc: e.matmul(
                    bank(spb), lhsT=KMT[:, 2 * h + cc, mt * 128:(mt + 1) * 128], rhs=QXT[:, 2 * h + cc, qb * 512:(qb + 1) * 512],
                    start=(cc == 0), stop=(cc == 1)), [("KMT", 2 * h + cc), ("QXT", 2 * h + cc, qb)], [PB(spb)])
            P.op("act", lambda e: e.activation(out=PTX[r][:, :], in_=bank(spb), func=AF.Exp,
                                               scale=float(SCALE_XA), bias=sv(SV_NEGMX + h)),
                 [PB(spb), ("NEGMX", hp)], [("PT", r)])
            xinfo[si] = r

        def x_pv(si):
            h, qb, mt = xsteps[si]
            r = xinfo.pop(si)
            for qi in range(4):
                P.op("pe", lambda e, qi=qi: e.matmul(
                    bank(qi, 257), lhsT=PTX[r][:, qi * 128:(qi + 1) * 128], rhs=VMA[:, mt, h, :], start=(mt == 0), stop=(mt == 1)),
                    [("PT", r), ("VMA", mt, h // 2), "VMA1"], [PB(qi)])
            if mt == 1:
                attn_out_group((None, O2X, None, ONX, RRX, None), OT_XA, "OTXA", h * 2, 4 * qb * 128, False, delay=2)

        x_st(0)
        for si in range(len(xsteps)):
            if si + 1 < len(xsteps):
                x_st(si + 1)
            x_pv(si)
            tick()
        flush()
    flush()
    if "OTXA" in dbg_out:
        for c in range(8):
            dump("OTXA", OT_XA[:, c, :], [("OTXA", (c // 2) * 2, q * 128) for q in range(8)], rows=slice(c * 128, (c + 1) * 128))
    P.reset("Y", mY4, top=True)
    if stop == "E":
        return finish()

    MERGED = P.alloc("MERGED", [128, 16, 1024], BF16, "Y", top=True)
    MG = [P.alloc(f"MG{i}", [128, 1024], F32, "Y") for i in range(4)]
    SIGB = [[P.alloc(f"SIGB{a}{i}", [128, 1024], BF16, "Y") for i in range(4)] for a in range(2)]
    TMPm = [P.alloc(f"TMPm{i}", [128, 512], F32, "Y") for i in range(2)]
    branches = [("w_conv_out", CACT, "CACT"), ("w_da_out", OT_DA, "OTDA"), ("w_xa_out", OT_XA, "OTXA")]

    def act_keys(r, k, half):
        if r == 0:
            return [("CACT", k)]
        nm = "OTDA" if r == 1 else "OTXA"
        return [(nm, (k // 2) * 2, (half * 4 + t) * 128) for t in range(4)]

    for cg in range(4):
        for r in range(3):
            wsrc, ACTr, _ = branches[r]
            sa = nxt("sigb", [0, 1])
            ig = wslab("cols", "w_in", C_GATE + r * 2048 + cg * 512, 16)
            for cl in range(4):
                for half in range(2):
                    pg = nxt("mgg", [0, 1, 2, 3])
                    hs = slice(half * 512, (half + 1) * 512)
                    for k in range(16):
                        P.op("pe", lambda e, ig=ig, k=k, cl=cl, hs=hs, pg=pg: e.matmul(
                            bank(pg), lhsT=WS[ig][:, k, cl * 128:(cl + 1) * 128], rhs=HT[:, k, hs],
                            start=(k == 0), stop=(k == 15)), [("WS", ig)] + [("HT", half * 4 + t) for t in range(4)], [PB(pg)])
                    P.op("act", lambda e, sa=sa, cl=cl, hs=hs, pg=pg: e.activation(out=SIGB[sa][cl][:, hs], in_=bank(pg), func=AF.Sigmoid),
                         [PB(pg)], [("SIGB", sa, cl, half)])
            wrel()
            io = wslab("cols", wsrc, cg * 512, 8)
            for cl in range(4):
                c = cg * 4 + cl
                for half in range(2):
                    py = nxt("mgy", [4, 5, 6, 7])
                    hs = slice(half * 512, (half + 1) * 512)
                    for k in range(8):
                        P.op("pe", lambda e, io=io, k=k, cl=cl, hs=hs, py=py, ACTr=ACTr: e.matmul(
                            bank(py), lhsT=WS[io][:, k, cl * 128:(cl + 1) * 128], rhs=ACTr[:, k, hs],
                            start=(k == 0), stop=(k == 7)), [("WS", io)] + act_keys(r, k, half), [PB(py)])
                    if r == 0:
                        P.op("dve", lambda e, sa=sa, py=py, cl=cl, hs=hs: e.tensor_tensor(out=MG[cl][:, hs], in0=bank(py), in1=SIGB[sa][cl][:, hs], op=ALU.mult),
                             [PB(py), ("SIGB", sa, cl, half)], [("MG", cl, half)])
                    else:
                        q = nxt("tmpm", [0, 1])
                        P.op("dve", lambda e, sa=sa, q=q, py=py, cl=cl, hs=hs: e.tensor_tensor(out=TMPm[q][:, :], in0=bank(py), in1=SIGB[sa][cl][:, hs], op=ALU.mult),
                             [PB(py), ("SIGB", sa, cl, half)], [("TMPm", q)])
                        if r == 1:
                            P.op("pool", lambda e, q=q, cl=cl, hs=hs: e.tensor_tensor(out=MG[cl][:, hs], in0=MG[cl][:, hs], in1=TMPm[q][:, :], op=ALU.add),
                                 [("MG", cl, half), ("TMPm", q)], [("MG", cl, half)])
                        else:
                            P.op("pool", lambda e, q=q, cl=cl, hs=hs, c=c: e.tensor_tensor(out=MERGED[:, c, hs], in0=MG[cl][:, hs], in1=TMPm[q][:, :], op=ALU.add),
                                 [("MG", cl, half), ("TMPm", q)], [("MERGED", c, half)])
            wrel()
    if "MERGED" in dbg_out:
        for c in range(16):
            dump("MERGED", MERGED[:, c, :], [("MERGED", c, 0), ("MERGED", c, 1)], rows=slice(c * 128, (c + 1) * 128))
    P.reset("Y")
    P.reset("H")
    P.reset("X")
    if stop == "F":
        return finish()

    X1T = ([P.alloc(f"X1T{c}", [128, 1024], F32, "H") for c in range(8)]
           + [P.alloc(f"X1T{c}", [128, 1024], F32, "X") for c in range(8, 12)]
           + [P.alloc(f"X1T{c}", [128, 1024], F32, "Y") for c in range(12, 16)])
    H2T = P.alloc("H2T", [128, 16, 1024], BF16, "Y")
    RS2 = P.alloc("RS2", [128, 1024], F32, "Y")
    mY6 = P.mark("Y")
    XR = [P.alloc(f"XR{i}", [128, 8, 128], F32, "Y") for i in range(2)]
    SQm = [P.alloc(f"SQm{i}", [128, 1024], BF16, "Y") for i in range(2)]

    def ssq_mm(c, q):
        for half in range(2):
            P.op("pe", lambda e, half=half: e.matmul(bank(6 + half), lhsT=ONES_B, rhs=SQm[q][:, half * 512:(half + 1) * 512],
                                                     start=(c == 0), stop=(c == 15)), [("SQm", q), "CB"], [PB(6 + half)])

    for cg in range(4):
        i = wslab("cols", "w_mix_out", cg * 512, 16)
        for cl in range(4):
            c = cg * 4 + cl
            q = c % 2
            dma("sp", XR[q][:, :, :], xo[:, c * 128:(c + 1) * 128].rearrange("(t p) c -> p t c", p=128), [], [("XR", q)], ("xr", q))
            pp = nxt("mix", [0, 2, 4])
            for half in range(2):
                pb = pp + half
                for k in range(16):
                    P.op("pe", lambda e, i=i, k=k, cl=cl, half=half, pb=pb: e.matmul(
                        bank(pb), lhsT=WS[i][:, k, cl * 128:(cl + 1) * 128], rhs=MERGED[:, k, half * 512:(half + 1) * 512],
                        start=(k == 0), stop=False), [("WS", i), ("MERGED", k, half)], [PB(pb)])
                for tl in range(4):
                    P.op("pe", lambda e, q=q, half=half, tl=tl, pb=pb: e.matmul(
                        bank(pb, 128, tl * 128), lhsT=XR[q][:, half * 4 + tl, :], rhs=ID_F, start=False, stop=(tl == 3)),
                        [("XR", q), "CF"], [PB(pb)])
            tick()
            P.op("act", lambda e, c=c, pp=pp: e.activation(out=X1T[c][:, :], in_=span(pp, 1024), func=AF.Copy), [PB(pp), PB(pp + 1)], [("X1T", c)])
            P.op("act", lambda e, c=c, pp=pp: e.activation(out=H2T[:, c, :], in_=span(pp, 1024), func=AF.Copy, scale=VECT[:, 24 + c:25 + c]),
                 [PB(pp), PB(pp + 1), "VECT"], [("H2T", c)])
            P.op("act", lambda e, q=q, pp=pp: e.activation(out=SQm[q][:, :], in_=span(pp, 1024), func=AF.Square), [PB(pp), PB(pp + 1)], [("SQm", q)])
            defer(1, lambda c=c, q=q: ssq_mm(c, q))
        wrel()
    flush()
    P.op("act", lambda e: e.activation(out=RS2[:, :], in_=span(6, 1024), func=AF.Ln, scale=1.0 / D, bias=EPSB), [PB(6), PB(7), "CF"], ["RS2"])
    P.op("act", lambda e: e.activation(out=RS2[:, :], in_=RS2[:, :], func=AF.Exp, scale=-0.5), ["RS2"], ["RS2"])
    if "X1T" in dbg_out:
        for c in range(16):
            dump("X1T", X1T[c][:, :], [("X1T", c)], rows=slice(c * 128, (c + 1) * 128))
            dump("H2T", H2T[:, c, :], [("H2T", c)], rows=slice(c * 128, (c + 1) * 128))
    P.reset("Y", mY6, top=True)
    if stop == "G":
        return finish()

    ACTG = [P.alloc(f"ACTG{i}", [128, 4, 1024], BF16, "Y") for i in range(2)]
    RL = [P.alloc(f"RL{i}", [128, 1024], BF16, "Y") for i in range(2)]
    TRL = [P.alloc(f"TRL{i}", [128, 1024], BF16, "Y") for i in range(2)]

    def ffn_up(g):
        iu = wslab("cols", "w_up", g * 512, 16)
        for f in range(4):
            pp = nxt("up", [0, 2])
            for half in range(2):
                for k in range(16):
                    P.op("pe", lambda e, iu=iu, k=k, f=f, half=half, pp=pp: e.matmul(
                        bank(pp + half), lhsT=WS[iu][:, k, f * 128:(f + 1) * 128], rhs=H2T[:, k, half * 512:(half + 1) * 512],
                        start=(k == 0), stop=(k == 15)), [("WS", iu), ("H2T", k)], [PB(pp + half)])
            q = nxt("rl", [0, 1])
            P.op("act", lambda e, q=q, pp=pp: e.activation(out=RL[q][:, :], in_=span(pp, 1024), func=AF.Relu), [PB(pp), PB(pp + 1)], [("RL", q)])
            P.op("dve", lambda e, q=q: e.tensor_tensor(out=TRL[q][:, :], in0=RL[q][:, :], in1=RS2[:, :], op=ALU.mult),
                 [("RL", q), "RS2"], [("TRL", q)])
            P.op("dve", lambda e, q=q, g=g, f=f: e.tensor_tensor(out=ACTG[g % 2][:, f, :], in0=TRL[q][:, :], in1=TRL[q][:, :], op=ALU.mult),
                 [("TRL", q)], [("ACTG", g % 2, f)])
        wrel()

    def ffn_down(g):
        idn = wslab("rows", g)
        WD = wd_view(idn)
        for c in range(16):
            pp = nxt("dn", [4, 6])
            for half in range(2):
                for f in range(4):
                    P.op("pe", lambda e, WD=WD, f=f, c=c, half=half, pp=pp, g=g: e.matmul(
                        bank(pp + half), lhsT=WD[:, f, c * 128:(c + 1) * 128], rhs=ACTG[g % 2][:, f, half * 512:(half + 1) * 512],
                        start=(f == 0), stop=(f == 3)), [("WS", idn), ("ACTG", g % 2, f)], [PB(pp + half)])
            P.op("dve", lambda e, c=c, pp=pp: e.tensor_tensor(out=X1T[c][:, :], in0=span(pp, 1024), in1=X1T[c][:, :], op=ALU.add),
                 [PB(pp), PB(pp + 1), ("X1T", c)], [("X1T", c)])
        wrel()

    NG = DFF // 512
    ffn_up(0)
    for g in range(NG):
        if g + 1 < NG:
            ffn_up(g + 1)
        ffn_down(g)
    if "X2T" in dbg_out:
        for c in range(16):
            dump("X2T", X1T[c][:, :], [("X1T", c)], rows=slice(c * 128, (c + 1) * 128))
    P.reset("Y", mY6)

    GF = P.alloc("GF", [128, D], F32, "Y")
    YT = [P.alloc(f"YT{i}", [128, D], F32, "Y") for i in range(2)]
    JKF = P.alloc("JKF", [128, D], BF16, "Y")
    SF = P.alloc("SF", [128, 8], F32, "Y")
    dma("sp", GF[:, :], g_final.partition_broadcast(128), [], ["GF"], ("misc", 8))
    for t in range(8):
        st = t % 2
        for c in range(16):
            P.op("pe", lambda e, c=c, t=t, st=st: e.transpose(out=PS[:, st * 2048 + c * 128: st * 2048 + (c + 1) * 128],
                                                             in_=X1T[c][:, t * 128:(t + 1) * 128], identity=ID_F),
                 [("X1T", c), "CF"], [PB(st * 4 + c // 4)])
        pkeys = [PB(st * 4 + b) for b in range(4)]
        P.op("act", lambda e, st=st: e.activation(out=JKF[:, :], in_=PS[:, st * 2048:(st + 1) * 2048], func=AF.Square,
                                                  accum_out=SF[:, st:st + 1]), pkeys, ["JKF", ("SF", st)])
        P.op("act", lambda e, st=st: e.activation(out=SF[:, 2 + st:3 + st], in_=SF[:, st:st + 1], func=AF.Sqrt, scale=1.0 / D, bias=EPSB),
             [("SF", st), "CF"], [("SF2", st)])
        P.op("dve", lambda e, st=st: e.reciprocal(out=SF[:, 4 + st:5 + st], in_=SF[:, 2 + st:3 + st]), [("SF2", st)], [("SF4", st)])
        P.op("dve", lambda e, st=st: e.scalar_tensor_tensor(out=YT[st][:, :], in0=PS[:, st * 2048:(st + 1) * 2048], scalar=SF[:, 4 + st:5 + st],
                                                            in1=GF[:, :], op0=ALU.mult, op1=ALU.mult), pkeys + [("SF4", st), "GF"], [("YT", st)])
        dma("sp", y[t * 128:(t + 1) * 128, :], YT[st][:, :], [("YT", st)], [("y", t)], ("yo", st))
    return finish()


def make_consts(half):
    cf = np.zeros((128, NCF), np.float32)
    cf[:, CF_ID:CF_ID + 128] = np.eye(128, dtype=np.float32)
    cf[:, CF_ONES:CF_ONES + 128] = 1.0
    k = np.arange(128)[:, None]
    q = np.arange(128)[None, :]
    cf[:, CF_TRI:CF_TRI + 128] = (q >= k).astype(np.float32)
    cf[:, CF_SWAP:CF_SWAP + 128] = (k == (q + 64) % 128).astype(np.float32)
    inv_freq = (1.0 / (np.float32(10000.0) ** (np.arange(0, 128, 2, dtype=np.float32) / np.float32(128)))).astype(np.float32)
    cf[:, CF_INVF] = np.concatenate([inv_freq, inv_freq])
    cf[:64, CF_SGN] = -1.0
    cf[64:, CF_SGN] = 1.0
    cf[:, CF_CBIAS] = 0.0 if half == 1 else -30000.0
    cf[:, CF_EPS] = EPS
    return cf


def make_in_maps(inp, cores=range(8)):
    x = np.asarray(inp["x"], np.float32)
    mem = np.asarray(inp["mem"], np.float32)
    pos = np.asarray(inp["positions"], np.int32)
    f = lambda k: np.ascontiguousarray(np.asarray(inp[k], np.float32))
    shared = {
        "g_mix": f("g_mix").reshape(1, D),
        "w_in": f("w_in").reshape(D, DIN),
        "w_dw": f("w_dw").reshape(31 * 8, 128),
        "vecs": np.concatenate([f("b_dw").reshape(8, 128), f("g_conv_ln").reshape(8, 128),
                                f("b_conv_ln").reshape(8, 128), f("g_mlp").reshape(16, 128)], axis=0),
        "w_conv_out": f("w_conv_out").reshape(1024, D),
        "lams": np.concatenate([f("lambda_q1").reshape(1, 128), f("lambda_k1").reshape(1, 128),
                                f("lambda_q2").reshape(1, 128), f("lambda_k2").reshape(1, 128)], axis=0),
        "g_subln": f("g_subln").reshape(1, 256),
        "w_da_out": f("w_da_out").reshape(1024, D),
        "g_mem": f("g_mem").reshape(1, D),
        "w_mem_kv": f("w_mem_kv").reshape(D, D),
        "w_xa_out": f("w_xa_out").reshape(1024, D),
        "w_mix_out": f("w_mix_out").reshape(D, D),
        "w_up": f("w_up").reshape(D, DFF),
        "w_down": f("w_down").reshape(DFF, D),
        "g_final": f("g_final").reshape(1, D),
    }
    maps = []
    for c in cores:
        b, half = c // 2, c % 2
        m = dict(shared)
        m["xo"] = np.ascontiguousarray(x[b, half * NTOK:(half + 1) * NTOK])
        if half == 1:
            m["xc"] = np.ascontiguousarray(x[b, 0:NCTX])
            pc = pos[b, 0:NCTX]
        else:
            m["xc"] = np.zeros((NCTX, D), np.float32)
            pc = np.zeros((NCTX,), np.int32)
        m["posa"] = np.concatenate([pc, pos[b, half * NTOK:(half + 1) * NTOK]]).reshape(1, -1).astype(np.int32)
        m["memx"] = np.ascontiguousarray(mem[b])
        m["cf"] = make_consts(half)
        maps.append(m)
    return maps


_NC_CACHE = {}


def kernel(**inputs):
    if "nc" not in _NC_CACHE:
        plan = build().plan_out
        _NC_CACHE["nc"] = build(plan=plan)
    nc = _NC_CACHE["nc"]
    maps = make_in_maps(inputs)
    res = run_bass_kernel_spmd(nc, maps, core_ids=list(range(8)))
    out = np.zeros((B, S, D), np.float32)
    for c in range(8):
        b, half = c // 2, c % 2
        out[b, half * NTOK:(half + 1) * NTOK] = res.results[c]["y"]
    return out
```

```python
import math
import numpy as np
import concourse.bass as bass
import concourse.mybir as mybir
from concourse.bass_utils import run_bass_kernel_spmd

F32 = mybir.dt.float32
BF16 = mybir.dt.bfloat16
I32 = mybir.dt.int32
AF = mybir.ActivationFunctionType
ALU = mybir.AluOpType
AX = mybir.AxisListType

D = 2048
S = 2048
B = 4
NTOK = 1024
NCTX = 1024
MEM = 256
DIN = 12288
DFF = 8192
EPS = 1e-6
LAMBDA_INIT = 0.8 - 0.6 * math.exp(0.0)
PI = math.pi
SCALE_DA = 128 ** -0.5
SCALE_XA = 256 ** -0.5

C_GLU_A, C_GLU_B, C_QDA, C_KDA, C_VDA, C_QXA, C_GATE = 0, 1024, 2048, 3072, 4096, 5120, 6144

CF_ID, CF_ONES, CF_TRI, CF_SWAP, CF_INVF, CF_SGN, CF_CBIAS, CF_EPS, NCF = 0, 128, 256, 384, 512, 513, 514, 515, 520

STRICT_SAME_ENGINE = True
SB_BASE = 16640
SB_END = 229120


class Op:
    __slots__ = ("eng", "fn", "deps", "signal", "dma", "val")

    def __init__(self, eng, fn):
        self.eng = eng
        self.fn = fn
        self.deps = []
        self.signal = False
        self.dma = None
        self.val = None


class Prog:
    ENGS = ("pe", "act", "dve", "pool", "sp")

    def __init__(self, nc):
        self.nc = nc
        self.ops = []
        self.res = {}
        self.dma_cnt = {}
        self.last = {e: None for e in self.ENGS}
        self.lastc = {e: None for e in self.ENGS}
        self.strict = False
        self.bar = {e: [] for e in self.ENGS}
        self.zones = {}
        self.offs = {}
        self.nalloc = 0

    def zone(self, name, lo, hi):
        self.zones[name] = [lo, hi, lo, hi]

    def alloc(self, name, shape, dtype, zone, top=False):
        esz = 4 if dtype in (F32, I32) else 2
        n = 1
        for s in shape[1:]:
            n *= s
        nbytes = (n * esz + 63) // 64 * 64
        z = self.zones[zone]
        if top:
            z[3] -= nbytes
            off = z[3]
        else:
            off = z[2]
            z[2] += nbytes
        assert z[2] <= z[3], (name, zone, z)
        self.offs[name] = off
        return self.alloc_at(name, shape, dtype, off)

    def alloc_at(self, name, shape, dtype, off):
        self.nalloc += 1
        return self.nc.alloc_sbuf_tensor_at(f"{name}_{self.nalloc}", list(shape), dtype, offset=off)

    def mark(self, zone):
        return self.zones[zone][2]

    def reset(self, zone, m=None, top=False):
        z = self.zones[zone]
        z[2] = z[0] if m is None else m
        if top:
            z[3] = z[1]
        self.barrier()

    def barrier(self):
        snap = [o for o in self.lastc.values() if o is not None]
        if self.strict:
            snap += [o for o in self.last.values() if o is not None and o.dma is not None]
        for e in self.ENGS:
            if e != "pe":
                self.bar[e] = list(snap)

    def op(self, eng, fn, reads=(), writes=(), dma=None):
        o = Op(eng, fn)
        pr = [r for r in reads if isinstance(r, tuple) and r[0] == "ps"]
        if pr:
            writes = list(writes) + [r for r in pr if r not in writes]
        if dma is not None:
            c = self.dma_cnt.get(dma, 0) + 16
            self.dma_cnt[dma] = c
            o.dma = (dma, c)
        deps = {}

        def add(d, kind):
            deps.setdefault(id(d), [d, set()])[1].add(kind)

        for r in reads:
            st = self.res.get(r)
            if st is not None and st[0] is not None:
                add(st[0], "raw")
        for w in writes:
            st = self.res.get(w)
            if st is not None:
                if st[0] is not None:
                    add(st[0], "waw")
                for rd in st[1].values():
                    add(rd, "war")
                for rd in st[2]:
                    add(rd, "war")
        for b in self.bar[eng]:
            add(b, "raw")
        self.bar[eng] = []
        for d, kinds in deps.values():
            if d is o:
                continue
            if d.dma is not None:
                if o.dma is not None and o.dma[0] == d.dma[0] and kinds == {"waw"}:
                    continue
                o.deps.append(d)
                continue
            if d.eng == eng and o.dma is None:
                if eng == "pe":
                    continue
                if "raw" not in kinds and not STRICT_SAME_ENGINE:
                    continue
            d.signal = True
            o.deps.append(d)
        for r in reads:
            st = self.res.setdefault(r, [None, {}, []])
            if o.dma is not None:
                st[2].append(o)
            else:
                st[1][eng] = o
        for w in writes:
            self.res[w] = [o, {}, []]
        self.ops.append(o)
        self.last[eng] = o
        if o.dma is None and fn is not None:
            self.lastc[eng] = o
        return o

    def emit(self):
        nc = self.nc
        per = {e: [o for o in self.ops if o.eng == e] for e in self.ENGS}
        for e in self.ENGS:
            c = 0
            for o in per[e]:
                if o.dma is None and o.signal:
                    c += 1
                    o.val = c
        self.stats = {e: (len(per[e]), sum(1 for o in per[e] if o.signal)) for e in self.ENGS}
        import contextlib
        with contextlib.ExitStack() as st:
            esem = {e: st.enter_context(nc.semaphore(f"s_{e}")) for e in self.ENGS}
            dsem = {k: st.enter_context(nc.semaphore(f"d_{i}")) for i, k in enumerate(self.dma_cnt)}
            block = st.enter_context(nc.Block())

            def run(eng_name):
                def body(eng):
                    seen = {}
                    for o in per[eng_name]:
                        for d in o.deps:
                            if d.dma is not None:
                                sem, v = dsem[d.dma[0]], d.dma[1]
                                key = ("d", d.dma[0])
                            else:
                                sem, v = esem[d.eng], d.val
                                key = ("e", d.eng)
                            if seen.get(key, 0) >= v:
                                continue
                            seen[key] = v
                            eng.wait_ge(sem, v)
                        if o.fn is None:
                            continue
                        ins = o.fn(eng)
                        if o.dma is not None:
                            ins.then_inc(dsem[o.dma[0]], 16)
                        elif o.signal:
                            ins.then_inc(esem[eng_name], 1)
                return body

            block.tensor(run("pe"))
            block.scalar(run("act"))
            block.vector(run("dve"))
            block.gpsimd(run("pool"))
            block.sync(run("sp"))


def build(stop="all", dbg=(), plan=None):
    nc = bass.Bass("TRN2", target_bir_lowering=False)
    P = Prog(nc)
    P.strict = bool(dbg)
    o = SB_BASE
    P.zone("P", o, o + 7168); o += 7168
    P.zone("WS", o, o + 3 * 16384); o += 3 * 16384
    P.zone("H", o, o + 32768); o += 32768
    P.zone("X", o, o + 16384); o += 16384
    P.zone("Y", o, SB_END)

    def din(name, shape, dt=F32):
        return nc.dram_tensor(name, list(shape), dt, kind="ExternalInput").ap()

    xo = din("xo", [NTOK, D])
    xc = din("xc", [NCTX, D])
    memx = din("memx", [MEM, D])
    posa = din("posa", [1, NCTX + NTOK], I32)
    cf_d = din("cf", [128, NCF])
    g_mix = din("g_mix", [1, D])
    w_in = din("w_in", [D, DIN])
    w_dw = din("w_dw", [31 * 8, 128])
    vecs = din("vecs", [40, 128])
    w_conv_out = din("w_conv_out", [1024, D])
    lams = din("lams", [4, 128])
    g_subln = din("g_subln", [1, 256])
    w_da_out = din("w_da_out", [1024, D])
    g_mem = din("g_mem", [1, D])
    w_mem_kv = din("w_mem_kv", [D, D])
    w_xa_out = din("w_xa_out", [1024, D])
    w_mix_out = din("w_mix_out", [D, D])
    w_up = din("w_up", [D, DFF])
    w_down = din("w_down", [DFF, D])
    g_final = din("g_final", [1, D])
    y = nc.dram_tensor("y", [NTOK, D], F32, kind="ExternalOutput").ap()
    dbg_out = {}
    for name, shape in dbg:
        dbg_out[name] = nc.dram_tensor("dbg_" + name, list(shape), F32, kind="ExternalOutput").ap()

    PS = nc.alloc_psum_tensor("ps", [128, 4096], F32)

    def bank(b, n=512, off=0):
        return PS[:, b * 512 + off: b * 512 + off + n]

    def span(b, n):
        return PS[:, b * 512: b * 512 + n]

    def bank_bf(b, n=1024):
        return PS[:, b * 512:(b + 1) * 512].bitcast(BF16)[:, 0:n]

    def PB(b):
        return ("ps", b)

    rot = {}
    pend = []

    def defer(delay, fn):
        pend.append([delay, fn])

    def tick():
        due = []
        for it in pend:
            it[0] -= 1
        while pend and pend[0][0] <= 0:
            due.append(pend.pop(0)[1])
        for fn in due:
            fn()

    def flush():
        while pend:
            pend.pop(0)[1]()

    def nxt(name, choices):
        i = rot.get(name, 0)
        rot[name] = i + 1
        return choices[i % len(choices)]

    CF = P.alloc("CF", [128, NCF], F32, "P")
    CB = P.alloc("CB", [128, 512], BF16, "P")
    ID_F = CF[:, CF_ID:CF_ID + 128]
    EPSB = CF[:, CF_EPS:CF_EPS + 1]
    ID_B = CB[:, 0:128]
    ONES_B = CB[:, 128:256]
    TRI_B = CB[:, 256:384]
    SWAP_B = CB[:, 384:512]
    SV = P.alloc("SV", [128, 128], F32, "P")
    VECT = P.alloc("VECT", [128, 40], F32, "P")
    WDWT = P.alloc("WDWT", [128, 248], F32, "P")
    GSUB = P.alloc("GSUB", [128, 256], F32, "P")
    WS = [P.alloc(f"WS{i}", [128, 16, 512], BF16, "WS") for i in range(3)]

    def wd_view(i):
        return WS[i][:, :, :].rearrange("p k c -> p (k c)").rearrange("p (f c) -> p f c", f=4)

    WT = {"w_in": w_in, "w_mem_kv": w_mem_kv, "w_conv_out": w_conv_out, "w_da_out": w_da_out,
          "w_xa_out": w_xa_out, "w_mix_out": w_mix_out, "w_up": w_up}
    plan_out = []
    wst = {"n": 0, "emitted": 0, "rel": 0, "cap": 1}

    def slab_src(desc):
        if desc[0] == "cols":
            _, wname, c0, nk = desc
            return w_cols(WT[wname], c0), nk
        _, g = desc
        return w_down[g * 512:(g + 1) * 512, :].rearrange("(f p) c -> p f c", p=128), None

    def emit_load(m, desc):
        src, nk = slab_src(desc)
        i = m % 3
        dst = wd_view(i) if nk is None else WS[i][:, 0:nk, :]
        P.op("pool", lambda e, dst=dst, src=src: e.dma_start(out=dst, in_=src), [], [("WS", i)], dma=("ws", i))

    def prefetch():
        if plan is None:
            return
        while wst["emitted"] < min(wst["rel"] + 3, len(plan), wst["cap"]):
            emit_load(wst["emitted"], plan[wst["emitted"]])
            wst["emitted"] += 1

    def wslab(*desc):
        n = wst["n"]
        wst["n"] += 1
        plan_out.append(desc)
        assert n == wst["rel"], "slabs are consumed one at a time"
        if plan is None:
            emit_load(n, desc)
        else:
            assert tuple(plan[n]) == tuple(desc), (n, plan[n], desc)
            prefetch()
            assert wst["emitted"] > n
        return n % 3

    def wrel():
        wst["rel"] += 1
        prefetch()

    def w_cols(w, c0, n=512):
        return w[:, c0:c0 + n].rearrange("(k p) c -> p k c", p=128)

    SV_NEGLAM, SV_QMAX, SV_KMAX, SV_NEGM, SV_NEGMC = 0, 1, 9, 17, 25
    SV_QXMAX, SV_KMMAX, SV_NEGMX = 33, 37, 41
    SV_STK, SV_STQ, SV_STX, SV_T = 48, 80, 96, 112

    def sv(c, n=1):
        return SV[:, c:c + n]

    def dma(q, out, in_, reads, writes, sem):
        return P.op(q, lambda e, out=out, in_=in_: e.dma_start(out=out, in_=in_), reads, writes, dma=sem)

    def dump(name, ap_sb, keys, rows=None):
        if name in dbg_out:
            dma("pool", dbg_out[name] if rows is None else dbg_out[name][rows], ap_sb, list(keys), ["dbg_" + name], "dbg")

    def finish():
        P.op("sp", None, ["dbg_" + n for n in dbg_out] + [("y", t) for t in range(8)], [])
        P.emit()
        nc.plan_out = plan_out
        return nc

    KT = P.alloc("KT", [128, 8, 2048], BF16, "Y")
    VA = P.alloc("VA", [128, 16, 4, 257], BF16, "Y")
    QT = P.alloc("QT", [128, 8, 1024], BF16, "Y")
    HTH = P.alloc("HTH", [128, 16, 32], BF16, "P")
    mY = P.mark("Y")
    P.op("dve", lambda e: e.memset(VA[:, :, :, 256:257], 1.0), [], ["VA1"])

    XTk = [P.alloc_at(f"XTk{i}", [128, D], F32, P.offs["KT"] + i * 8192)[:, :] for i in range(4)]
    va_own = P.offs["VA"] + 8 * 4 * 257 * 2
    va_own = (va_own + 63) // 64 * 64
    XTv = [P.alloc_at(f"XTv{i}", [128, D], F32, va_own + i * 8192)[:, :] for i in range(2)]
    assert va_own + 2 * 8192 <= P.offs["VA"] + 16 * 4 * 257 * 2
    dma("sp", CF[:, :], cf_d[:, :], [], ["CF"], "cf")
    for t in range(4):
        dma("sp", XTk[t][:, :], xc[t * 128:(t + 1) * 128, :], [], [("XT", t)], ("xt", t))
    P.op("dve", lambda e: e.tensor_copy(out=CB[:, :], in_=CF[:, 0:512]), ["CF"], ["CB"])
    prefetch()

    VROW = P.alloc("VROW", [40, 128], F32, "X")
    WROW0 = P.alloc("WROW0", [128, 128], F32, "X")
    WROW1 = P.alloc("WROW1", [120, 128], F32, "X")
    LAMB = P.alloc("LAMB", [128, 4, 128], F32, "X")
    LTMP = P.alloc("LTMP", [128, 2, 128], F32, "X")
    dma("sp", VROW[:, :], vecs[:, :], [], ["VROW"], ("misc", 1))
    dma("sp", WROW0[:, :], w_dw[0:128, :], [], ["WROW0"], ("misc", 2))
    dma("sp", WROW1[:, :], w_dw[128:248, :], [], ["WROW1"], ("misc", 3))
    dma("sp", LAMB[:, :, :], lams.partition_broadcast(128), [], ["LAMB"], ("misc", 4))
    dma("sp", GSUB[:, :], g_subln.partition_broadcast(128), [], ["GSUB"], ("misc", 5))
    P.op("pe", lambda e: e.matmul(bank(0, 40), lhsT=VROW[:, :], rhs=CF[0:40, CF_ID:CF_ID + 40], start=True, stop=True),
         ["VROW", "CF"], [PB(0)])
    P.op("pe", lambda e: e.matmul(bank(1, 128), lhsT=WROW0[:, :], rhs=ID_F, start=True, stop=True),
         ["WROW0", "CF"], [PB(1)])
    P.op("pe", lambda e: e.matmul(bank(2, 120), lhsT=WROW1[:, :], rhs=CF[0:120, CF_ID:CF_ID + 120], start=True, stop=True),
         ["WROW1", "CF"], [PB(2)])
    P.op("dve", lambda e: e.tensor_copy(out=VECT[:, :], in_=bank(0, 40)), [PB(0)], ["VECT"])
    P.op("dve", lambda e: e.tensor_copy(out=WDWT[:, 0:128], in_=bank(1, 128)), [PB(1)], ["WDWT"])
    P.op("dve", lambda e: e.tensor_copy(out=WDWT[:, 128:248], in_=bank(2, 120)), [PB(2)], ["WDWT"])
    P.op("dve", lambda e: e.tensor_scalar(out=GSUB[:, :], in0=GSUB[:, :], scalar1=float(1.0 - LAMBDA_INIT), scalar2=None,
                                          op0=ALU.mult), ["GSUB"], ["GSUB"])
    P.op("dve", lambda e: e.tensor_tensor(out=LTMP[:, 0, :], in0=LAMB[:, 0, :], in1=LAMB[:, 1, :], op=ALU.mult), ["LAMB"], ["LTMP0"])
    P.op("dve", lambda e: e.tensor_tensor(out=LTMP[:, 1, :], in0=LAMB[:, 2, :], in1=LAMB[:, 3, :], op=ALU.mult), ["LAMB"], ["LTMP1"])
    P.op("dve", lambda e: e.tensor_reduce(out=sv(SV_T, 2), in_=LTMP[:, :, :], axis=AX.X, op=ALU.add), ["LTMP0", "LTMP1"], ["SVT0"])
    P.op("act", lambda e: e.activation(out=sv(SV_T + 2, 2), in_=sv(SV_T, 2), func=AF.Exp), ["SVT0"], ["SVT2"])
    P.op("dve", lambda e: e.tensor_tensor(out=sv(SV_T + 4), in0=sv(SV_T + 3), in1=sv(SV_T + 2), op=ALU.subtract), ["SVT2"], ["SVT4"])
    P.op("dve", lambda e: e.tensor_scalar(out=sv(SV_NEGLAM), in0=sv(SV_T + 4), scalar1=float(-LAMBDA_INIT), scalar2=None,
                                          op0=ALU.add), ["SVT4"], ["NEGLAM"])

    HT = P.alloc("HT", [128, 16, 1024], BF16, "H")
    def norm_transpose(src_rows, ntiles, gvec, HTd, htkey, xt_alias=None, preloaded=0):
        GBC = P.alloc("GBC", [128, D], F32, "Y")
        HB = [P.alloc(f"HB{i}", [128, D], BF16, "Y") for i in range(2)]
        SS = P.alloc("SS", [128, 8], F32, "Y")
        if xt_alias is not None:
            XT = xt_alias
        else:
            XT = [P.alloc(f"XT{i}", [128, D], F32, "Y") for i in range(2)]
        dma("sp", GBC[:, :], gvec.partition_broadcast(128), [], ["GBC"], ("misc", 7))

        nx = len(XT)

        def s1(t):
            b = t % 2
            xb = t % nx
            if t >= preloaded:
                dma("sp", XT[xb][:, :], src_rows[t * 128:(t + 1) * 128, :], [], [("XT", xb)], ("xt", xb))
            P.op("act", lambda e: e.activation(out=HB[b][:, :], in_=XT[xb][:, :], func=AF.Square,
                                               accum_out=SS[:, b:b + 1]), [("XT", xb)], [("HB", b), ("SS", b)])
            P.op("act", lambda e: e.activation(out=SS[:, 4 + b:5 + b], in_=SS[:, b:b + 1], func=AF.Sqrt,
                                               scale=1.0 / D, bias=EPSB), [("SS", b), "CF"], [("SQ", b)])
            P.op("dve", lambda e: e.reciprocal(out=SS[:, 2 + b:3 + b], in_=SS[:, 4 + b:5 + b]), [("SQ", b)], [("RS", b)])
            P.op("dve", lambda e: e.scalar_tensor_tensor(out=HB[b][:, :], in0=XT[xb][:, :], scalar=SS[:, 2 + b:3 + b],
                                                         in1=GBC[:, :], op0=ALU.mult, op1=ALU.mult),
                 [("XT", xb), ("RS", b), "GBC"], [("HB", b)])

        def s2(t):
            b = t % 2
            for hh in range(2):
                pb = nxt("ntp", [0, 1, 2, 3])
                for j in range(8):
                    k = hh * 8 + j
                    P.op("pe", lambda e, k=k, j=j, pb=pb: e.transpose(
                        out=bank_bf(pb)[:, j * 128:(j + 1) * 128], in_=HB[b][:, k * 128:(k + 1) * 128], identity=ID_B),
                        [("HB", b), "CB"], [PB(pb)])
                if hh == 0:
                    P.op("act", lambda e, hh=hh, pb=pb: e.activation(
                        out=HTd[:, hh * 8:(hh + 1) * 8, t * 128:(t + 1) * 128],
                        in_=bank_bf(pb).rearrange("p (j t) -> p j t", j=8), func=AF.Copy),
                        [PB(pb)], [(htkey, t)])
                else:
                    P.op("dve", lambda e, hh=hh, pb=pb: e.tensor_copy(
                        out=HTd[:, hh * 8:(hh + 1) * 8, t * 128:(t + 1) * 128],
                        in_=bank_bf(pb).rearrange("p (j t) -> p j t", j=8)),
                        [PB(pb), (htkey, t)], [(htkey, t)])

        for t in range(ntiles):
            s1(t)
            if t >= 1:
                s2(t - 1)
            if t == 5 and wst["cap"] < 10 ** 6:
                wst["cap"] = 10 ** 6
                prefetch()
        s2(ntiles - 1)

    def rope_tmps():
        TB = [P.alloc(f"TB{i}", [128, 512], BF16, "Y") for i in range(3)]
        T1 = [P.alloc(f"T1{i}", [128, 512], F32, "Y") for i in range(3)]
        T2 = [P.alloc(f"T2{i}", [128, 512], F32, "Y") for i in range(3)]
        SQ = [P.alloc(f"SQ{i}", [128, 512], BF16, "Y") for i in range(3)]
        return TB, T1, T2, SQ

    def rope_evac(tm, pb, dest, destkey, tc, statcol):
        TB, T1, T2, SQ = tm
        r = nxt("rope", [0, 1, 2])

        def stage1():
            P.op("act", lambda e: e.activation(out=TB[r][:, :], in_=bank(pb), func=AF.Copy), [PB(pb)], [("TB", r)])
            pb2 = nxt("swb", [4, 5])
            P.op("pe", lambda e: e.matmul(bank(pb2), lhsT=SWAP_B, rhs=TB[r][:, :], start=True, stop=True),
                 [("TB", r), "CB"], [PB(pb2)])
            P.op("dve", lambda e: e.tensor_tensor(out=T1[r][:, :], in0=bank(pb), in1=COS[:, tc:tc + 512], op=ALU.mult),
                 [PB(pb), "COS%d" % (tc // NCTX)], [("T1", r)])
            P.op("dve", lambda e: e.tensor_tensor(out=T2[r][:, :], in0=bank(pb2), in1=SINS[:, tc:tc + 512], op=ALU.mult),
                 [PB(pb2), "SINS%d" % (tc // NCTX)], [("T2", r)])
            P.op("pool", lambda e: e.tensor_tensor(out=dest, in0=T1[r][:, :], in1=T2[r][:, :], op=ALU.add),
                 [("T1", r), ("T2", r)], [destkey])
            P.op("act", lambda e: e.activation(out=SQ[r][:, :], in_=dest, func=AF.Square), [destkey], [("SQ", r)])
            defer(1, stage2)

        def stage2():
            pb3 = nxt("stb", [6, 7])
            P.op("pe", lambda e: e.matmul(bank(pb3), lhsT=ONES_B, rhs=SQ[r][:, :], start=True, stop=True),
                 [("SQ", r), "CB"], [PB(pb3)])
            P.op("dve", lambda e: e.tensor_reduce(out=sv(statcol), in_=bank(pb3), axis=AX.X, op=ALU.max),
                 [PB(pb3)], [("SVc", statcol)])

        defer(1, stage1)

    rgen = [None]

    def kv_proj(ctx, tm):
        tbs = (0, 1) if ctx else (2, 3)
        def k_part():
            for s in range(2):
                i = wslab("cols", "w_in", C_KDA + s * 512, 16)
                for tb in tbs:
                  for jl in range(4):
                    if True:
                        j = 4 * s + jl
                        toff = (tb % 2) * 512
                        pb = nxt("kb", [0, 1, 2, 3])
                        for k in range(16):
                            P.op("pe", lambda e, i=i, k=k, jl=jl, toff=toff, pb=pb: e.matmul(
                                bank(pb), lhsT=WS[i][:, k, jl * 128:(jl + 1) * 128], rhs=HT[:, k, toff:toff + 512],
                                start=(k == 0), stop=(k == 15)),
                                [("WS", i)] + [("HT", toff // 128 + t) for t in range(4)], [PB(pb)])
                        tick()
                        rope_evac(tm, pb, KT[:, j, tb * 512:(tb + 1) * 512], ("KT", j, tb), tb * 512, SV_STK + j * 4 + tb)
                wrel()

        def v_part():
            for n in range(2):
                i = wslab("cols", "w_in", C_VDA + n * 512, 16)
                for tl in range(8):
                    tt = tl + (0 if ctx else 8)
                    pb = nxt("kb", [0, 1, 2, 3])
                    for k in range(16):
                        P.op("pe", lambda e, i=i, k=k, tl=tl, pb=pb: e.matmul(
                            bank(pb), lhsT=HT[:, k, tl * 128:(tl + 1) * 128], rhs=WS[i][:, k, :],
                            start=(k == 0), stop=(k == 15)), [("WS", i), ("HT", tl)], [PB(pb)])
                    tick()
                    P.op("act", lambda e, tt=tt, n=n, pb=pb: e.activation(
                        out=VA[:, tt, 2 * n:2 * n + 2, 0:256], in_=bank(pb).rearrange("p (h e) -> p h e", h=2), func=AF.Copy),
                        [PB(pb)], [("VA", tt, n)])
                    if rgen[0] is not None and tl % 3 == 2:
                        next(rgen[0], None)
                wrel()

        k_part()
        rgen[0] = rope_tables_gen(NCTX, NCTX + NTOK, 256, final_reset=False) if ctx else None
        v_part()
        if rgen[0] is not None:
            for _ in rgen[0]:
                pass
            rgen[0] = None

    P_tabs = {}

    def rope_tables(c_lo, c_hi, NH, final_reset=True):
        for _ in rope_tables_gen(c_lo, c_hi, NH, final_reset):
            pass
        return P_tabs["COS"], P_tabs["SINS"]

    def rope_tables_gen(c_lo, c_hi, NH, final_reset=True):
        if "COS" not in P_tabs:
            P.reset("X")
            P_tabs["COS"] = P.alloc("COS", [128, NCTX + NTOK], F32, "X")
            P_tabs["SINS"] = P.alloc("SINS", [128, NCTX + NTOK], F32, "X")
        COS, SINS = P_tabs["COS"], P_tabs["SINS"]
        m_in = P.mark("Y")
        POSI = P.alloc("POSI", [128, NH], I32, "Y")
        XA_ = P.alloc("XA", [128, NH], F32, "Y")
        TT_ = P.alloc("TT", [128, NH], F32, "Y")
        KI_ = P.alloc("KI", [128, NH], I32, "Y")
        C1 = float(np.float32(2 * PI))
        C2 = float(2 * PI - np.float64(np.float32(2 * PI)))
        for c0 in range(c_lo, c_hi, NH):
            cs = slice(c0, c0 + NH)
            dma("sp", POSI[:, :], posa[:, cs].partition_broadcast(128), [], ["POSI"], ("misc", 6))
            P.op("dve", lambda e: e.tensor_copy(out=TT_[:, :], in_=POSI[:, :]), ["POSI"], ["TT"])
            P.op("dve", lambda e: e.tensor_scalar(out=XA_[:, :], in0=TT_[:, :], scalar1=CF[:, CF_INVF:CF_INVF + 1], scalar2=None,
                                                  op0=ALU.mult), ["TT", "CF"], ["XA"])
            P.op("dve", lambda e: e.tensor_scalar(out=KI_[:, :], in0=XA_[:, :], scalar1=float(1.0 / (2 * PI)), scalar2=None,
                                                  op0=ALU.mult), ["XA"], ["KI"])
            P.op("dve", lambda e: e.tensor_copy(out=TT_[:, :], in_=KI_[:, :]), ["KI"], ["TT"])
            P.op("dve", lambda e: e.scalar_tensor_tensor(out=XA_[:, :], in0=TT_[:, :], scalar=-C1, in1=XA_[:, :],
                                                         op0=ALU.mult, op1=ALU.add), ["TT", "XA"], ["XA"])
            P.op("dve", lambda e: e.scalar_tensor_tensor(out=XA_[:, :], in0=TT_[:, :], scalar=-C2, in1=XA_[:, :],
                                                         op0=ALU.mult, op1=ALU.add), ["TT", "XA"], ["XA"])
            P.op("dve", lambda e: e.tensor_scalar(out=TT_[:, :], in0=XA_[:, :], scalar1=PI, scalar2=2 * PI,
                                                  op0=ALU.is_gt, op1=ALU.mult), ["XA"], ["TT"])
            P.op("dve", lambda e: e.tensor_tensor(out=XA_[:, :], in0=XA_[:, :], in1=TT_[:, :], op=ALU.subtract), ["XA", "TT"], ["XA"])
            P.op("dve", lambda e: e.tensor_scalar(out=TT_[:, :], in0=XA_[:, :], scalar1=-PI, scalar2=2 * PI,
                                                  op0=ALU.is_lt, op1=ALU.mult), ["XA"], ["TT"])
            P.op("dve", lambda e: e.tensor_tensor(out=XA_[:, :], in0=XA_[:, :], in1=TT_[:, :], op=ALU.add), ["XA", "TT"], ["XA"])
            P.op("act", lambda e, cs=cs: e.activation(out=SINS[:, cs], in_=XA_[:, :], func=AF.Sin, scale=CF[:, CF_SGN:CF_SGN + 1]),
                 ["XA", "CF"], ["SINS%d" % (c0 // NCTX)])
            P.op("dve", lambda e: e.tensor_scalar(out=TT_[:, :], in0=XA_[:, :], scalar1=PI / 2, scalar2=2 * PI,
                                                  op0=ALU.is_gt, op1=ALU.mult), ["XA", "SINS%d" % (c0 // NCTX)], ["TT"])
            P.op("dve", lambda e: e.scalar_tensor_tensor(out=XA_[:, :], in0=XA_[:, :], scalar=PI / 2, in1=TT_[:, :],
                                                         op0=ALU.add, op1=ALU.subtract), ["XA", "TT"], ["XA"])
            P.op("act", lambda e, cs=cs: e.activation(out=COS[:, cs], in_=XA_[:, :], func=AF.Sin), ["XA"], ["COS%d" % (c0 // NCTX)])
            yield
        if final_reset:
            P.reset("Y", m_in)
        else:
            P.zones["Y"][2] = m_in

    XTq = [QT[:, 0:4, :].rearrange("p a b -> p (a b)").bitcast(F32), QT[:, 4:8, :].rearrange("p a b -> p (a b)").bitcast(F32)]
    norm_transpose(xc, NCTX // 128, g_mix, HT, "HT", xt_alias=XTk, preloaded=4)
    for t in range(4):
        dma("sp", (XTq + XTv)[t][:, :], xo[t * 128:(t + 1) * 128, :], [], [("XT", t)], ("xt", t))
    P.reset("Y", mY)
    COS, SINS = rope_tables(0, NCTX, 1024)
    if stop == "A0":
        return finish()
    P.op("dve", lambda e: e.tensor_copy(out=HTH[:, :, :], in_=HT[:, :, 992:1024]), [("HT", 7)], ["HTH"])
    if "HTC" in dbg_out:
        for k in range(16):
            dump("HTC", HT[:, k, :], [("HT", t) for t in range(8)], rows=slice(k * 128, (k + 1) * 128))
    tm = rope_tmps()
    kv_proj(True, tm)
    flush()
    P.reset("Y", mY)
    norm_transpose(xo, NTOK // 128, g_mix, HT, "HT", xt_alias=XTq + XTv, preloaded=4)
    P.op("dve", lambda e: e.memset(VA[:, 8:16, :, 256:257], 1.0), [], [("XT", 2), ("XT", 3), "VA1"])
    P.reset("Y", mY)
    if "HTO" in dbg_out:
        for k in range(16):
            dump("HTO", HT[:, k, :], [("HT", t) for t in range(8)], rows=slice(k * 128, (k + 1) * 128))
    if stop == "A":
        return finish()
    tm = rope_tmps()
    kv_proj(False, tm)
    for s in range(2):
        i = wslab("cols", "w_in", C_QDA + s * 512, 16)
        for jl in range(4):
            j = 4 * s + jl
            for tbl in range(2):
                toff = tbl * 512
                pb = nxt("kb", [0, 1, 2, 3])
                for k in range(16):
                    P.op("pe", lambda e, i=i, k=k, jl=jl, toff=toff, pb=pb: e.matmul(
                        bank(pb), lhsT=WS[i][:, k, jl * 128:(jl + 1) * 128], rhs=HT[:, k, toff:toff + 512],
                        start=(k == 0), stop=(k == 15)),
                        [("WS", i)] + [("HT", toff // 128 + t) for t in range(4)], [PB(pb)])
                tick()
                rope_evac(tm, pb, QT[:, j, toff:toff + 512], ("QT", j, tbl), 1024 + toff, SV_STQ + j * 2 + tbl)
        wrel()
    flush()
    P.op("dve", lambda e: e.tensor_reduce(out=sv(SV_KMAX, 8), in_=sv(SV_STK, 32).rearrange("p (j t) -> p j t", t=4),
                                          axis=AX.X, op=ALU.max), [("SVc", SV_STK + c) for c in range(32)], ["KMAX"])
    P.op("dve", lambda e: e.tensor_reduce(out=sv(SV_QMAX, 8), in_=sv(SV_STQ, 16).rearrange("p (j t) -> p j t", t=2),
                                          axis=AX.X, op=ALU.max), [("SVc", SV_STQ + c) for c in range(16)], ["QMAX"])
    P.op("dve", lambda e: e.tensor_tensor(out=sv(SV_T, 8), in0=sv(SV_QMAX, 8), in1=sv(SV_KMAX, 8), op=ALU.mult),
         ["KMAX", "QMAX"], ["SVT0"])
    P.op("act", lambda e: e.activation(out=sv(SV_T + 8, 8), in_=sv(SV_T, 8), func=AF.Sqrt), ["SVT0"], ["SVT8"])
    P.op("dve", lambda e: e.tensor_scalar(out=sv(SV_NEGM, 8), in0=sv(SV_T + 8, 8), scalar1=float(-SCALE_DA), scalar2=None,
                                          op0=ALU.mult), ["SVT8"], ["NEGM"])
    P.op("dve", lambda e: e.tensor_scalar(out=sv(SV_NEGMC, 8), in0=sv(SV_NEGM, 8), scalar1=CF[:, CF_CBIAS:CF_CBIAS + 1],
                                          scalar2=None, op0=ALU.add), ["NEGM", "CF"], ["NEGMC"])
    P.reset("Y", mY)
    P.reset("X")
    if "KT" in dbg_out:
        for j in range(8):
            dump("KT", KT[:, j, :], [("KT", j, tb) for tb in range(4)], rows=slice(j * 128, (j + 1) * 128))
            dump("QT", QT[:, j, :], [("QT", j, tb) for tb in range(2)], rows=slice(j * 128, (j + 1) * 128))
        for tt in range(16):
            dump("VA", VA[:, tt, :, :].rearrange("p h e -> p (h e)"), [("VA", tt, 0), ("VA", tt, 1), "VA1"],
                 rows=slice(tt * 128, (tt + 1) * 128))
        dump("SVB", SV[:, :], ["NEGM", "NEGMC", "NEGLAM"])
    if stop == "B":
        return finish()

    OT_DA = P.alloc("OT_DA", [128, 8, 1024], BF16, "X")
    PTD = [P.alloc(f"PT{i}", [128, 512], BF16, "Y") for i in range(4)]
    O1 = P.alloc("O1", [128, 4, 257], F32, "Y")
    O2 = P.alloc("O2", [128, 4, 257], F32, "Y")
    OS = P.alloc("OS", [128, 4, 256], F32, "Y")
    ON = P.alloc("ON", [128, 4, 256], BF16, "Y")
    JK = P.alloc("JK", [128, 256], F32, "Y")
    RR = P.alloc("RR", [128, 32], F32, "Y")

    aotb_banks = [7]

    def attn_out_group(bufs, OTd, otkey, ch0, qcol0, subln, delay=4):
        O1g, O2g, OSg, ONg, RRg, JKg = bufs
        for qi in range(4):
            P.op("dve", lambda e, qi=qi: e.tensor_copy(out=O2g[:, qi, :], in_=bank(qi, 257)), [PB(qi)], [("O2", qi)])
        o2k = [("O2", qi) for qi in range(4)]
        def stage_d():
            for qi in range(4):
                P.op("dve", lambda e, qi=qi: e.scalar_tensor_tensor(out=ONg[:, qi, :], in0=OSg[:, qi, :], scalar=RRg[:, 20 + qi:21 + qi],
                                                                    in1=GSUB[:, :], op0=ALU.mult, op1=ALU.mult),
                     [("OS", qi), ("RR", 5), "GSUB"], [("ON", qi)])
            defer(2, part_b)

        def stage_c():
            for qi in range(4):
                P.op("dve", lambda e, qi=qi: e.scalar_tensor_tensor(out=JKg[:, :], in0=OSg[:, qi, :], scalar=1.0, in1=OSg[:, qi, :],
                                                                    op0=ALU.mult, op1=ALU.mult, accum_out=RRg[:, 12 + qi:13 + qi]),
                     [("OS", qi)], ["JK", ("RR", 3, qi)])
            defer(3, stage_c2)

        def stage_c2():
            P.op("act", lambda e: e.activation(out=RRg[:, 16:20], in_=RRg[:, 12:16], func=AF.Ln, scale=1.0 / 256, bias=EPSB),
                 [("RR", 3, qi) for qi in range(4)] + ["CF"], [("RR", 4)])
            P.op("act", lambda e: e.activation(out=RRg[:, 20:24], in_=RRg[:, 16:20], func=AF.Exp, scale=-0.5), [("RR", 4)], [("RR", 5)])
            defer(1, stage_d)

        def stage_b():
            o1k = [("O1", qi) for qi in range(4)]
            P.op("dve", lambda e: e.reciprocal(out=RRg[:, 0:4], in_=O1g[:, :, 256]), o1k, [("RR", 0)])
            P.op("dve", lambda e: e.reciprocal(out=RRg[:, 4:8], in_=O2g[:, :, 256]), o2k, [("RR", 1)])
            P.op("dve", lambda e: e.tensor_scalar(out=RRg[:, 8:12], in0=RRg[:, 4:8], scalar1=sv(SV_NEGLAM), scalar2=None, op0=ALU.mult),
                 [("RR", 1), "NEGLAM"], [("RR", 2)])
            for qi in range(4):
                P.op("dve", lambda e, qi=qi: e.tensor_scalar(out=OSg[:, qi, :], in0=O1g[:, qi, 0:256], scalar1=RRg[:, qi:qi + 1], scalar2=None,
                                                             op0=ALU.mult), [("O1", qi), ("RR", 0)], [("OS", qi)])
            for qi in range(4):
                P.op("dve", lambda e, qi=qi: e.scalar_tensor_tensor(out=OSg[:, qi, :], in0=O2g[:, qi, 0:256], scalar=RRg[:, 8 + qi:9 + qi],
                                                                    in1=OSg[:, qi, :], op0=ALU.mult, op1=ALU.add),
                     [("O2", qi), ("RR", 2), ("OS", qi)], [("OS", qi)])
            defer(2, stage_c)

        if subln:
            defer(1, stage_b)
        else:
            P.op("dve", lambda e: e.reciprocal(out=RRg[:, 0:4], in_=O2g[:, :, 256]), o2k, [("RR", 0)])
            for qi in range(4):
                P.op("dve", lambda e, qi=qi: e.tensor_scalar(out=ONg[:, qi, :], in0=O2g[:, qi, 0:256], scalar1=RRg[:, qi:qi + 1], scalar2=None,
                                                             op0=ALU.mult), [("O2", qi), ("RR", 0)], [("ON", qi)])

        def part_b():
            for qi in range(4):
                tb_ = nxt("aotb", aotb_banks)
                for ec in range(2):
                    P.op("pe", lambda e, ec=ec, qi=qi, tb_=tb_: e.transpose(out=bank_bf(tb_)[:, ec * 128:(ec + 1) * 128],
                                                                         in_=ONg[:, qi, ec * 128:(ec + 1) * 128], identity=ID_B),
                         [("ON", qi), "CB"], [PB(tb_)])
                qcol = qcol0 + qi * 128
                P.op("dve", lambda e, qcol=qcol, tb_=tb_: e.tensor_copy(out=OTd[:, ch0:ch0 + 2, qcol:qcol + 128],
                                                                      in_=bank_bf(tb_, 256).rearrange("p (c q) -> p c q", c=2)),
                     [PB(tb_)], [(otkey, ch0, qcol)])
        if not subln:
            defer(delay, part_b)

    def span_ap(key):
        return bank(key[1], 256)

    steps = []
    for hd in range(4):
        for qb in range(2):
            for mp in range(2):
                nkv = 8 + 4 * qb + 4
                for kt in range(nkv):
                    steps.append((hd, qb, mp, kt, kt == nkv - 1))
    stinfo = {}

    def emit_st(si):
        hd, qb, mp, kt, _ = steps[si]
        j = hd * 2 + mp
        if kt < 8:
            q_lo = 4 * qb
            bias = sv(SV_NEGMC + j)
            bkey = "NEGMC"
        else:
            q_lo = max(kt - 8, 4 * qb)
            bias = sv(SV_NEGM + j)
            bkey = "NEGM"
        q0 = q_lo * 128
        nq = (4 * qb + 4 - q_lo) * 128
        spb = nxt("spbd", [4, 5, 6])
        r = nxt("pt", [0, 1, 2, 3])
        P.op("pe", lambda e: e.matmul(bank(spb, nq), lhsT=KT[:, j, kt * 128:(kt + 1) * 128], rhs=QT[:, j, q0:q0 + nq],
                                      start=True, stop=True), [("KT", j, kt // 4), ("QT", j, qb)], [PB(spb)])
        P.op("act", lambda e: e.activation(out=PTD[r][:, 0:nq], in_=bank(spb, nq), func=AF.Exp, scale=float(SCALE_DA), bias=bias),
             [PB(spb), bkey], [("PT", r)])
        if kt >= 8 and (kt - 8) >= 4 * qb:
            P.op("dve", lambda e: e.tensor_tensor(out=PTD[r][:, 0:128], in0=PTD[r][:, 0:128], in1=TRI_B, op=ALU.mult),
                 [("PT", r), "CB"], [("PT", r)])
        stinfo[si] = (r, q_lo, nq)

    def emit_pv(si):
        hd, qb, mp, kt, last = steps[si]
        r, q_lo, nq = stinfo.pop(si)
        for t in range(nq // 128):
            qi = q_lo - 4 * qb + t
            P.op("pe", lambda e, t=t, qi=qi: e.matmul(
                bank(qi, 257), lhsT=PTD[r][:, t * 128:(t + 1) * 128], rhs=VA[:, kt, hd, :],
                start=(kt == 0), stop=(kt == 8 + 4 * qb + qi)),
                [("PT", r), ("VA", kt, hd // 2), "VA1"], [PB(qi)])
        if last:
            if mp == 0:
                for qi in range(4):
                    P.op("dve", lambda e, qi=qi: e.tensor_copy(out=O1[:, qi, :], in_=bank(qi, 257)),
                         [PB(qi)], [("O1", qi)])
            else:
                attn_out_group((O1, O2, OS, ON, RR, JK), OT_DA, "OTDA", hd * 2, 4 * qb * 128, True)

    emit_st(0)
    emit_st(1)
    for si in range(len(steps)):
        if si + 2 < len(steps):
            emit_st(si + 2)
        emit_pv(si)
        tick()
    flush()
    if "OTDA" in dbg_out:
        for c in range(8):
            dump("OTDA", OT_DA[:, c, :], [("OTDA", (c // 2) * 2, q * 128) for q in range(8)], rows=slice(c * 128, (c + 1) * 128))
    P.reset("Y")
    if stop == "C":
        return finish()

    CACT = P.alloc("CACT", [128, 8, 1024], BF16, "Y")
    QXT = P.alloc("QXT", [128, 8, 1024], BF16, "Y", top=True)
    mY2 = P.mark("Y")
    ACC = [P.alloc(f"ACC{c}", [128, 1024], F32, "Y") for c in range(8)]
    mY3 = P.mark("Y")
    U = [P.alloc(f"U{c}", [128, 1056], BF16, "Y") for c in range(8)]
    SGB = [P.alloc(f"SGB{i}", [128, 1056], BF16, "Y") for i in range(4)]
    DG = [P.alloc(f"DG{i}", [128, 31, 128], BF16, "Y") for i in range(2)]

    NPE = 31

    def dg_build(c):
        d = c % 2
        for jt in range(NPE):
            eng = "dve" if jt % 2 == 0 else "act"
            if eng == "dve":
                P.op("dve", lambda e, jt=jt: e.tensor_scalar(out=DG[d][:, jt, :], in0=ID_B, scalar1=WDWT[:, jt * 8 + c:jt * 8 + c + 1],
                                                             scalar2=None, op0=ALU.mult), ["CB", "WDWT"], [("DG", d, 0)])
            else:
                P.op("act", lambda e, jt=jt: e.activation(out=DG[d][:, jt, :], in_=ID_B, func=AF.Copy,
                                                          scale=WDWT[:, jt * 8 + c:jt * 8 + c + 1]), ["CB", "WDWT"], [("DG", d, 1)])

    def conv_chunk(c):
        d = c % 2
        pp = nxt("cvb", [4, 6])
        for half in range(2):
            for jt in range(NPE):
                P.op("pe", lambda e, jt=jt, half=half: e.matmul(
                    bank(pp + half), lhsT=DG[d][:, jt, :], rhs=U[c][:, 2 + jt + half * 512: 2 + jt + half * 512 + 512],
                    start=(jt == 0), stop=(jt == NPE - 1)), [("DG", d, 0), ("DG", d, 1), ("U", c)], [PB(pp + half)])
        P.op("act", lambda e: e.activation(out=ACC[c][:, :], in_=span(pp, 1024), func=AF.Identity, bias=VECT[:, c:c + 1]),
             [PB(pp), PB(pp + 1), "VECT"], [("ACC", c)])

    def conv_tail(c0, c1):
        for jt in range(NPE, 31):
            for c in (c0, c1):
                P.op("dve", lambda e, c=c, jt=jt: e.scalar_tensor_tensor(
                    out=ACC[c][:, :], in0=U[c][:, 2 + jt:2 + jt + 1024], scalar=WDWT[:, jt * 8 + c:jt * 8 + c + 1],
                    in1=ACC[c][:, :], op0=ALU.mult, op1=ALU.add), [("U", c), ("ACC", c), "WDWT"], [("ACC", c)])

    for cg in range(2):
        ib = wslab("cols", "w_in", C_GLU_B + cg * 512, 16)
        for cl in range(4):
            pbh = nxt("glu", [0, 1, 2, 3])
            for k in range(16):
                P.op("pe", lambda e, k=k, cl=cl, pbh=pbh, ib=ib: e.matmul(
                    bank(pbh, 32), lhsT=WS[ib][:, k, cl * 128:(cl + 1) * 128], rhs=HTH[:, k, :],
                    start=(k == 0), stop=(k == 15)), [("WS", ib), "HTH"], [PB(pbh)])
            P.op("act", lambda e, cl=cl, pbh=pbh: e.activation(out=SGB[cl][:, 0:32], in_=bank(pbh, 32), func=AF.Sigmoid),
                 [PB(pbh)], [("SGB", cl)])
            for half in range(2):
                pb_ = nxt("glu", [0, 1, 2, 3])
                for k in range(16):
                    P.op("pe", lambda e, k=k, cl=cl, half=half, pb_=pb_, ib=ib: e.matmul(
                        bank(pb_), lhsT=WS[ib][:, k, cl * 128:(cl + 1) * 128], rhs=HT[:, k, half * 512:(half + 1) * 512],
                        start=(k == 0), stop=(k == 15)), [("WS", ib)] + [("HT", half * 4 + t) for t in range(4)], [PB(pb_)])
                P.op("act", lambda e, cl=cl, half=half, pb_=pb_: e.activation(
                    out=SGB[cl][:, 32 + half * 512:32 + (half + 1) * 512], in_=bank(pb_), func=AF.Sigmoid), [PB(pb_)], [("SGB", cl)])
        wrel()
        ia = wslab("cols", "w_in", C_GLU_A + cg * 512, 16)
        for cl in range(4):
            c = cg * 4 + cl
            dg_build(c)
            pbh = nxt("glu", [0, 1, 2, 3])
            for k in range(16):
                P.op("pe", lambda e, k=k, cl=cl, pbh=pbh, ia=ia: e.matmul(
                    bank(pbh, 32), lhsT=WS[ia][:, k, cl * 128:(cl + 1) * 128], rhs=HTH[:, k, :],
                    start=(k == 0), stop=(k == 15)), [("WS", ia), "HTH"], [PB(pbh)])
            P.op("dve", lambda e, c=c, cl=cl, pbh=pbh: e.tensor_tensor(out=U[c][:, 0:32], in0=bank(pbh, 32), in1=SGB[cl][:, 0:32], op=ALU.mult),
                 [PB(pbh), ("SGB", cl)], [("U", c)])
            for half in range(2):
                pa = nxt("glu", [0, 1, 2, 3])
                for k in range(16):
                    P.op("pe", lambda e, k=k, cl=cl, half=half, pa=pa, ia=ia: e.matmul(
                        bank(pa), lhsT=WS[ia][:, k, cl * 128:(cl + 1) * 128], rhs=HT[:, k, half * 512:(half + 1) * 512],
                        start=(k == 0), stop=(k == 15)), [("WS", ia)] + [("HT", half * 4 + t) for t in range(4)], [PB(pa)])
                P.op("dve", lambda e, c=c, cl=cl, half=half, pa=pa: e.tensor_tensor(
                    out=U[c][:, 32 + half * 512:32 + (half + 1) * 512], in0=bank(pa),
                    in1=SGB[cl][:, 32 + half * 512:32 + (half + 1) * 512], op=ALU.mult), [PB(pa), ("SGB", cl)], [("U", c)])
            if cl >= 1:
                conv_chunk(c - 1)
            if cl == 2:
                conv_tail(c - 2, c - 1)
        wrel()
        conv_chunk(cg * 4 + 3)
        conv_tail(cg * 4 + 2, cg * 4 + 3)
    if "ACC" in dbg_out:
        for c in range(8):
            dump("ACC", ACC[c][:, :], [("ACC", c)], rows=slice(c * 128, (c + 1) * 128))
    P.reset("Y", mY3)
    AB = [P.alloc(f"AB{i}", [128, 1024], BF16, "Y") for i in range(2)]
    SQL = [P.alloc(f"SQL{i}", [128, 1024], BF16, "Y") for i in range(2)]
    MEAN = P.alloc("MEAN", [128, 1024], F32, "Y")
    RSTD = P.alloc("RSTD", [128, 1024], F32, "Y")
    TL = [P.alloc(f"TL{i}", [128, 1024], F32, "Y") for i in range(2)]
    xtm_off = P.zones["Y"][0] + 72 * 1024
    assert P.mark("Y") <= xtm_off and xtm_off + 16384 <= P.zones["Y"][3]
    XTm = [P.alloc_at(f"XTm{i}", [128, D], F32, xtm_off + i * 8192)[:, :] for i in range(2)]
    for t in range(2):
        dma("sp", XTm[t][:, :], memx[t * 128:(t + 1) * 128, :], [], [("XT", t)], ("xt", t))
    def qxa_blocks():
        for s_ in range(2):
            i = wslab("cols", "w_in", C_QXA + s_ * 512, 16)
            for jl in range(4):
                j = 4 * s_ + jl
                for half in range(2):
                    pb = nxt("qxb", [4, 5, 6, 7])
                    for k in range(16):
                        P.op("pe", lambda e, i=i, k=k, jl=jl, half=half, pb=pb: e.matmul(
                            bank(pb), lhsT=WS[i][:, k, jl * 128:(jl + 1) * 128], rhs=HT[:, k, half * 512:(half + 1) * 512],
                            start=(k == 0), stop=(k == 15)), [("WS", i)] + [("HT", half * 4 + t) for t in range(4)], [PB(pb)])
                    P.op("act", lambda e, j=j, half=half, pb=pb: e.activation(out=QXT[:, j, half * 512:(half + 1) * 512], in_=bank(pb),
                                                                             func=AF.Copy), [PB(pb)], [("QXT", j, half)])
                    yield
            wrel()
    qgen = qxa_blocks()

    def qstep():
        next(qgen, None)

    for c in range(8):
        r = c % 2
        P.op("act", lambda e, c=c, r=r: e.activation(out=AB[r][:, :], in_=ACC[c][:, :], func=AF.Copy), [("ACC", c)], [("AB", r)])
        P.op("act", lambda e, c=c, r=r: e.activation(out=SQL[r][:, :], in_=ACC[c][:, :], func=AF.Square), [("ACC", c)], [("SQL", r)])
        qstep()
        for half in range(2):
            P.op("pe", lambda e, c=c, r=r, half=half: e.matmul(bank(half), lhsT=ONES_B, rhs=AB[r][:, half * 512:(half + 1) * 512],
                                                              start=(c == 0), stop=(c == 7)), [("AB", r), "CB"], [PB(half)])
            P.op("pe", lambda e, c=c, r=r, half=half: e.matmul(bank(2 + half), lhsT=ONES_B, rhs=SQL[r][:, half * 512:(half + 1) * 512],
                                                              start=(c == 0), stop=(c == 7)), [("SQL", r), "CB"], [PB(2 + half)])
    P.op("act", lambda e: e.activation(out=MEAN[:, :], in_=span(0, 1024), func=AF.Copy, scale=1.0 / 1024), [PB(0), PB(1)], ["MEAN"])
    P.op("pool", lambda e: e.tensor_tensor(out=RSTD[:, :], in0=MEAN[:, :], in1=MEAN[:, :], op=ALU.mult), ["MEAN"], ["RSTD"])
    P.op("dve", lambda e: e.scalar_tensor_tensor(out=RSTD[:, :], in0=span(2, 1024), scalar=1.0 / 1024, in1=RSTD[:, :],
                                                 op0=ALU.mult, op1=ALU.subtract), [PB(2), PB(3), "RSTD"], ["RSTD"])
    P.op("act", lambda e: e.activation(out=RSTD[:, :], in_=RSTD[:, :], func=AF.Ln, bias=EPSB), ["RSTD", "CF"], ["RSTD"])
    P.op("act", lambda e: e.activation(out=RSTD[:, :], in_=RSTD[:, :], func=AF.Exp, scale=-0.5), ["RSTD"], ["RSTD"])

    def ln_apply(c):
        r = c % 2
        if c % 2 == 0:
            P.op("pool", lambda e: e.tensor_tensor(out=TL[r][:, :], in0=ACC[c][:, :], in1=MEAN[:, :], op=ALU.subtract),
                 [("ACC", c), "MEAN"], [("TL", r)])
        else:
            P.op("dve", lambda e: e.tensor_tensor(out=TL[r][:, :], in0=ACC[c][:, :], in1=MEAN[:, :], op=ALU.subtract),
                 [("ACC", c), "MEAN"], [("TL", r)])
        P.op("dve", lambda e: e.tensor_tensor(out=TL[r][:, :], in0=TL[r][:, :], in1=RSTD[:, :], op=ALU.mult),
             [("TL", r), "RSTD"], [("TL", r)])
        P.op("act", lambda e: e.activation(out=CACT[:, c, :], in_=TL[r][:, :], func=AF.Silu,
                                           scale=VECT[:, 8 + c:9 + c], bias=VECT[:, 16 + c:17 + c]),
             [("TL", r), "VECT"], [("CACT", c)])

    for c in range(8):
        qstep()
        ln_apply(c)
    for _ in range(16):
        qstep()
    if "CACT" in dbg_out:
        for c in range(8):
            dump("CACT", CACT[:, c, :], [("CACT", c)], rows=slice(c * 128, (c + 1) * 128))
    P.reset("Y", mY2)
    if stop == "D":
        return finish()

    OT_XA = P.alloc("OT_XA", [128, 8, 1024], BF16, "Y")
    mY4 = P.mark("Y")
    MEMT = P.alloc("MEMT", [128, 16, 256], BF16, "Y")
    KMT = P.alloc("KMT", [128, 8, 256], BF16, "Y")
    VMA = P.alloc("VMA", [128, 2, 4, 257], BF16, "Y")
    mY5 = P.mark("Y")
    P.op("dve", lambda e: e.memset(VMA[:, :, :, 256:257], 1.0), [], ["VMA1"])
    norm_transpose(memx, 2, g_mem, MEMT, "MEMT", xt_alias=XTm, preloaded=2)
    assert P.mark("Y") <= xtm_off
    P.reset("Y", mY5)
    aotb_banks[:] = [6, 7]
    SQX = [P.alloc(f"SQX{i}", [128, 512], BF16, "Y") for i in range(4)]
    PTX = [P.alloc(f"PTx{i}", [128, 512], BF16, "Y") for i in range(4)]
    ONX = P.alloc("ONx", [128, 4, 256], BF16, "Y")
    O2X = P.alloc("O2x", [128, 4, 257], F32, "Y")
    RRX = P.alloc("RRx", [128, 32], F32, "Y")
    assert P.mark("Y") <= xtm_off
    for h in range(4):
        for half in range(2):
            pb3 = nxt("stb", [6, 7])
            for cc in range(2):
                r = nxt("sqx", [0, 1, 2, 3])
                P.op("act", lambda e, h=h, half=half, cc=cc, r=r: e.activation(
                    out=SQX[r][:, :], in_=QXT[:, 2 * h + cc, half * 512:(half + 1) * 512], func=AF.Square),
                    [("QXT", 2 * h + cc, half)], [("SQX", r)])
                P.op("pe", lambda e, r=r, cc=cc, pb3=pb3: e.matmul(bank(pb3), lhsT=ONES_B, rhs=SQX[r][:, :], start=(cc == 0), stop=(cc == 1)),
                     [("SQX", r), "CB"], [PB(pb3)])
            P.op("dve", lambda e, h=h, half=half, pb3=pb3: e.tensor_reduce(out=sv(SV_STX + h * 2 + half), in_=bank(pb3), axis=AX.X, op=ALU.max),
                 [PB(pb3)], [("SVc", SV_STX + h * 2 + half)])
    P.op("dve", lambda e: e.tensor_reduce(out=sv(SV_QXMAX, 4), in_=sv(SV_STX, 8).rearrange("p (j t) -> p j t", t=2),
                                          axis=AX.X, op=ALU.max), [("SVc", SV_STX + c) for c in range(8)], ["QXMAX"])
    for hp in range(2):
        i = wslab("cols", "w_mem_kv", hp * 512, 16)
        for jl in range(4):
            j = 4 * hp + jl
            pb = nxt("kb", [0, 1, 2, 3])
            for k in range(16):
                P.op("pe", lambda e, i=i, k=k, jl=jl, pb=pb: e.matmul(
                    bank(pb, 256), lhsT=WS[i][:, k, jl * 128:(jl + 1) * 128], rhs=MEMT[:, k, :], start=(k == 0), stop=(k == 15)),
                    [("WS", i), ("MEMT", 0), ("MEMT", 1)], [PB(pb)])
            P.op("act", lambda e, j=j, pb=pb: e.activation(out=KMT[:, j, :], in_=bank(pb, 256), func=AF.Copy), [PB(pb)], [("KMT", j)])
        wrel()
        i = wslab("cols", "w_mem_kv", 1024 + hp * 512, 16)
        for mt in range(2):
            pb = nxt("kb", [0, 1, 2, 3])
            for k in range(16):
                P.op("pe", lambda e, i=i, k=k, mt=mt, pb=pb: e.matmul(
                    bank(pb), lhsT=MEMT[:, k, mt * 128:(mt + 1) * 128], rhs=WS[i][:, k, :], start=(k == 0), stop=(k == 15)),
                    [("WS", i), ("MEMT", mt)], [PB(pb)])
            P.op("act", lambda e, mt=mt, hp=hp, pb=pb: e.activation(
                out=VMA[:, mt, 2 * hp:2 * hp + 2, 0:256], in_=bank(pb).rearrange("p (h e) -> p h e", h=2), func=AF.Copy),
                [PB(pb)], [("VMA", mt, hp)])
        wrel()
        for h in (2 * hp, 2 * hp + 1):
            pb3 = nxt("stb", [6, 7])
            for cc in range(2):
                r = nxt("sqx", [0, 1, 2, 3])
                P.op("act", lambda e, h=h, cc=cc, r=r: e.activation(out=SQX[r][:, 0:256], in_=KMT[:, 2 * h + cc, :], func=AF.Square),
                     [("KMT", 2 * h + cc)], [("SQX", r)])
                P.op("pe", lambda e, r=r, cc=cc, pb3=pb3: e.matmul(bank(pb3, 256), lhsT=ONES_B, rhs=SQX[r][:, 0:256], start=(cc == 0), stop=(cc == 1)),
                     [("SQX", r), "CB"], [PB(pb3)])
            P.op("dve", lambda e, h=h, pb3=pb3: e.tensor_reduce(out=sv(SV_KMMAX + h), in_=bank(pb3, 256), axis=AX.X, op=ALU.max),
                 [PB(pb3)], [("SVk", h)])
        h0 = 2 * hp
        P.op("dve", lambda e, h0=h0: e.tensor_tensor(out=sv(SV_T + h0, 2), in0=sv(SV_QXMAX + h0, 2), in1=sv(SV_KMMAX + h0, 2), op=ALU.mult),
             ["QXMAX", ("SVk", h0), ("SVk", h0 + 1)], [("SVT0x", hp)])
        P.op("act", lambda e, h0=h0: e.activation(out=sv(SV_T + 8 + h0, 2), in_=sv(SV_T + h0, 2), func=AF.Sqrt), [("SVT0x", hp)], [("SVT8x", hp)])
        P.op("dve", lambda e, h0=h0: e.tensor_scalar(out=sv(SV_NEGMX + h0, 2), in0=sv(SV_T + 8 + h0, 2), scalar1=float(-SCALE_XA), scalar2=None,
                                                     op0=ALU.mult), [("SVT8x", hp)], [("NEGMX", hp)])
        xsteps = [(h, qb, mt) for h in (2 * hp, 2 * hp + 1) for qb in range(2) for mt in range(2)]
        xinfo = {}

        def x_st(si):
            h, qb, mt = xsteps[si]
            spb = nxt("spbx", [4, 5])
            r = nxt("pt", [0, 1, 2, 3])
            for cc in range(2):
                P.op("pe", lambda e, cc=cc: e.matmul(
                    bank(spb), lhsT=KMT[:, 2 * h + cc, mt * 128:(mt + 1) * 128], rhs=QXT[:, 2 * h + cc, qb * 512:(qb + 1) * 512],
                    start=(cc == 0), stop=(cc == 1)), [("KMT", 2 * h + cc), ("QXT", 2 * h + cc, qb)], [PB(spb)])
            P.op("act", lambda e: e.activation(out=PTX[r][:, :], in_=bank(spb), func=AF.Exp,
                                               scale=float(SCALE_XA), bias=sv(SV_NEGMX + h)),
                 [PB(spb), ("NEGMX", hp)], [("PT", r)])
            xinfo[si] = r

        def x_pv(si):
            h, qb, mt = xsteps[si]
            r = xinfo.pop(si)
            for qi in range(4):
                P.op("pe", lambda e, qi=qi: e.matmul(
                    bank(qi, 257), lhsT=PTX[r][:, qi * 128:(qi + 1) * 128], rhs=VMA[:, mt, h, :], start=(mt == 0), stop=(mt == 1)),
                    [("PT", r), ("VMA", mt, h // 2), "VMA1"], [PB(qi)])
            if mt == 1:
                attn_out_group((None, O2X, None, ONX, RRX, None), OT_XA, "OTXA", h * 2, 4 * qb * 128, False, delay=2)

        x_st(0)
        for si in range(len(xsteps)):
            if si + 1 < len(xsteps):
                x_st(si + 1)
            x_pv(si)
            tick()
        flush()
    flush()
    if "OTXA" in dbg_out:
        for c in range(8):
            dump("OTXA", OT_XA[:, c, :], [("OTXA", (c // 2) * 2, q * 128) for q in range(8)], rows=slice(c * 128, (c + 1) * 128))
    P.reset("Y", mY4, top=True)
    if stop == "E":
        return finish()

    MERGED = P.alloc("MERGED", [128, 16, 1024], BF16, "Y", top=True)
    MG = [P.alloc(f"MG{i}", [128, 1024], F32, "Y") for i in range(4)]
    SIGB = [[P.alloc(f"SIGB{a}{i}", [128, 1024], BF16, "Y") for i in range(4)] for a in range(2)]
    TMPm = [P.alloc(f"TMPm{i}", [128, 512], F32, "Y") for i in range(2)]
    branches = [("w_conv_out", CACT, "CACT"), ("w_da_out", OT_DA, "OTDA"), ("w_xa_out", OT_XA, "OTXA")]

    def act_keys(r, k, half):
        if r == 0:
            return [("CACT", k)]
        nm = "OTDA" if r == 1 else "OTXA"
        return [(nm, (k // 2) * 2, (half * 4 + t) * 128) for t in range(4)]

    for cg in range(4):
        for r in range(3):
            wsrc, ACTr, _ = branches[r]
            sa = nxt("sigb", [0, 1])
            ig = wslab("cols", "w_in", C_GATE + r * 2048 + cg * 512, 16)
            for cl in range(4):
                for half in range(2):
                    pg = nxt("mgg", [0, 1, 2, 3])
                    hs = slice(half * 512, (half + 1) * 512)
                    for k in range(16):
                        P.op("pe", lambda e, ig=ig, k=k, cl=cl, hs=hs, pg=pg: e.matmul(
                            bank(pg), lhsT=WS[ig][:, k, cl * 128:(cl + 1) * 128], rhs=HT[:, k, hs],
                            start=(k == 0), stop=(k == 15)), [("WS", ig)] + [("HT", half * 4 + t) for t in range(4)], [PB(pg)])
                    P.op("act", lambda e, sa=sa, cl=cl, hs=hs, pg=pg: e.activation(out=SIGB[sa][cl][:, hs], in_=bank(pg), func=AF.Sigmoid),
                         [PB(pg)], [("SIGB", sa, cl, half)])
            wrel()
            io = wslab("cols", wsrc, cg * 512, 8)
            for cl in range(4):
                c = cg * 4 + cl
                for half in range(2):
                    py = nxt("mgy", [4, 5, 6, 7])
                    hs = slice(half * 512, (half + 1) * 512)
                    for k in range(8):
                        P.op("pe", lambda e, io=io, k=k, cl=cl, hs=hs, py=py, ACTr=ACTr: e.matmul(
                            bank(py), lhsT=WS[io][:, k, cl * 128:(cl + 1) * 128], rhs=ACTr[:, k, hs],
                            start=(k == 0), stop=(k == 7)), [("WS", io)] + act_keys(r, k, half), [PB(py)])
                    if r == 0:
                        P.op("dve", lambda e, sa=sa, py=py, cl=cl, hs=hs: e.tensor_tensor(out=MG[cl][:, hs], in0=bank(py), in1=SIGB[sa][cl][:, hs], op=ALU.mult),
                             [PB(py), ("SIGB", sa, cl, half)], [("MG", cl, half)])
                    else:
                        q = nxt("tmpm", [0, 1])
                        P.op("dve", lambda e, sa=sa, q=q, py=py, cl=cl, hs=hs: e.tensor_tensor(out=TMPm[q][:, :], in0=bank(py), in1=SIGB[sa][cl][:, hs], op=ALU.mult),
                             [PB(py), ("SIGB", sa, cl, half)], [("TMPm", q)])
                        if r == 1:
                            P.op("pool", lambda e, q=q, cl=cl, hs=hs: e.tensor_tensor(out=MG[cl][:, hs], in0=MG[cl][:, hs], in1=TMPm[q][:, :], op=ALU.add),
                                 [("MG", cl, half), ("TMPm", q)], [("MG", cl, half)])
                        else:
                            P.op("pool", lambda e, q=q, cl=cl, hs=hs, c=c: e.tensor_tensor(out=MERGED[:, c, hs], in0=MG[cl][:, hs], in1=TMPm[q][:, :], op=ALU.add),
                                 [("MG", cl, half), ("TMPm", q)], [("MERGED", c, half)])
            wrel()
    if "MERGED" in dbg_out:
        for c in range(16):
            dump("MERGED", MERGED[:, c, :], [("MERGED", c, 0), ("MERGED", c, 1)], rows=slice(c * 128, (c + 1) * 128))
    P.reset("Y")
    P.reset("H")
    P.reset("X")
    if stop == "F":
        return finish()

    X1T = ([P.alloc(f"X1T{c}", [128, 1024], F32, "H") for c in range(8)]
           + [P.alloc(f"X1T{c}", [128, 1024], F32, "X") for c in range(8, 12)]
           + [P.alloc(f"X1T{c}", [128, 1024], F32, "Y") for c in range(12, 16)])
    H2T = P.alloc("H2T", [128, 16, 1024], BF16, "Y")
    RS2 = P.alloc("RS2", [128, 1024], F32, "Y")
    mY6 = P.mark("Y")
    XR = [P.alloc(f"XR{i}", [128, 8, 128], F32, "Y") for i in range(2)]
    SQm = [P.alloc(f"SQm{i}", [128, 1024], BF16, "Y") for i in range(2)]

    def ssq_mm(c, q):
        for half in range(2):
            P.op("pe", lambda e, half=half: e.matmul(bank(6 + half), lhsT=ONES_B, rhs=SQm[q][:, half * 512:(half + 1) * 512],
                                                     start=(c == 0), stop=(c == 15)), [("SQm", q), "CB"], [PB(6 + half)])

    for cg in range(4):
        i = wslab("cols", "w_mix_out", cg * 512, 16)
        for cl in range(4):
            c = cg * 4 + cl
            q = c % 2
            dma("sp", XR[q][:, :, :], xo[:, c * 128:(c + 1) * 128].rearrange("(t p) c -> p t c", p=128), [], [("XR", q)], ("xr", q))
            pp = nxt("mix", [0, 2, 4])
            for half in range(2):
                pb = pp + half
                for k in range(16):
                    P.op("pe", lambda e, i=i, k=k, cl=cl, half=half, pb=pb: e.matmul(
                        bank(pb), lhsT=WS[i][:, k, cl * 128:(cl + 1) * 128], rhs=MERGED[:, k, half * 512:(half + 1) * 512],
                        start=(k == 0), stop=False), [("WS", i), ("MERGED", k, half)], [PB(pb)])
                for tl in range(4):
                    P.op("pe", lambda e, q=q, half=half, tl=tl, pb=pb: e.matmul(
                        bank(pb, 128, tl * 128), lhsT=XR[q][:, half * 4 + tl, :], rhs=ID_F, start=False, stop=(tl == 3)),
                        [("XR", q), "CF"], [PB(pb)])
            tick()
            P.op("act", lambda e, c=c, pp=pp: e.activation(out=X1T[c][:, :], in_=span(pp, 1024), func=AF.Copy), [PB(pp), PB(pp + 1)], [("X1T", c)])
            P.op("act", lambda e, c=c, pp=pp: e.activation(out=H2T[:, c, :], in_=span(pp, 1024), func=AF.Copy, scale=VECT[:, 24 + c:25 + c]),
                 [PB(pp), PB(pp + 1), "VECT"], [("H2T", c)])
            P.op("act", lambda e, q=q, pp=pp: e.activation(out=SQm[q][:, :], in_=span(pp, 1024), func=AF.Square), [PB(pp), PB(pp + 1)], [("SQm", q)])
            defer(1, lambda c=c, q=q: ssq_mm(c, q))
        wrel()
    flush()
    P.op("act", lambda e: e.activation(out=RS2[:, :], in_=span(6, 1024), func=AF.Ln, scale=1.0 / D, bias=EPSB), [PB(6), PB(7), "CF"], ["RS2"])
    P.op("act", lambda e: e.activation(out=RS2[:, :], in_=RS2[:, :], func=AF.Exp, scale=-0.5), ["RS2"], ["RS2"])
    if "X1T" in dbg_out:
        for c in range(16):
            dump("X1T", X1T[c][:, :], [("X1T", c)], rows=slice(c * 128, (c + 1) * 128))
            dump("H2T", H2T[:, c, :], [("H2T", c)], rows=slice(c * 128, (c + 1) * 128))
    P.reset("Y", mY6, top=True)
    if stop == "G":
        return finish()

    ACTG = [P.alloc(f"ACTG{i}", [128, 4, 1024], BF16, "Y") for i in range(2)]
    RL = [P.alloc(f"RL{i}", [128, 1024], BF16, "Y") for i in range(2)]
    TRL = [P.alloc(f"TRL{i}", [128, 1024], BF16, "Y") for i in range(2)]

    def ffn_up(g):
        iu = wslab("cols", "w_up", g * 512, 16)
        for f in range(4):
            pp = nxt("up", [0, 2])
            for half in range(2):
                for k in range(16):
                    P.op("pe", lambda e, iu=iu, k=k, f=f, half=half, pp=pp: e.matmul(
                        bank(pp + half), lhsT=WS[iu][:, k, f * 128:(f + 1) * 128], rhs=H2T[:, k, half * 512:(half + 1) * 512],
                        start=(k == 0), stop=(k == 15)), [("WS", iu), ("H2T", k)], [PB(pp + half)])
            q = nxt("rl", [0, 1])
            P.op("act", lambda e, q=q, pp=pp: e.activation(out=RL[q][:, :], in_=span(pp, 1024), func=AF.Relu), [PB(pp), PB(pp + 1)], [("RL", q)])
            P.op("dve", lambda e, q=q: e.tensor_tensor(out=TRL[q][:, :], in0=RL[q][:, :], in1=RS2[:, :], op=ALU.mult),
                 [("RL", q), "RS2"], [("TRL", q)])
            P.op("dve", lambda e, q=q, g=g, f=f: e.tensor_tensor(out=ACTG[g % 2][:, f, :], in0=TRL[q][:, :], in1=TRL[q][:, :], op=ALU.mult),
                 [("TRL", q)], [("ACTG", g % 2, f)])
        wrel()

    def ffn_down(g):
        idn = wslab("rows", g)
        WD = wd_view(idn)
        for c in range(16):
            pp = nxt("dn", [4, 6])
            for half in range(2):
                for f in range(4):
                    P.op("pe", lambda e, WD=WD, f=f, c=c, half=half, pp=pp, g=g: e.matmul(
                        bank(pp + half), lhsT=WD[:, f, c * 128:(c + 1) * 128], rhs=ACTG[g % 2][:, f, half * 512:(half + 1) * 512],
                        start=(f == 0), stop=(f == 3)), [("WS", idn), ("ACTG", g % 2, f)], [PB(pp + half)])
            P.op("dve", lambda e, c=c, pp=pp: e.tensor_tensor(out=X1T[c][:, :], in0=span(pp, 1024), in1=X1T[c][:, :], op=ALU.add),
                 [PB(pp), PB(pp + 1), ("X1T", c)], [("X1T", c)])
        wrel()

    NG = DFF // 512
    ffn_up(0)
    for g in range(NG):
        if g + 1 < NG:
            ffn_up(g + 1)
        ffn_down(g)
    if "X2T" in dbg_out:
        for c in range(16):
            dump("X2T", X1T[c][:, :], [("X1T", c)], rows=slice(c * 128, (c + 1) * 128))
    P.reset("Y", mY6)

    GF = P.alloc("GF", [128, D], F32, "Y")
    YT = [P.alloc(f"YT{i}", [128, D], F32, "Y") for i in range(2)]
    JKF = P.alloc("JKF", [128, D], BF16, "Y")
    SF = P.alloc("SF", [128, 8], F32, "Y")
    dma("sp", GF[:, :], g_final.partition_broadcast(128), [], ["GF"], ("misc", 8))
    for t in range(8):
        st = t % 2
        for c in range(16):
            P.op("pe", lambda e, c=c, t=t, st=st: e.transpose(out=PS[:, st * 2048 + c * 128: st * 2048 + (c + 1) * 128],
                                                             in_=X1T[c][:, t * 128:(t + 1) * 128], identity=ID_F),
                 [("X1T", c), "CF"], [PB(st * 4 + c // 4)])
        pkeys = [PB(st * 4 + b) for b in range(4)]
        P.op("act", lambda e, st=st: e.activation(out=JKF[:, :], in_=PS[:, st * 2048:(st + 1) * 2048], func=AF.Square,
                                                  accum_out=SF[:, st:st + 1]), pkeys, ["JKF", ("SF", st)])
        P.op("act", lambda e, st=st: e.activation(out=SF[:, 2 + st:3 + st], in_=SF[:, st:st + 1], func=AF.Sqrt, scale=1.0 / D, bias=EPSB),
             [("SF", st), "CF"], [("SF2", st)])
        P.op("dve", lambda e, st=st: e.reciprocal(out=SF[:, 4 + st:5 + st], in_=SF[:, 2 + st:3 + st]), [("SF2", st)], [("SF4", st)])
        P.op("dve", lambda e, st=st: e.scalar_tensor_tensor(out=YT[st][:, :], in0=PS[:, st * 2048:(st + 1) * 2048], scalar=SF[:, 4 + st:5 + st],
                                                            in1=GF[:, :], op0=ALU.mult, op1=ALU.mult), pkeys + [("SF4", st), "GF"], [("YT", st)])
        dma("sp", y[t * 128:(t + 1) * 128, :], YT[st][:, :], [("YT", st)], [("y", t)], ("yo", st))
    return finish()


def make_consts(half):
    cf = np.zeros((128, NCF), np.float32)
    cf[:, CF_ID:CF_ID + 128] = np.eye(128, dtype=np.float32)
    cf[:, CF_ONES:CF_ONES + 128] = 1.0
    k = np.arange(128)[:, None]
    q = np.arange(128)[None, :]
    cf[:, CF_TRI:CF_TRI + 128] = (q >= k).astype(np.float32)
    cf[:, CF_SWAP:CF_SWAP + 128] = (k == (q + 64) % 128).astype(np.float32)
    inv_freq = (1.0 / (np.float32(10000.0) ** (np.arange(0, 128, 2, dtype=np.float32) / np.float32(128)))).astype(np.float32)
    cf[:, CF_INVF] = np.concatenate([inv_freq, inv_freq])
    cf[:64, CF_SGN] = -1.0
    cf[64:, CF_SGN] = 1.0
    cf[:, CF_CBIAS] = 0.0 if half == 1 else -30000.0
    cf[:, CF_EPS] = EPS
    return cf


def make_in_maps(inp, cores=range(8)):
    x = np.asarray(inp["x"], np.float32)
    mem = np.asarray(inp["mem"], np.float32)
    pos = np.asarray(inp["positions"], np.int32)
    f = lambda k: np.ascontiguousarray(np.asarray(inp[k], np.float32))
    shared = {
        "g_mix": f("g_mix").reshape(1, D),
        "w_in": f("w_in").reshape(D, DIN),
        "w_dw": f("w_dw").reshape(31 * 8, 128),
        "vecs": np.concatenate([f("b_dw").reshape(8, 128), f("g_conv_ln").reshape(8, 128),
                                f("b_conv_ln").reshape(8, 128), f("g_mlp").reshape(16, 128)], axis=0),
        "w_conv_out": f("w_conv_out").reshape(1024, D),
        "lams": np.concatenate([f("lambda_q1").reshape(1, 128), f("lambda_k1").reshape(1, 128),
                                f("lambda_q2").reshape(1, 128), f("lambda_k2").reshape(1, 128)], axis=0),
        "g_subln": f("g_subln").reshape(1, 256),
        "w_da_out": f("w_da_out").reshape(1024, D),
        "g_mem": f("g_mem").reshape(1, D),
        "w_mem_kv": f("w_mem_kv").reshape(D, D),
        "w_xa_out": f("w_xa_out").reshape(1024, D),
        "w_mix_out": f("w_mix_out").reshape(D, D),
        "w_up": f("w_up").reshape(D, DFF),
        "w_down": f("w_down").reshape(DFF, D),
        "g_final": f("g_final").reshape(1, D),
    }
    maps = []
    for c in cores:
        b, half = c // 2, c % 2
        m = dict(shared)
        m["xo"] = np.ascontiguousarray(x[b, half * NTOK:(half + 1) * NTOK])
        if half == 1:
            m["xc"] = np.ascontiguousarray(x[b, 0:NCTX])
            pc = pos[b, 0:NCTX]
        else:
            m["xc"] = np.zeros((NCTX, D), np.float32)
            pc = np.zeros((NCTX,), np.int32)
        m["posa"] = np.concatenate([pc, pos[b, half * NTOK:(half + 1) * NTOK]]).reshape(1, -1).astype(np.int32)
        m["memx"] = np.ascontiguousarray(mem[b])
        m["cf"] = make_consts(half)
        maps.append(m)
    return maps


_NC_CACHE = {}


def kernel(**inputs):
    if "nc" not in _NC_CACHE:
        plan = build().plan_out
        _NC_CACHE["nc"] = build(plan=plan)
    nc = _NC_CACHE["nc"]
    maps = make_in_maps(inputs)
    res = run_bass_kernel_spmd(nc, maps, core_ids=list(range(8)))
    out = np.zeros((B, S, D), np.float32)
    for c in range(8):
        b, half = c // 2, c % 2
        out[b, half * NTOK:(half + 1) * NTOK] = res.results[c]["y"]
    return out
```
